# Optimizing a Trainium2 kernel written in Bass

```python
import math
import jax
import jax.numpy as jnp
from jax import lax
import numpy as np

D_MODEL = 1024
BATCH = 8
SEQ = 4096
DEPTH = 1

N_MEM = 256
D_MIX = D_MODEL
D_HYENA = D_MIX // 2
HYENA_GROUPS = 8
HYENA_ORDER = 2
D_MLSTM = D_MIX - D_HYENA
MLSTM_HEADS = 4
MLSTM_HEAD_DIM = D_MLSTM // MLSTM_HEADS
MLSTM_CHUNK = 64
FILTER_BANDS = 16
FILTER_EMB = 1 + 2 * FILTER_BANDS
FILTER_HIDDEN = 64
FILTER_DIRS = 2
FILTER_CH = HYENA_ORDER * FILTER_DIRS * D_HYENA
FILTER_OUT_SCALE = 0.05
DECAY_TARGET = 1e-2
SHORT_DECAY_PCT = 0.3
LONG_DECAY_PCT = 1.5
XATTN_HEADS = 4
XATTN_HEAD_DIM = D_MODEL // XATTN_HEADS
D_FF = 4 * D_MODEL
EPS = 1e-6
HY_COLS = (HYENA_ORDER + 1) * D_HYENA
ML_QK_COLS = 2 * D_MLSTM
ML_GATE_COLS = 4 * MLSTM_HEADS
D_IN_PROJ = HY_COLS + ML_QK_COLS + 2 * D_MLSTM + ML_GATE_COLS

kernel_name = 'hybrid_hyena_mlstm_encoder_layer'


def _rms_norm(x, g):
    xf = x.astype(jnp.float32)
    y = xf * lax.rsqrt(jnp.mean(xf * xf, axis=-1, keepdims=True) + EPS)
    return (y * g.astype(jnp.float32)).astype(x.dtype)


def _group_rms_norm(x, g, n_groups):
    lead = x.shape[:-1]
    c = x.shape[-1]
    xg = x.astype(jnp.float32).reshape(*lead, n_groups, c // n_groups)
    xg = xg * lax.rsqrt(jnp.mean(xg * xg, axis=-1, keepdims=True) + EPS)
    return (xg.reshape(*lead, c) * g.astype(jnp.float32)).astype(x.dtype)


def _short_conv_centred(u, w, b):
    L = u.shape[1]
    up = jnp.pad(u, ((0, 0), (1, 1), (0, 0)))
    return up[:, :L] * w[0] + up[:, 1:L + 1] * w[1] + up[:, 2:] * w[2] + b


def _hyena_filter_spectrum(L, w1, b1, fr1, w2, b2, fr2, w3):
    f32 = jnp.float32
    t = jnp.linspace(0.0, 1.0, L, dtype=f32)[:, None]
    ang = (2.0 * math.pi / L) * jnp.arange(L, dtype=f32)[:, None]
    bands = jnp.linspace(1e-4, FILTER_BANDS - 1, FILTER_BANDS, dtype=f32)[None, :]
    z = jnp.concatenate([t, jnp.cos(bands * ang), -jnp.sin(bands * ang)], axis=-1)
    hid = jnp.sin(fr1.astype(f32) * (z @ w1.astype(f32) + b1.astype(f32)))
    hid = jnp.sin(fr2.astype(f32) * (hid @ w2.astype(f32) + b2.astype(f32)))
    filt = (hid @ w3.astype(f32)).reshape(L, HYENA_ORDER, FILTER_DIRS, D_HYENA)
    max_decay = math.log(DECAY_TARGET) / SHORT_DECAY_PCT
    min_decay = math.log(DECAY_TARGET) / LONG_DECAY_PCT
    deltas = jnp.linspace(min_decay, max_decay, D_HYENA, dtype=f32)
    filt = filt * jnp.exp(-t * jnp.abs(deltas))[:, None, None, :]
    fwd = filt[:, :, 0]
    bwd = jnp.flip(filt[1:, :, 1], axis=0)
    two_sided = jnp.concatenate([fwd, jnp.zeros((1, HYENA_ORDER, D_HYENA), f32), bwd], axis=0)
    return jnp.fft.rfft(two_sided, axis=0)


def _hyena(hy, spec, skip):
    L = hy.shape[1]
    u = hy.astype(jnp.float32)
    z, g1, g2 = u[..., :D_HYENA], u[..., D_HYENA:2 * D_HYENA], u[..., 2 * D_HYENA:]
    for o, gate in enumerate((g1, g2)):
        conv = jnp.fft.irfft(jnp.fft.rfft(z, n=2 * L, axis=1) * spec[:, o], n=2 * L, axis=1)[:, :L]
        z = gate * (conv + z * skip[o].astype(jnp.float32))
    return z


def _mlstm_chunkwise(q, k, v, i_pre, log_f):
    B, H, L, Dh = q.shape
    nc = L // MLSTM_CHUNK

    def chunks(a):
        a = a.reshape(B, H, nc, MLSTM_CHUNK, *a.shape[3:])
        return jnp.moveaxis(a, 2, 0)

    lower = jnp.tril(jnp.ones((MLSTM_CHUNK, MLSTM_CHUNK), dtype=bool))

    def step(carry, inp):
        C, n, m = carry
        qc, kc, vc, ic, fc = inp
        F = jnp.cumsum(fc, axis=-1)
        logw = F[..., :, None] - F[..., None, :] + ic[..., None, :]
        logw = jnp.where(lower, logw, -jnp.inf)
        log_inter = F + m[..., None]
        m_t = jnp.maximum(log_inter, jnp.max(logw, axis=-1))
        w = jnp.exp(logw - m_t[..., None])
        inter = jnp.exp(log_inter - m_t)
        s = jnp.einsum('bhtd,bhsd->bhts', qc, kc) * w
        num = jnp.einsum('bhts,bhse->bhte', s, vc) + inter[..., None] * jnp.einsum('bhtd,bhde->bhte', qc, C)
        den = jnp.sum(s, axis=-1) + inter * jnp.einsum('bhtd,bhd->bht', qc, n)
        h = num / jnp.maximum(jnp.abs(den), jnp.exp(-m_t))[..., None]
        F_end = F[..., -1]
        log_end = F_end[..., None] - F + ic
        m_new = jnp.maximum(F_end + m, jnp.max(log_end, axis=-1))
        we = jnp.exp(log_end - m_new[..., None])
        carry_decay = jnp.exp(F_end + m - m_new)
        C_new = carry_decay[..., None, None] * C + jnp.einsum('bhs,bhsd,bhse->bhde', we, kc, vc)
        n_new = carry_decay[..., None] * n + jnp.einsum('bhs,bhsd->bhd', we, kc)
        return (C_new, n_new, m_new), h

    init = (jnp.zeros((B, H, Dh, Dh), jnp.float32),
            jnp.zeros((B, H, Dh), jnp.float32),
            jnp.zeros((B, H), jnp.float32))
    _, hs = lax.scan(step, init, tuple(chunks(a) for a in (q, k, v, i_pre, log_f)))
    return jnp.moveaxis(hs, 0, 2).reshape(B, H, L, Dh)


def _mlstm_bidirectional(q, k, v, o_pre, gates, gate_b):
    B, L, _ = q.shape
    f32 = jnp.float32

    def heads(a):
        return a.astype(f32).reshape(B, L, MLSTM_HEADS, MLSTM_HEAD_DIM).transpose(0, 2, 1, 3)

    qh = heads(q)
    kh = heads(k) * (MLSTM_HEAD_DIM ** -0.5)
    vh = heads(v)
    g = (gates.astype(f32) + gate_b.astype(f32)).reshape(B, L, 4, MLSTM_HEADS).transpose(2, 0, 3, 1)
    i_f, f_f, i_b, f_b = g[0], g[1], g[2], g[3]
    h_f = _mlstm_chunkwise(qh, kh, vh, i_f, jax.nn.log_sigmoid(f_f))

    def flip(a):
        return jnp.flip(a, axis=2)

    h_b = flip(_mlstm_chunkwise(flip(qh), flip(kh), flip(vh), flip(i_b), flip(jax.nn.log_sigmoid(f_b))))
    h_sum = (h_f + h_b).transpose(0, 2, 1, 3).reshape(B, L, D_MLSTM)
    return jax.nn.sigmoid(o_pre.astype(f32)) * h_sum


def _cross_attention(xn, memn, wq, wk, wv, wo):
    B, L, _ = xn.shape
    M = memn.shape[1]
    q = (xn @ wq).reshape(B, L, XATTN_HEADS, XATTN_HEAD_DIM)
    k = (memn @ wk).reshape(B, M, XATTN_HEADS, XATTN_HEAD_DIM)
    v = (memn @ wv).reshape(B, M, XATTN_HEADS, XATTN_HEAD_DIM)
    s = jnp.einsum('blhd,bmhd->bhlm', q, k).astype(jnp.float32) * (XATTN_HEAD_DIM ** -0.5)
    p = jax.nn.softmax(s, axis=-1).astype(v.dtype)
    o = jnp.einsum('bhlm,bmhd->blhd', p, v).reshape(B, L, D_MODEL)
    return o @ wo


def setup_inputs(seed: int = 0) -> dict:
    key = jax.random.key(seed)
    keys = iter(jax.random.split(key, 32))

    def nrm(shape, scale):
        return jax.random.normal(next(keys), shape, jnp.float32) * scale

    def gain(shape):
        return 1.0 + nrm(shape, 0.02)

    D = D_MODEL
    x = nrm((BATCH, SEQ, D), 1.0)
    mem = nrm((BATCH, N_MEM, D), 1.0)
    f_bias = jnp.linspace(3.0, 6.0, MLSTM_HEADS, dtype=jnp.float32)
    zero_b = jnp.zeros_like(f_bias)
    gate_noise = nrm((DEPTH, 4, MLSTM_HEADS), 0.1)
    ml_gate_b = (gate_noise + jnp.stack([zero_b, f_bias, zero_b, f_bias])).reshape(DEPTH, ML_GATE_COLS)
    return {
        'x': x,
        'mem': mem,
        'norm_mix_g': gain((DEPTH, D)),
        'w_in': nrm((DEPTH, D, D_IN_PROJ), D ** -0.5),
        'hy_conv_w': nrm((DEPTH, 3, HY_COLS), 3 ** -0.5),
        'hy_conv_b': nrm((DEPTH, HY_COLS), 0.02),
        'hy_filt_w1': nrm((DEPTH, FILTER_EMB, FILTER_HIDDEN), FILTER_EMB ** -0.5),
        'hy_filt_b1': nrm((DEPTH, FILTER_HIDDEN), 0.1),
        'hy_filt_freq1': gain((DEPTH, FILTER_HIDDEN)),
        'hy_filt_w2': nrm((DEPTH, FILTER_HIDDEN, FILTER_HIDDEN), FILTER_HIDDEN ** -0.5),
        'hy_filt_b2': nrm((DEPTH, FILTER_HIDDEN), 0.1),
        'hy_filt_freq2': gain((DEPTH, FILTER_HIDDEN)),
        'hy_filt_w3': nrm((DEPTH, FILTER_HIDDEN, FILTER_CH), FILTER_OUT_SCALE * FILTER_HIDDEN ** -0.5),
        'hy_skip': nrm((DEPTH, HYENA_ORDER, D_HYENA), 1.0),
        'hy_norm_g': gain((DEPTH, D_HYENA)),
        'ml_conv_w': nrm((DEPTH, 3, ML_QK_COLS), 3 ** -0.5),
        'ml_conv_b': nrm((DEPTH, ML_QK_COLS), 0.02),
        'ml_gate_b': ml_gate_b,
        'ml_norm_g': gain((DEPTH, D_MLSTM)),
        'w_out': nrm((DEPTH, D_MIX, D), D_MIX ** -0.5),
        'norm_x_g': gain((DEPTH, D)),
        'norm_mem_g': gain((DEPTH, D)),
        'xa_wq': nrm((DEPTH, D, D), D ** -0.5),
        'xa_wk': nrm((DEPTH, D, D), D ** -0.5),
        'xa_wv': nrm((DEPTH, D, D), D ** -0.5),
        'xa_wo': nrm((DEPTH, D, D), D ** -0.5),
        'norm_ff_g': gain((DEPTH, D)),
        'ff_w1': nrm((DEPTH, D, D_FF), D ** -0.5),
        'ff_w2': nrm((DEPTH, D_FF, D), D_FF ** -0.5),
        'final_norm_g': gain((D,)),
    }


def reference(x, mem, norm_mix_g, w_in, hy_conv_w, hy_conv_b, hy_filt_w1, hy_filt_b1, hy_filt_freq1,
              hy_filt_w2, hy_filt_b2, hy_filt_freq2, hy_filt_w3, hy_skip, hy_norm_g, ml_conv_w, ml_conv_b,
              ml_gate_b, ml_norm_g, w_out, norm_x_g, norm_mem_g, xa_wq, xa_wk, xa_wv, xa_wo, norm_ff_g,
              ff_w1, ff_w2, final_norm_g):
    L = x.shape[1]
    off = HY_COLS + ML_QK_COLS
    h = x
    for l in range(DEPTH):
        u = _rms_norm(h, norm_mix_g[l])
        proj = u @ w_in[l]
        hy = _short_conv_centred(proj[..., :HY_COLS], hy_conv_w[l], hy_conv_b[l])
        spec = _hyena_filter_spectrum(L, hy_filt_w1[l], hy_filt_b1[l], hy_filt_freq1[l],
                                      hy_filt_w2[l], hy_filt_b2[l], hy_filt_freq2[l], hy_filt_w3[l])
        y_hy = _group_rms_norm(_hyena(hy, spec, hy_skip[l]), hy_norm_g[l], HYENA_GROUPS)
        qk = jax.nn.silu(_short_conv_centred(proj[..., HY_COLS:off], ml_conv_w[l], ml_conv_b[l]))
        y_ml = _mlstm_bidirectional(qk[..., :D_MLSTM], qk[..., D_MLSTM:],
                                    proj[..., off:off + D_MLSTM],
                                    proj[..., off + D_MLSTM:off + 2 * D_MLSTM],
                                    proj[..., off + 2 * D_MLSTM:], ml_gate_b[l])
        y_ml = _group_rms_norm(y_ml, ml_norm_g[l], MLSTM_HEADS)
        mixed = jnp.concatenate([y_hy.astype(h.dtype), y_ml.astype(h.dtype)], axis=-1)
        h = h + mixed @ w_out[l]
        h = h + _cross_attention(_rms_norm(h, norm_x_g[l]), _rms_norm(mem, norm_mem_g[l]),
                                 xa_wq[l], xa_wk[l], xa_wv[l], xa_wo[l])
        h = h + jnp.square(jax.nn.relu(_rms_norm(h, norm_ff_g[l]) @ ff_w1[l])) @ ff_w2[l]
    return _rms_norm(h, final_norm_g)
```

```python
import math
import numpy as np
import ml_dtypes
import concourse.bass as bass
import concourse.mybir as mybir
from concourse.bass_utils import run_bass_kernel_spmd

F32 = mybir.dt.float32
BF = mybir.dt.bfloat16
ALU = mybir.AluOpType
AF = mybir.ActivationFunctionType
AX = mybir.AxisListType

L = 4096
D = 1024
NTT = L // 128
DIN = 3600
NMEM = 256
EPS = 1e-6
DH = 512
ENGS = ("pe", "act", "dve", "pool", "sp")


class K:
    def __init__(self, nc):
        self.nc = nc
        self.streams = {e: [] for e in ENGS}
        self.count = {e: 0 for e in ENGS}
        self.waited = {e: {} for e in ENGS}
        self.last_write = {}
        self.readers = {}
        self.dma_slots = {"sp": 12, "pool": 6, "act": 4}
        self.dma_next = {q: 0 for q in self.dma_slots}
        self.dma_val = {}
        self.sems = {}
        self.ps_next = 0
        self.ps_pool = None
        self.ps_pool_next = {}

    def _deps(self, reads, writes):
        deps = {}

        def add(tok):
            if tok is None:
                return
            k, v = tok
            if deps.get(k, 0) < v:
                deps[k] = v
        for key in reads:
            add(self.last_write.get(key))
        for key in writes:
            add(self.last_write.get(key))
            for k, v in self.readers.get(key, {}).items():
                add((k, v))
        return deps

    def _emit_waits(self, eng, deps):
        w = self.waited[eng]
        for k, v in deps.items():
            if w.get(k, 0) >= v:
                continue
            w[k] = v
            self.streams[eng].append(("wait", k, v))

    def _record(self, tok, reads, writes):
        k, v = tok
        for key in reads:
            r = self.readers.setdefault(key, {})
            if r.get(k, 0) < v:
                r[k] = v
        for key in writes:
            self.last_write[key] = tok
            self.readers[key] = {}

    def op(self, eng, fns, reads=(), writes=(), after=()):
        if not isinstance(fns, (list, tuple)):
            fns = [fns]
        writes = list(writes) + [kk for kk in reads if kk[0] == "ps"]
        reads = [kk for kk in reads if kk[0] != "ps"]
        deps = self._deps(reads, writes)
        for tok in after:
            if tok is not None and deps.get(tok[0], 0) < tok[1]:
                deps[tok[0]] = tok[1]
        self._emit_waits(eng, deps)
        self.count[eng] += 1
        tok = (("c", eng), self.count[eng])
        for f in fns[:-1]:
            self.streams[eng].append(("ins", f, None))
        self.streams[eng].append(("ins", fns[-1], tok))
        self._record(tok, reads, writes)
        return tok

    def dma(self, q, out, in_, reads=(), writes=(), after=(), **kw):
        slot = self.dma_next[q]
        self.dma_next[q] = (slot + 1) % self.dma_slots[q]
        key = ("d", q, slot)
        prev = self.dma_val.get(key, 0)
        deps = self._deps(reads, writes)
        if prev:
            deps[key] = max(deps.get(key, 0), prev)
        for tok in after:
            if tok is not None and deps.get(tok[0], 0) < tok[1]:
                deps[tok[0]] = tok[1]
        self._emit_waits(q, deps)
        val = prev + 16
        self.dma_val[key] = val
        tok = (key, val)
        self.streams[q].append(("dma", (out, in_, kw), tok))
        self._record(tok, reads, writes)
        return tok

    def barrier(self):
        deps = {("c", e): self.count[e] for e in ENGS if self.count[e]}
        for key, v in self.dma_val.items():
            deps[key] = v
        for e in ENGS:
            self._emit_waits(e, dict(deps))

    def psum(self):
        if self.ps_pool is not None:
            banks = self.ps_pool
            i = self.ps_pool_next.get(banks, 0)
            self.ps_pool_next[banks] = (i + 1) % len(banks)
            return banks[i]
        b = self.ps_next
        self.ps_next = (b + 1) % 8
        return b

    def emit(self, final_tokens):
        nc = self.nc
        import contextlib
        with contextlib.ExitStack() as st:
            semkeys = [("c", e) for e in ENGS]
            for q, n in self.dma_slots.items():
                semkeys += [("d", q, i) for i in range(n)]
            for sk in semkeys:
                self.sems[sk] = st.enter_context(nc.semaphore("s_" + "_".join(str(x) for x in sk)))
            block = st.enter_context(nc.Block())
            deps = {}
            for tok in final_tokens:
                if deps.get(tok[0], 0) < tok[1]:
                    deps[tok[0]] = tok[1]
            self._emit_waits("sp", deps)

            def run(engname):
                def body(eng):
                    for item in self.streams[engname]:
                        if item[0] == "wait":
                            eng.wait_ge(self.sems[item[1]], item[2])
                        elif item[0] == "ins":
                            ins = item[1](eng)
                            if item[2] is not None:
                                ins.then_inc(self.sems[item[2][0]], 1)
                        else:
                            out, in_, kw = item[1]
                            eng.dma_start(out=out, in_=in_, **kw).then_inc(self.sems[item[2][0]], 16)
                return body
            block.tensor(run("pe"))
            block.scalar(run("act"))
            block.vector(run("dve"))
            block.gpsimd(run("pool"))
            block.sync(run("sp"))


def _bf(a):
    return np.ascontiguousarray(a.astype(np.float32)).astype(ml_dtypes.bfloat16)


_CONST_CACHE = {}


def host_consts():
    if _CONST_CACHE:
        return _CONST_CACHE
    c = {}
    N = 2 * L
    c["ident_bf"] = _bf(np.eye(128))
    c["ident_f"] = np.eye(128, dtype=np.float32)
    t1 = np.arange(32)[:, None]
    f1 = np.arange(64)[None, :]
    th = 2 * np.pi * t1 * (f1 + 0.5) / 64.0
    F1 = np.concatenate([np.cos(th), -np.sin(th)], axis=1)
    F1pad = np.zeros((32, 4, 4, 128))
    for q in range(4):
        F1pad[:, q, q, :] = F1
    c["F1pad"] = _bf(F1pad.reshape(128, 4 * 128))
    th2 = 2 * np.pi * (t1 + 32) * (f1 + 0.5) / 64.0
    F1b = np.concatenate([np.cos(th2), -np.sin(th2)], axis=1)
    F1pad2 = np.zeros((32, 4, 4, 128))
    for q in range(4):
        F1pad2[:, q, q, :] = F1b
    c["F1pad2"] = _bf(F1pad2.reshape(128, 4 * 128))
    t2 = np.arange(128)[:, None, None]
    f1 = np.arange(64)[None, :, None]
    f2 = np.arange(64)[None, None, :]
    ph = 2 * np.pi * t2 * (f1 + 64 * f2 + 0.5) / N
    cc, ss = np.cos(ph), np.sin(ph)
    W4 = np.zeros((128, 64, 2, 128))
    W4[:, :, 0, :64] = cc
    W4[:, :, 0, 64:] = -ss
    W4[:, :, 1, :64] = ss
    W4[:, :, 1, 64:] = cc
    c["W4"] = _bf(W4.reshape(128, 64 * 2 * 128))
    cT = np.transpose(cc, (2, 1, 0))
    sT = np.transpose(ss, (2, 1, 0))
    WI = np.zeros((128, 64, 2, 128))
    WI[:64, :, 0, :] = cT
    WI[64:, :, 0, :] = -sT
    WI[:64, :, 1, :] = sT
    WI[64:, :, 1, :] = cT
    c["WI"] = _bf(WI.reshape(128, 64 * 2 * 128))
    f1 = np.arange(64)[:, None]
    t1 = np.arange(32)[None, :]
    th = 2 * np.pi * t1 * (f1 + 0.5) / 64.0
    G = np.concatenate([np.cos(th), -np.sin(th)], axis=0) * (2.0 / N)
    c["G"] = _bf(G)
    k = np.arange(128)[:, None]
    m = np.arange(128)[None, :]
    c["Pswap"] = _bf((k == (m + 64) % 128) * 1.0)
    D1 = ((k == (m % 64)) * 1.0)
    sg = np.where(m < 64, -1.0, 1.0)
    D2 = ((k == 64 + (m % 64)) * sg)
    c["Dmat"] = _bf(np.concatenate([D1, D2, -D2], axis=1))
    c["blk64"] = _bf(((k // 64) == (m // 64)) * 1.0)
    c["ones_bf"] = _bf(np.ones((128, 128)))
    c["mask_f"] = (k <= m).astype(np.float32)
    c["mask_b"] = (k >= m).astype(np.float32)
    sel = np.zeros((16, 16, 128), np.float32)
    for p in range(16):
        sel[p, p, :] = 1.0
    c["sel"] = sel.reshape(16, 16 * 128)
    f32 = np.float32
    t = np.linspace(0.0, 1.0, L, dtype=f32)[:, None]
    ang = (f32(2.0 * math.pi / L) * np.arange(L, dtype=f32))[:, None]
    bands = np.linspace(1e-4, 16 - 1, 16, dtype=f32)[None, :]
    z = np.concatenate([t, np.cos(bands * ang), -np.sin(bands * ang)], axis=-1).astype(f32)
    c["zT"] = np.ascontiguousarray(z.T)
    c["tlin"] = np.ascontiguousarray(t.T)
    max_decay = math.log(1e-2) / 0.3
    min_decay = math.log(1e-2) / 1.5
    deltas = np.linspace(min_decay, max_decay, DH, dtype=f32)
    c["negabsdelta"] = np.ascontiguousarray((-np.abs(deltas)).reshape(4, 128).T.astype(f32))
    dm = np.zeros((8, 2), np.float32)
    dm[0:4, 0] = 1.0
    dm[4:8, 1] = 1.0
    c["dirmask"] = dm
    _CONST_CACHE.update(c)
    return c


CONST_DT = {"ident_bf": BF, "ident_f": F32, "F1pad": BF, "F1pad2": BF, "W4": BF, "WI": BF, "G": BF, "Pswap": BF,
            "Dmat": BF, "blk64": BF, "ones_bf": BF, "mask_f": F32, "mask_b": F32, "sel": F32,
            "zT": F32, "tlin": F32, "negabsdelta": F32, "dirmask": F32}


IN_SPECS = [
    ("x", [L, D], F32), ("mem", [NMEM, D], F32), ("w_in", [D, DIN], F32),
    ("gmix_col", [128, 8], F32),
    ("hy_cw", [128, 12 * 3], F32), ("hy_cb", [128, 12], F32),
    ("f_w1", [33, 64], F32), ("f_b1", [64, 1], F32), ("f_fr1", [64, 1], F32),
    ("f_w2", [64, 64], F32), ("f_b2", [64, 1], F32), ("f_fr2", [64, 1], F32),
    ("f_w3", [64, 2048], F32),
    ("hy_skip", [128, 8], F32), ("hy_ng", [128, 4], F32),
    ("ml_cw", [128, 8 * 3], F32), ("ml_cb", [128, 8], F32), ("ml_gb", [16, 1], F32),
    ("ml_ng", [128, 4], F32),
    ("w_out", [D, D], F32), ("gx_col", [128, 8], F32), ("gmem_col", [128, 8], F32),
    ("xa_wq", [D, D], F32), ("xa_wk", [D, D], F32), ("xa_wv", [D, D], F32), ("xa_wo", [D, D], F32),
    ("gff_col", [128, 8], F32), ("ff_w1", [D, 4 * D], F32), ("ff_w2", [4 * D, D], F32),
    ("gfin_row", [1, D], F32),
]


DEBUG = False
epsc = None


def build(debug=False, phases="ABCD"):
    global DEBUG
    DEBUG = debug
    import contextlib
    nc = bass.Bass("TRN2", target_bir_lowering=False)
    k = K(nc)
    dr = {}
    for name, shape, dt in IN_SPECS:
        dr[name] = nc.dram_tensor(name, shape, dt, kind="ExternalInput").ap()
    hc = host_consts()
    for name, arr in hc.items():
        dr[name] = nc.dram_tensor("c_" + name, list(arr.shape), CONST_DT[name], kind="ExternalInput").ap()
    out = nc.dram_tensor("out", [L, D], F32, kind="ExternalOutput").ap()
    dbgkind = "ExternalOutput" if debug else "Internal"
    projT = nc.dram_tensor("projT", [3584, L], BF, kind=dbgkind).ap()
    gatesT = nc.dram_tensor("gatesT", [16, L], F32, kind=dbgkind).ap()
    mixT = nc.dram_tensor("mixT", [D, L], BF, kind=dbgkind).ap()
    wscr = {n: nc.dram_tensor("bf_" + n, shp, BF).ap() for n, shp in
            (("w_out", [D, D]), ("xa_wq", [D, D]), ("xa_wk", [D, D]), ("xa_wv", [D, D]), ("xa_wo", [D, D]),
             ("ff_w2", [4 * D, D]), ("ff_w1", [D, 4 * D]))}
    dr["wscr"] = wscr

    finals = []
    with contextlib.ExitStack() as top:
        ps = [top.enter_context(nc.psum_tensor(f"ps{b}", [128, 512], F32)) for b in range(8)]
        psbf = [p[:].bitcast(BF) for p in ps]

        def sbuf(st, name, shape, dt):
            return st.enter_context(nc.sbuf_tensor("s_" + name, shape, dt))

        ident_bf = sbuf(top, "ident_bf", [128, 128], BF)
        ident_f = sbuf(top, "ident_f", [128, 128], F32)
        global epsc
        epsc = sbuf(top, "epsc", [128, 1], F32)
        k.op("pool", lambda e: e.memset(epsc[:], EPS), writes=[("epsc",)])
        k.dma("sp", ident_bf[:], dr["ident_bf"], writes=[("ident_bf",)])
        k.dma("sp", ident_f[:], dr["ident_f"], writes=[("ident_f",)])

        if "A" in phases:
            phase_A(nc, k, dr, ps, psbf, ident_bf, projT, gatesT, finals)
        k.barrier()
        if "B" in phases:
            phase_B(nc, k, dr, ps, psbf, ident_bf, projT, mixT, finals)
        if "C" in phases:
            phase_C(nc, k, dr, ps, psbf, ident_bf, ident_f, projT, gatesT, mixT, finals)
        if "D" in phases:
            phase_D(nc, k, dr, ps, psbf, ident_bf, mixT, out, finals)
        k.emit(finals)
    return nc


def evac_engine(i):
    return "act" if i % 2 == 0 else "dve"


def copy_op(k, eng, out, in_, reads, writes):
    if eng == "act":
        return k.op("act", lambda e: e.copy(out=out, in_=in_), reads=reads, writes=writes)
    return k.op(eng, lambda e: e.tensor_copy(out=out, in_=in_), reads=reads, writes=writes)


def phase_A(nc, k, dr, ps, psbf, ident_bf, projT, gatesT, finals):
    import contextlib
    with contextlib.ExitStack() as st:
        def sb(name, shape, dt):
            return st.enter_context(nc.sbuf_tensor("s_" + name, shape, dt))
        uT = sb("uT", [128, 8, L], BF)
        wbf = sb("wbf", [128, 8, DIN], BF)
        wst = [sb(f"wst{i}", [128, DIN], F32) for i in range(2)]
        gcol = sb("gcol", [128, 8], F32)
        xt = [sb(f"xt{i}", [128, D], F32) for i in range(2)]
        xn = [sb(f"xn{i}", [128, D], BF) for i in range(2)]
        sq = sb("sq", [128, D], BF)
        stat = sb("stat", [128, 3 * NTT], F32)
        stg = [sb(f"stg{i}", [128, L], BF) for i in range(2)]
        gstg = [sb(f"gstg{i}", [16, 512], F32) for i in range(2)]

        gD = sb("gD", [128, 3, 8], F32)
        cst = [sb(f"cst{i}", [128, D], F32) for i in range(3)]
        cbo = [sb(f"cbo{i}", [128, D], BF) for i in range(3)]
        k.dma("sp", gD[:, 0, :], dr["gx_col"], writes=[("gD",)])
        k.dma("sp", gD[:, 1, :], dr["gmem_col"], writes=[("gD",)])
        k.dma("sp", gD[:, 2, :], dr["gff_col"], writes=[("gD",)])
        jobs = []
        for c in range(8):
            rows = slice(c * 128, (c + 1) * 128)
            jobs.append(("w_out", rows, slice(0, D), None))
            jobs.append(("xa_wq", rows, slice(0, D), (0, c)))
            jobs.append(("xa_wk", rows, slice(0, D), (1, c)))
            jobs.append(("xa_wv", rows, slice(0, D), (1, c)))
            jobs.append(("xa_wo", rows, slice(0, D), None))
        for c in range(32):
            jobs.append(("ff_w2", slice(c * 128, (c + 1) * 128), slice(0, D), None))
        for c in range(8):
            for qd in range(4):
                jobs.append(("ff_w1", slice(c * 128, (c + 1) * 128), slice(qd * D, (qd + 1) * D), (2, c)))

        def wload(i):
            nme, rows, cols, g = jobs[i]
            k.dma("pool", cst[i % 3][:], dr[nme][rows, cols], writes=[("cst", i % 3)])

        def wconv(i):
            nme, rows, cols, g = jobs[i]
            s = i % 3
            if g is None:
                k.op("pool", lambda e: e.tensor_copy(out=cbo[s][:], in_=cst[s][:]), reads=[("cst", s)],
                     writes=[("cbo", s)])
            else:
                gi, c = g
                k.op("pool", lambda e: e.tensor_scalar(out=cbo[s][:], in0=cst[s][:], scalar1=gD[:, gi, c:c + 1],
                                                       scalar2=1.0, op0=ALU.mult, op1=ALU.mult),
                     reads=[("cst", s), ("gD",)], writes=[("cbo", s)])
            finals.append(k.dma("pool", dr["wscr"][nme][rows, cols], cbo[s][:], reads=[("cbo", s)],
                                writes=[("wscr", nme)]))
        wload(0)
        wload(1)
        for i in range(len(jobs)):
            if i + 2 < len(jobs):
                wload(i + 2)
            wconv(i)

        k.dma("sp", gcol[:], dr["gmix_col"], writes=[("gcol",)])
        for c in range(8):
            k.dma("sp", wst[c % 2][:], dr["w_in"][c * 128:(c + 1) * 128, :], writes=[("wst", c % 2)])
            k.op("dve", lambda e, c=c: e.tensor_scalar(out=wbf[:, c, :], in0=wst[c % 2][:],
                                                        scalar1=gcol[:, c:c + 1], scalar2=None, op0=ALU.mult),
                 reads=[("wst", c % 2), ("gcol",)], writes=[("wbf", c)])
        for i in range(NTT):
            j = i % 2
            k.dma("act", xt[j][:], dr["x"][i * 128:(i + 1) * 128, :], writes=[("xt", j)])
            k.op("act", lambda e, i=i, j=j: e.activation(out=sq[:], in_=xt[j][:], func=AF.Square,
                                                         accum_out=stat[:, i:i + 1]),
                 reads=[("xt", j)], writes=[("sq",), ("stat", i)])
            k.op("act", lambda e, i=i: e.activation(out=stat[:, NTT + i:NTT + i + 1], in_=stat[:, i:i + 1],
                                                    func=AF.Sqrt, scale=1.0 / D, bias=EPS),
                 reads=[("stat", i)], writes=[("stat", i)])
            k.op("dve", lambda e, i=i: e.reciprocal(out=stat[:, 2 * NTT + i:2 * NTT + i + 1],
                                                    in_=stat[:, NTT + i:NTT + i + 1]),
                 reads=[("stat", i)], writes=[("stat", i)])
            k.op("act", lambda e, i=i, j=j: e.activation(out=xn[j][:], in_=xt[j][:], func=AF.Copy,
                                                         scale=stat[:, 2 * NTT + i:2 * NTT + i + 1]),
                 reads=[("xt", j), ("stat", i)], writes=[("xn", j)])
            b = k.psum()
            pv = psbf[b].rearrange("p (c t) -> p c t", t=128)
            k.op("pe", [lambda e, c=c, j=j, pv=pv: e.transpose(out=pv[:, c, :], in_=xn[j][:, c * 128:(c + 1) * 128],
                                                               identity=ident_bf[:]) for c in range(8)],
                 reads=[("xn", j), ("ident_bf",)], writes=[("ps", b)])
            copy_op(k, evac_engine(i), uT[:, :, i * 128:(i + 1) * 128], pv, reads=[("ps", b)], writes=[("uT", i // 4)])
        ev = 0
        for cc in range(29):
            M = 128 if cc < 28 else 16
            s = cc % 2
            for tg in range(8):
                b = k.psum()
                k.op("pe", [lambda e, kc=kc, cc=cc, tg=tg, b=b, M=M: e.matmul(
                    ps[b][0:M, :], lhsT=wbf[:, kc, cc * 128:cc * 128 + M], rhs=uT[:, kc, tg * 512:(tg + 1) * 512],
                    start=(kc == 0), stop=(kc == 7)) for kc in range(8)],
                    reads=[("wbf", kc) for kc in range(8)] + [("uT", tg)], writes=[("ps", b)])
                if cc < 28:
                    copy_op(k, evac_engine(ev), stg[s][:, tg * 512:(tg + 1) * 512], ps[b][:], reads=[("ps", b)],
                            writes=[("stg", s)])
                else:
                    copy_op(k, evac_engine(ev), gstg[tg % 2][:], ps[b][0:16, :], reads=[("ps", b)],
                            writes=[("gstg", tg % 2)])
                    finals.append(k.dma("sp", gatesT[:, tg * 512:(tg + 1) * 512], gstg[tg % 2][:],
                                        reads=[("gstg", tg % 2)], writes=[("gatesT",)]))
                ev += 1
            if cc < 28:
                t = k.dma("sp", projT[cc * 128:(cc + 1) * 128, :], stg[s][:], reads=[("stg", s)],
                          writes=[("projT", cc)])
                finals.append(t)


def _cols(v, n):
    return np.ascontiguousarray(np.asarray(v, np.float32).reshape(n, 128).T)


def shared_inputs(inp):
    f = lambda a: np.ascontiguousarray(np.asarray(a, np.float32))
    m = {}
    m["w_in"] = f(inp["w_in"][0])
    m["gmix_col"] = _cols(inp["norm_mix_g"][0], 8)
    cw = f(inp["hy_conv_w"][0])
    m["hy_cw"] = np.ascontiguousarray(cw.reshape(3, 12, 128).transpose(2, 1, 0).reshape(128, 36))
    m["hy_cb"] = _cols(inp["hy_conv_b"][0], 12)
    m["f_w1"] = f(inp["hy_filt_w1"][0])
    m["f_b1"] = f(inp["hy_filt_b1"][0]).reshape(64, 1)
    m["f_fr1"] = f(inp["hy_filt_freq1"][0]).reshape(64, 1)
    m["f_w2"] = f(inp["hy_filt_w2"][0])
    m["f_b2"] = f(inp["hy_filt_b2"][0]).reshape(64, 1)
    m["f_fr2"] = f(inp["hy_filt_freq2"][0]).reshape(64, 1)
    m["f_w3"] = f(inp["hy_filt_w3"][0])
    m["hy_skip"] = _cols(f(inp["hy_skip"][0]).reshape(-1), 8)
    m["hy_ng"] = _cols(inp["hy_norm_g"][0], 4)
    mw = f(inp["ml_conv_w"][0])
    m["ml_cw"] = np.ascontiguousarray(mw.reshape(3, 8, 128).transpose(2, 1, 0).reshape(128, 24))
    m["ml_cb"] = _cols(inp["ml_conv_b"][0], 8)
    m["ml_gb"] = f(inp["ml_gate_b"][0]).reshape(16, 1)
    m["ml_ng"] = _cols(inp["ml_norm_g"][0], 4)
    m["w_out"] = f(inp["w_out"][0])
    m["gx_col"] = _cols(inp["norm_x_g"][0], 8)
    m["gmem_col"] = _cols(inp["norm_mem_g"][0], 8)
    for n in ("xa_wq", "xa_wk", "xa_wv", "xa_wo", "ff_w1", "ff_w2"):
        m[n] = f(inp[n][0])
    m["gff_col"] = _cols(inp["norm_ff_g"][0], 8)
    m["gfin_row"] = f(inp["final_norm_g"]).reshape(1, D)
    for name, arr in host_consts().items():
        m["c_" + name] = arr
    return m


def kernel(**inputs):
    nb = 8
    sh = shared_inputs(inputs)
    x = np.asarray(inputs["x"], np.float32)
    mem = np.asarray(inputs["mem"], np.float32)
    in_maps = []
    for b in range(nb):
        m = dict(sh)
        m["x"] = np.ascontiguousarray(x[b])
        m["mem"] = np.ascontiguousarray(mem[b])
        in_maps.append(m)
    nc = build()
    res = run_bass_kernel_spmd(nc, in_maps, core_ids=list(range(nb)))
    return np.stack([np.asarray(r["out"], np.float32) for r in res.results], axis=0)


def phase_B(nc, k, dr, ps, psbf, ident_bf, projT, mixT, finals):
    import contextlib
    PI = math.pi
    with contextlib.ExitStack() as st:
        def sb(name, shape, dt):
            return st.enter_context(nc.sbuf_tensor("s_" + name, shape, dt))
        W4 = sb("W4", [128, 64, 2, 128], BF)
        WI = sb("WI", [128, 64, 2, 128], BF)
        F1pad = sb("F1pad", [128, 4, 128], BF)
        Gm = sb("Gm", [128, 32], BF)
        Pswap = sb("Pswap", [128, 128], BF)
        Dmat = sb("Dmat", [128, 3, 128], BF)
        blk64 = sb("blk64", [128, 128], BF)
        Z1 = sb("Z1", [128, 32, 64], BF)
        Yfm = sb("Yfm", [128, 128, 64], BF)
        Yt = sb("Yt", [128, 64, 128], BF)
        Xs = [sb(f"Xs{i}", [128, 64, 64], BF) for i in range(2)]
        sgn = sb("sgn", [128, 1], F32)
        hy_cw = sb("hy_cw", [128, 36], F32)
        hy_cb = sb("hy_cb", [128, 12], F32)
        hy_skip = sb("hy_skip", [128, 8], F32)
        hy_ng = sb("hy_ng", [128, 4], F32)
        stf = contextlib.ExitStack()

        def sbf(name, shape, dt):
            return stf.enter_context(nc.sbuf_tensor("s_" + name, shape, dt))
        tl512 = sbf("tl512", [128, 512], F32)
        nad = sbf("nad", [128, 4], F32)
        wbias = sbf("wbias", [128, 32], F32)
        w3bf = sbf("w3bf", [64, 2048], BF)
        hid2 = sbf("hid2", [64, L], BF)
        F1pad2 = sbf("F1pad2", [128, 4, 128], BF)
        Z1L = sbf("Z1L", [128, 2, 32, 64], BF)
        k.dma("sp", F1pad2[:], dr["F1pad2"].rearrange("p (a b) -> p a b", a=4), writes=[("F1pad2",)])
        for name, t, src in (("W4", W4, dr["W4"].rearrange("p (a b c) -> p a b c", a=64, b=2)),
                             ("WI", WI, dr["WI"].rearrange("p (a b c) -> p a b c", a=64, b=2)),
                             ("F1pad", F1pad, dr["F1pad"].rearrange("p (a b) -> p a b", a=4)),
                             ("Dmat", Dmat, dr["Dmat"].rearrange("p (a b) -> p a b", a=3)),
                             ("Gm", Gm, dr["G"]), ("Pswap", Pswap, dr["Pswap"]), ("blk64", blk64, dr["blk64"]),
                             ("nad", nad, dr["negabsdelta"]), ("hy_cw", hy_cw, dr["hy_cw"]),
                             ("hy_cb", hy_cb, dr["hy_cb"]), ("hy_skip", hy_skip, dr["hy_skip"]),
                             ("hy_ng", hy_ng, dr["hy_ng"])):
            k.dma("sp", t[:], src, writes=[(name,)])
        k.dma("sp", tl512[:], dr["tlin"][0:1, 0:512].partition_broadcast(128), writes=[("tl512",)])
        k.op("pool", lambda e: e.memset(sgn[0:64, :], 1.0), writes=[("sgn",)])
        k.op("pool", lambda e: e.memset(sgn[64:128, :], -1.0), writes=[("sgn",)])
        for j in range(4):
            for tg in range(8):
                k.op("pool", lambda e, j=j, tg=tg: e.tensor_scalar(
                    out=wbias[:, j * 8 + tg:j * 8 + tg + 1], in0=nad[:, j:j + 1], scalar1=float(512 * tg) / (L - 1),
                    scalar2=1.0, op0=ALU.mult, op1=ALU.mult), reads=[("nad",)], writes=[("wbias",)])

        with contextlib.ExitStack() as st2:
            def sb2(name, shape, dt):
                return st2.enter_context(nc.sbuf_tensor("s_" + name, shape, dt))
            zT = sb2("zT", [33, L], F32)
            fw1 = sb2("fw1", [33, 64], F32)
            fw2 = sb2("fw2", [64, 64], F32)
            fw3 = sb2("fw3", [64, 2048], F32)
            fcol = sb2("fcol", [64, 4], F32)
            hid1 = sb2("hid1", [64, L], F32)
            arg = sb2("arg", [64, L], F32)
            cnt1 = [sb2(f"cnt1_{i}", [64, 512], F32) for i in range(2)]
            cnt2 = [sb2(f"cnt2_{i}", [64, 512], F32) for i in range(2)]
            k.dma("sp", zT[:], dr["zT"], writes=[("zT",)])
            k.dma("sp", fw1[:], dr["f_w1"], writes=[("fw1",)])
            k.dma("sp", fw2[:], dr["f_w2"], writes=[("fw2",)])
            k.dma("sp", fw3[:], dr["f_w3"], writes=[("fw3",)])
            for i, nme in enumerate(("f_b1", "f_fr1", "f_b2", "f_fr2")):
                k.dma("sp", fcol[:, i:i + 1], dr[nme], writes=[("fcol",)])
            k.op("pool", lambda e: e.tensor_copy(out=w3bf[:], in_=fw3[:]), reads=[("fw3",)], writes=[("w3bf",)])

            def ffn_layer(lhsT, lkey, kdim, rhs_t, rkey, bcol, fcolx, out_t, okey):
                for tg in range(8):
                    b = k.psum()
                    q = tg % 2
                    sl = slice(tg * 512, (tg + 1) * 512)
                    k.op("pe", lambda e, b=b, sl=sl: e.matmul(ps[b][0:64, :], lhsT=lhsT[0:kdim, :],
                                                             rhs=rhs_t[0:kdim, sl], start=True, stop=True),
                         reads=[(lkey,), (rkey,)], writes=[("ps", b)])
                    k.op("dve", lambda e, b=b, sl=sl: e.tensor_scalar(
                        out=arg[:, sl], in0=ps[b][0:64, :], scalar1=fcol[:, bcol:bcol + 1],
                        scalar2=fcol[:, fcolx:fcolx + 1], op0=ALU.add, op1=ALU.mult),
                        reads=[("ps", b), ("fcol",)], writes=[("arg", tg)])
                    k.op("dve", lambda e, sl=sl, q=q: e.tensor_single_scalar(out=cnt1[q][:], in_=arg[:, sl],
                                                                             scalar=PI, op=ALU.is_gt),
                         reads=[("arg", tg)], writes=[("cnt1", q)])
                    k.op("dve", lambda e, sl=sl, q=q: e.scalar_tensor_tensor(
                        out=cnt1[q][:], in0=arg[:, sl], scalar=3 * PI, in1=cnt1[q][:], op0=ALU.is_gt, op1=ALU.add),
                        reads=[("arg", tg), ("cnt1", q)], writes=[("cnt1", q)])
                    k.op("dve", lambda e, sl=sl, q=q: e.tensor_single_scalar(out=cnt2[q][:], in_=arg[:, sl],
                                                                             scalar=-PI, op=ALU.is_lt),
                         reads=[("arg", tg)], writes=[("cnt2", q)])
                    k.op("dve", lambda e, sl=sl, q=q: e.scalar_tensor_tensor(
                        out=cnt2[q][:], in0=arg[:, sl], scalar=-3 * PI, in1=cnt2[q][:], op0=ALU.is_lt, op1=ALU.add),
                        reads=[("arg", tg), ("cnt2", q)], writes=[("cnt2", q)])
                    k.op("dve", lambda e, sl=sl, q=q: e.scalar_tensor_tensor(
                        out=arg[:, sl], in0=cnt1[q][:], scalar=-2 * PI, in1=arg[:, sl], op0=ALU.mult, op1=ALU.add),
                        reads=[("arg", tg), ("cnt1", q)], writes=[("arg", tg)])
                    k.op("dve", lambda e, sl=sl, q=q: e.scalar_tensor_tensor(
                        out=arg[:, sl], in0=cnt2[q][:], scalar=2 * PI, in1=arg[:, sl], op0=ALU.mult, op1=ALU.add),
                        reads=[("arg", tg), ("cnt2", q)], writes=[("arg", tg)])
                k.op("act", lambda e: e.activation(out=out_t[:], in_=arg[:], func=AF.Sin),
                     reads=[("arg", tg) for tg in range(8)], writes=[(okey,)])

            ffn_layer(fw1, "fw1", 33, zT, "zT", 0, 1, hid1, "hid1")
            ffn_layer(fw2, "fw2", 64, hid1, "hid1", 2, 3, hid2, "hid2")
        k.barrier()

        cnt = {"ev": 0, "raw": 0, "pt": 0, "gt": 0, "hn": 0}

        def ev():
            cnt["ev"] += 1
            return "dve" if cnt["ev"] % 3 == 0 else "act"

        def fwd_half_gen(src, skeys, h, xs, xkey, long=False, bufs=None):
            Z1_, Z1L_, Yfm_, Yt_, tg_ = bufs if bufs is not None else (Z1, Z1L, Yfm, Yt, 0)
            sv = src[:].rearrange("p (n j) -> p j n", j=32)
            for u in range(2 if long else 1):
                for jb in range(2):
                    b = k.psum()
                    pv = psbf[b].rearrange("p (j c) -> p j c", c=64)
                    k.op("pe", [lambda e, jj=jj, pv=pv, jb=jb, u=u: e.transpose(
                        out=pv[:, jj, :], in_=sv[64 * h:64 * h + 64, jb * 16 + jj, u * 128:(u + 1) * 128],
                        identity=ident_bf[64 * h:64 * h + 64, 64 * h:64 * h + 64]) for jj in range(16)],
                        reads=list(skeys) + [("ident_bf",)], writes=[("ps", b)])
                    if long:
                        copy_op(k, ev(), Z1L_[:, u, jb * 16:(jb + 1) * 16, :], pv, reads=[("ps", b)],
                                writes=[("Z1L", tg_, u, jb)])
                    else:
                        copy_op(k, ev(), Z1_[:, jb * 16:(jb + 1) * 16, :], pv, reads=[("ps", b)], writes=[("Z1", tg_, jb)])
            yield
            for q in range(4):
                for jg in range(4):
                    b = k.psum()
                    if long:
                        k.op("pe", [lambda e, b=b, q=q, jg=jg: e.matmul(
                            ps[b][:, :], lhsT=F1pad[:, q, :], rhs=Z1L_[:, 0, jg * 8:(jg + 1) * 8, :], start=True,
                            stop=False),
                            lambda e, b=b, q=q, jg=jg: e.matmul(
                            ps[b][:, :], lhsT=F1pad2[:, q, :], rhs=Z1L_[:, 1, jg * 8:(jg + 1) * 8, :], start=False,
                            stop=True)],
                            reads=[("F1pad",), ("F1pad2",), ("Z1L", tg_, 0, jg // 2), ("Z1L", tg_, 1, jg // 2)],
                            writes=[("ps", b)])
                    else:
                        k.op("pe", lambda e, b=b, q=q, jg=jg: e.matmul(
                            ps[b][:, :], lhsT=F1pad[:, q, :], rhs=Z1_[:, jg * 8:(jg + 1) * 8, :], start=True,
                            stop=True),
                            reads=[("F1pad",), ("Z1", tg_, jg // 2)], writes=[("ps", b)])
                    t20 = 32 * q + 8 * jg
                    copy_op(k, ev(), Yfm_[:, t20:t20 + 8, :], ps[b][:, :].rearrange("p (j c) -> p j c", c=64),
                            reads=[("ps", b)], writes=[("Yfm", tg_, t20 // 8)])
            yield
            for cb in range(8):
                b = k.psum()
                pv = psbf[b].rearrange("p (c f) -> p c f", f=128)
                k.op("pe", [lambda e, cc=cc, pv=pv, cb=cb: e.transpose(
                    out=pv[:, cc, :], in_=Yfm_[:, :, cb * 8 + cc], identity=ident_bf[:]) for cc in range(8)],
                    reads=[("Yfm", tg_, i) for i in range(16)] + [("ident_bf",)], writes=[("ps", b)])
                copy_op(k, ev(), Yt_[:, cb * 8:(cb + 1) * 8, :], pv, reads=[("ps", b)], writes=[("Yt", tg_, cb)])
            yield
            for fb in range(8):
                b = k.psum()
                fns = []
                for fl in range(8):
                    f1 = fb * 8 + fl
                    fns.append(lambda e, b=b, f1=f1, fl=fl: e.matmul(
                        ps[b][:, fl * 64:(fl + 1) * 64], lhsT=W4[:, f1, 0, :], rhs=Yt_[:, :, f1], start=True,
                        stop=False))
                    fns.append(lambda e, b=b, f1=f1, fl=fl: e.matmul(
                        ps[b][:, fl * 64:(fl + 1) * 64], lhsT=W4[:, f1, 1, :], rhs=Yt_[:, :, 64 + f1], start=False,
                        stop=True))
                k.op("pe", fns, reads=[("W4",)] + [("Yt", tg_, i) for i in range(8)], writes=[("ps", b)])
                copy_op(k, ev(), xs[:, fb * 8:(fb + 1) * 8, :], ps[b][:, :].rearrange("p (f c) -> p f c", c=64),
                        reads=[("ps", b)], writes=[(xkey, fb)])


        def fwd_half(*a_, **kw_):
            for _ in fwd_half_gen(*a_, **kw_):
                pass

        hspec = nc.dram_tensor("hspec", [16, 128, 64 * 64], BF, kind=("ExternalOutput" if DEBUG else "Internal")).ap()
        dbgB = nc.dram_tensor("dbgB", [4, 128, L], BF, kind="ExternalOutput").ap() if DEBUG else None

        with contextlib.ExitStack() as st3:
            def sb3(name, shape, dt):
                return st3.enter_context(nc.sbuf_tensor("s_" + name, shape, dt))
            zsrc = [sb3("zsrc0", [128, 2 * L], BF)] * 2
            bufs1 = (None, sb3("Z1L2", [128, 2, 32, 64], BF), sb3("Yfm2", [128, 128, 64], BF),
                     sb3("Yt2", [128, 64, 128], BF), 1)
            winp = [sb3(f"winp{i}", [128, 512], F32) for i in range(2)]
            k.op("pool", lambda e: e.memset(zsrc[0][:, L:L + 1], 0.0), writes=[("zsrc", 0)])

            def make_filter(j, o, d, zs, zkey):
                col0 = o * 1024 + d * 512 + j * 128
                for tg in range(8):
                    b = k.psum()
                    k.op("pe", lambda e, b=b, tg=tg: e.matmul(ps[b][:, :], lhsT=w3bf[:, col0:col0 + 128],
                                                             rhs=hid2[:, tg * 512:(tg + 1) * 512], start=True, stop=True),
                         reads=[("w3bf",), ("hid2",)], writes=[("ps", b)])
                    w = tg % 2
                    k.op("act", lambda e, w=w, tg=tg: e.activation(out=winp[w][:], in_=tl512[:], func=AF.Exp,
                                                                   scale=nad[:, j:j + 1],
                                                                   bias=wbias[:, j * 8 + tg:j * 8 + tg + 1]),
                         reads=[("tl512",), ("nad",), ("wbias",)], writes=[("winp", w)])
                    if d == 0:
                        k.op("dve", lambda e, b=b, w=w, tg=tg: e.tensor_tensor(
                            out=zs[:, tg * 512:(tg + 1) * 512], in0=ps[b][:, :], in1=winp[w][:], op=ALU.mult),
                            reads=[("ps", b), ("winp", w)], writes=[zkey])
                    else:
                        lo = 1 if tg == 0 else 0
                        start = 2 * L - 512 * tg - lo
                        stop = 2 * L - 512 * tg - 512
                        k.op("dve", lambda e, b=b, w=w, lo=lo, start=start, stop=stop: e.scalar_tensor_tensor(
                            out=zs[:, start:stop:-1], in0=ps[b][:, lo:512], scalar=-1.0, in1=winp[w][:, lo:512],
                            op0=ALU.mult, op1=ALU.mult),
                            reads=[("ps", b), ("winp", w)], writes=[zkey])

            for j in range(4):
                for o in range(2):
                    zi = 0
                    make_filter(j, o, 0, zsrc[zi], ("zsrc", zi))
                    make_filter(j, o, 1, zsrc[zi], ("zsrc", zi))
                    import itertools
                    g0 = fwd_half_gen(zsrc[zi], [("zsrc", zi)], 0, Xs[0], "Xs0", long=True)
                    g1 = fwd_half_gen(zsrc[zi], [("zsrc", zi)], 1, Xs[1], "Xs1", long=True, bufs=bufs1)
                    for _ in itertools.zip_longest(g0, g1):
                        pass
                    for h in range(2):
                        idx = (j * 2 + o) * 2 + h
                        k.dma("sp", hspec[idx], Xs[h][:].rearrange("p a b -> p (a b)"),
                              reads=[("Xs%d" % h, i) for i in range(8)], writes=[("hspec", idx)])
        k.barrier()
        stf.close()

        raw = [sb("raw0", [128, L + 2], BF)]
        zc = sb("zc", [128, L], BF)
        xg = sb("xg", [128, L], BF)
        ctmp = [sb(f"ctmp{i}", [128, 512], F32) for i in range(2)]
        Vfm = sb("Vfm", [128, 128, 128], BF)
        Hn = [sb("Hn0", [128, 64, 64], BF)]
        hct = [sb(f"hct{i}", [128, 8, 64], BF) for i in range(4)]
        Xp = Xs[1]
        ptmp = [sb(f"ptmp{i}", [128, 512], F32) for i in range(4)]
        gtmp = [sb(f"gtmp{i}", [128, 512], F32) for i in range(2)]
        sqb = [sb(f"sqb{i}", [128, 512], BF) for i in range(2)]
        nstg = [sb(f"nstg{i}", [128, 512], BF) for i in range(2)]
        for i, r in enumerate(raw):
            k.op("pool", lambda e, r=r: e.memset(r[:, 0:1], 0.0), writes=[("rawpadl", i)])
            k.op("pool", lambda e, r=r: e.memset(r[:, L + 1:L + 2], 0.0), writes=[("rawpadr", i)])

        def short_conv(row0, wcol, dst, dkeys):
            r = 0
            k.dma("sp", raw[r][:, 1:L + 1], projT[row0:row0 + 128, :], reads=[("projT", row0 // 128)],
                  writes=[("raw", r)])
            rk = [("raw", r), ("rawpadl", r), ("rawpadr", r), ("hy_cw",), ("hy_cb",)]
            for pc in range(8):
                c = pc % 2
                s0 = pc * 512
                k.op("dve", lambda e, s0=s0, c=c: e.tensor_scalar(
                    out=ctmp[c][:], in0=raw[r][:, s0:s0 + 512], scalar1=hy_cw[:, wcol * 3:wcol * 3 + 1],
                    scalar2=hy_cb[:, wcol:wcol + 1], op0=ALU.mult, op1=ALU.add),
                    reads=rk, writes=[("ctmp", c)])
                k.op("dve", lambda e, s0=s0, c=c: e.scalar_tensor_tensor(
                    out=ctmp[c][:], in0=raw[r][:, s0 + 1:s0 + 513], scalar=hy_cw[:, wcol * 3 + 1:wcol * 3 + 2],
                    in1=ctmp[c][:], op0=ALU.mult, op1=ALU.add),
                    reads=rk + [("ctmp", c)], writes=[("ctmp", c)])
                k.op("dve", lambda e, s0=s0, c=c: e.scalar_tensor_tensor(
                    out=dst[:, s0:s0 + 512], in0=raw[r][:, s0 + 2:s0 + 514],
                    scalar=hy_cw[:, wcol * 3 + 2:wcol * 3 + 3], in1=ctmp[c][:], op0=ALU.mult, op1=ALU.add),
                    reads=rk + [("ctmp", c)], writes=dkeys)

        def inv_half(h):
            vt_v = Yt[:].rearrange("p c (r f) -> p f r c", r=2)
            for g4 in range(16):
                b = k.psum()
                fns = []
                for fl in range(4):
                    for ro in range(2):
                        f1 = g4 * 4 + fl
                        slot = fl * 2 + ro
                        fns.append(lambda e, b=b, f1=f1, ro=ro, slot=slot: e.matmul(
                            ps[b][:, slot * 64:(slot + 1) * 64], lhsT=WI[:, f1, ro, :], rhs=Xp[:, f1, :],
                            start=True, stop=True))
                k.op("pe", fns, reads=[("WI",), ("Xs1", g4 // 2)], writes=[("ps", b)])
                copy_op(k, ev(), vt_v[:, g4 * 4:(g4 + 1) * 4, :, :],
                        ps[b][:, :].rearrange("p (f r c) -> p f r c", r=2, c=64),
                        reads=[("ps", b)], writes=[("Yt", 0, i) for i in range(8)])
            for cb in range(8):
                b = k.psum()
                pv = psbf[b].rearrange("p (c f) -> p c f", f=128)
                k.op("pe", [lambda e, cc=cc, pv=pv, cb=cb: e.transpose(
                    out=pv[:, cc, :], in_=Yt[:, cb * 8 + cc, :], identity=ident_bf[:]) for cc in range(8)],
                    reads=[("Yt", 0, cb), ("ident_bf",)], writes=[("ps", b)])
                c0 = 64 * h + cb * 8
                copy_op(k, ev(), Vfm[:, c0:c0 + 8, :], pv, reads=[("ps", b)], writes=[("Vfm", c0 // 8)])

        def inv_finish(j, o):
            zv = zc[:].rearrange("p (t1 t2) -> p t2 t1", t2=128)
            xv = xg[:].rearrange("p (t1 t2) -> p t2 t1", t2=128)
            sk = hy_skip[:, o * 4 + j:o * 4 + j + 1]
            for b8 in range(8):
                b = k.psum()
                k.op("pe", [lambda e, b=b, tl=tl, b8=b8: e.matmul(
                    ps[b][:, tl * 32:(tl + 1) * 32], lhsT=Vfm[:, :, b8 * 16 + tl], rhs=Gm[:, :], start=True,
                    stop=True) for tl in range(16)],
                    reads=[("Vfm", i) for i in range(16)] + [("Gm",)], writes=[("ps", b)])
                g = cnt["gt"] % 2
                cnt["gt"] += 1
                gv = gtmp[g][:].rearrange("p (a b) -> p a b", b=32)
                k.op("dve", lambda e, b=b, b8=b8, gv=gv: e.scalar_tensor_tensor(
                    out=gv, in0=zv[:, b8 * 16:(b8 + 1) * 16, :], scalar=sk,
                    in1=ps[b][:, :].rearrange("p (a b) -> p a b", b=32), op0=ALU.mult, op1=ALU.add),
                    reads=[("ps", b), ("zc", b8), ("hy_skip",)], writes=[("gtmp", g)])
                k.op("pool", lambda e, b8=b8, gv=gv: e.tensor_tensor(
                    out=zv[:, b8 * 16:(b8 + 1) * 16, :], in0=gv, in1=xv[:, b8 * 16:(b8 + 1) * 16, :], op=ALU.mult),
                    reads=[("gtmp", g)] + XGK, writes=[("zc", b8)])

        def hy_norm(j):
            for tg in range(8):
                sl = slice(tg * 512, (tg + 1) * 512)
                q = tg % 2
                k.op("pool", lambda e, sl=sl, q=q: e.tensor_tensor(out=sqb[q][:], in0=zc[:, sl], in1=zc[:, sl],
                                                                   op=ALU.mult),
                     reads=ZCK, writes=[("sqb", q)])
                b = k.psum()
                k.op("pe", lambda e, b=b, q=q: e.matmul(ps[b][:, :], lhsT=blk64[:, :], rhs=sqb[q][:], start=True,
                                                        stop=True),
                     reads=[("blk64",), ("sqb", q)], writes=[("ps", b)])
                p = cnt["pt"] % 4
                cnt["pt"] += 1
                k.op("act", lambda e, b=b, p=p: e.activation(out=ptmp[p][:], in_=ps[b][:, :], func=AF.Ln,
                                                             scale=1.0 / 64.0, bias=epsc[:, 0:1]),
                     reads=[("ps", b), ("epsc",)], writes=[("ptmp", p)])
                k.op("act", lambda e, p=p: e.activation(out=ptmp[p][:], in_=ptmp[p][:], func=AF.Exp, scale=-0.5),
                     reads=[("ptmp", p)], writes=[("ptmp", p)])
                k.op("dve", lambda e, sl=sl, p=p, q=q: e.scalar_tensor_tensor(
                    out=nstg[q][:], in0=zc[:, sl], scalar=hy_ng[:, j:j + 1], in1=ptmp[p][:], op0=ALU.mult,
                    op1=ALU.mult), reads=ZCK + [("ptmp", p), ("hy_ng",)], writes=[("nstg", q)])
                t = k.dma("sp", mixT[j * 128:(j + 1) * 128, sl], nstg[q][:], reads=[("nstg", q)],
                          writes=[("mixT", j, tg)])
                finals.append(t)


        def spectral_mul(h, hi):
            fwd_half(zc, ZCK, h, Xs[0], "Xs0")
            for g in range(8):
                sl = slice(g * 8, (g + 1) * 8)
                hb = []
                for w in range(2):
                    b = k.psum()
                    hq = (2 * g + w) % 4
                    k.op("pe", lambda e, b=b, sl=sl, w=w: e.matmul(
                        ps[b][:, :], lhsT=Dmat[:, w, :], rhs=Hn[hi][:, sl, :], start=True, stop=True),
                        reads=[("Dmat",), ("Hn", hi)], writes=[("ps", b)])
                    copy_op(k, "act", hct[hq][:], ps[b][:, :].rearrange("p (f c) -> p f c", c=64),
                            reads=[("ps", b)], writes=[("hct", hq)])
                    hb.append(hq)
                b = k.psum()
                k.op("pe", lambda e, b=b, sl=sl: e.matmul(ps[b][:, :], lhsT=Pswap[:, :], rhs=Xs[0][:, sl, :],
                                                         start=True, stop=True),
                     reads=[("Pswap",), ("Xs0", g)], writes=[("ps", b)])
                p = cnt["pt"] % 4
                cnt["pt"] += 1
                p2 = cnt["pt"] % 4
                cnt["pt"] += 1
                pv = ptmp[p][:].rearrange("p (f c) -> p f c", c=64)
                pv2 = ptmp[p2][:].rearrange("p (f c) -> p f c", c=64)
                k.op("pool", lambda e, sl=sl, pv=pv, hq=hb[0]: e.tensor_tensor(
                    out=pv, in0=Xs[0][:, sl, :], in1=hct[hq][:], op=ALU.mult),
                    reads=[("Xs0", g), ("hct", hb[0])], writes=[("ptmp", p)])
                k.op("dve", lambda e, b=b, pv2=pv2, hq=hb[1]: e.tensor_tensor(
                    out=pv2, in0=ps[b][:, :].rearrange("p (f c) -> p f c", c=64), in1=hct[hq][:],
                    op=ALU.mult), reads=[("ps", b), ("hct", hb[1])], writes=[("ptmp", p2)])
                k.op("pool", lambda e, sl=sl, pv=pv, pv2=pv2: e.tensor_tensor(out=Xp[:, sl, :], in0=pv,
                                                                           in1=pv2, op=ALU.add),
                     reads=[("ptmp", p), ("ptmp", p2)], writes=[("Xs1", g)])

        ZCK = [("zc", i) for i in range(8)]
        XGK = [("xg",)]
        for j in range(4):
            short_conv(j * 128, j, zc, ZCK)
            if DEBUG and j == 0:
                finals.append(k.dma("sp", dbgB[0], zc[:], reads=ZCK, writes=[("dbgB", 0)]))
            for o in range(2):
                short_conv(512 * (o + 1) + j * 128, 4 * (o + 1) + j, xg, XGK)
                if DEBUG and j == 0 and o == 0:
                    finals.append(k.dma("sp", dbgB[3], xg[:], reads=XGK, writes=[("dbgB", 3)]))
                for h in range(2):
                    idx = (j * 2 + o) * 2 + h
                    hi = 0
                    k.dma("sp", Hn[hi][:].rearrange("p a b -> p (a b)"), hspec[idx], reads=[("hspec", idx)],
                          writes=[("Hn", hi)])
                    spectral_mul(h, hi)
                    inv_half(h)
                inv_finish(j, o)
                if DEBUG and j == 0:
                    finals.append(k.dma("sp", dbgB[1 + o], zc[:], reads=ZCK, writes=[("dbgB", 1 + o)]))
            hy_norm(j)
    k.barrier()


def phase_C(nc, k, dr, ps, psbf, ident_bf, ident_f, projT, gatesT, mixT, finals):
    import contextlib
    LNS = -0.5 * math.log(128.0)
    NCH = 32
    with contextlib.ExitStack() as st:
        def sb(name, shape, dt):
            return st.enter_context(nc.sbuf_tensor("s_" + name, shape, dt))
        sel = sb("sel", [8, 8, 128], F32)
        mask = [sb("mask_f", [128, 128], F32), sb("mask_b", [128, 128], F32)]
        ones_bf = sb("ones_bf", [128, 128], BF)
        ml_cw = sb("ml_cw", [128, 24], F32)
        ml_cb = sb("ml_cb", [128, 8], F32)
        ml_ng = sb("ml_ng", [128, 4], F32)
        gbi = sb("gbi", [8, 1], F32)
        gbf = sb("gbf", [8, 1], F32)
        dmk = sb("dmk", [8, 2], F32)
        negA = sb("negA", [8, L + 2], F32)
        eM = sb("eM", [8, L], F32)
        acol = sb("acol", [128, NCH, 8], F32)
        acols = sb("acols", [128, NCH, 8], F32)
        k.dma("sp", sel[:], dr["sel"][0:8, 0:1024].rearrange("p (a b) -> p a b", a=8), writes=[("sel",)])
        k.dma("sp", mask[0][:], dr["mask_f"], writes=[("mask", 0)])
        k.dma("sp", mask[1][:], dr["mask_b"], writes=[("mask", 1)])
        k.dma("sp", ones_bf[:], dr["ones_bf"], writes=[("ones_bf",)])
        k.dma("sp", ml_cw[:], dr["ml_cw"], writes=[("ml_cw",)])
        k.dma("sp", ml_cb[:], dr["ml_cb"], writes=[("ml_cb",)])
        k.dma("sp", ml_ng[:], dr["ml_ng"], writes=[("ml_ng",)])
        k.dma("sp", dmk[:], dr["dirmask"], writes=[("dmk",)])
        k.dma("sp", gbi[0:4, :], dr["ml_gb"][0:4, :], writes=[("gbi",)])
        k.dma("sp", gbi[4:8, :], dr["ml_gb"][8:12, :], writes=[("gbi",)])
        k.dma("sp", gbf[0:4, :], dr["ml_gb"][4:8, :], writes=[("gbf",)])
        k.dma("sp", gbf[4:8, :], dr["ml_gb"][12:16, :], writes=[("gbf",)])

        with contextlib.ExitStack() as st2:
            def sb2(name, shape, dt):
                return st2.enter_context(nc.sbuf_tensor("s_" + name, shape, dt))
            T1 = sb2("T1", [8, L], F32)
            T2 = sb2("T2", [8, L], F32)
            T3 = sb2("T3", [8, L], F32)
            T4 = sb2("T4", [8, L], F32)
            nfb = sb2("nfb", [8, 1], F32)
            k.dma("sp", T4[0:4, :], gatesT[0:4, :], reads=[("gatesT",)], writes=[("T4",)])
            k.dma("sp", T4[4:8, :], gatesT[8:12, :], reads=[("gatesT",)], writes=[("T4",)])
            k.dma("sp", T1[0:4, :], gatesT[4:8, :], reads=[("gatesT",)], writes=[("T1",)])
            k.dma("sp", T1[4:8, :], gatesT[12:16, :], reads=[("gatesT",)], writes=[("T1",)])
            k.op("dve", lambda e: e.tensor_scalar(out=nfb[:], in0=gbf[:], scalar1=-1.0, scalar2=None, op0=ALU.mult),
                 reads=[("gbf",)], writes=[("nfb",)])
            k.op("act", lambda e: e.activation(out=T1[:], in_=T1[:], func=AF.Exp, scale=-1.0, bias=nfb[:, 0:1]),
                 reads=[("T1",), ("nfb",)], writes=[("T1",)])
            k.op("act", lambda e: e.activation(out=T1[:], in_=T1[:], func=AF.Ln, bias=1.0),
                 reads=[("T1",)], writes=[("T1",)])
            k.op("dve", lambda e: e.tensor_tensor_scan(out=T2[:], data0=T1[:], data1=T1[:], initial=0.0,
                                                       op0=ALU.add, op1=ALU.max),
                 reads=[("T1",)], writes=[("T2",)])
            k.op("dve", lambda e: e.tensor_tensor_scan(out=T3[:, ::-1], data0=T1[:, ::-1], data1=T1[:, ::-1],
                                                       initial=0.0, op0=ALU.add, op1=ALU.max),
                 reads=[("T1",)], writes=[("T3",)])
            k.op("dve", lambda e: e.tensor_scalar(out=T2[:], in0=T2[:], scalar1=dmk[:, 0:1], scalar2=None,
                                                  op0=ALU.mult), reads=[("T2",), ("dmk",)], writes=[("T2",)])
            k.op("dve", lambda e: e.scalar_tensor_tensor(out=T2[:], in0=T3[:], scalar=dmk[:, 1:2], in1=T2[:],
                                                         op0=ALU.mult, op1=ALU.add),
                 reads=[("T2",), ("T3",), ("dmk",)], writes=[("T2",)])
            k.op("dve", lambda e: e.scalar_tensor_tensor(out=T4[:], in0=T4[:], scalar=gbi[:, 0:1], in1=T2[:],
                                                         op0=ALU.add, op1=ALU.add),
                 reads=[("T4",), ("T2",), ("gbi",)], writes=[("T4",)])
            k.op("dve", lambda e: e.tensor_tensor_scan(out=T3[:], data0=T4[:], data1=T4[:], initial=0.0,
                                                       op0=ALU.max, op1=ALU.max),
                 reads=[("T4",)], writes=[("T3",)])
            k.op("dve", lambda e: e.tensor_tensor_scan(out=T1[:, ::-1], data0=T4[:, ::-1], data1=T4[:, ::-1],
                                                       initial=0.0, op0=ALU.max, op1=ALU.max),
                 reads=[("T4",)], writes=[("T1",)])
            k.op("dve", lambda e: e.tensor_scalar(out=T3[:], in0=T3[:], scalar1=dmk[:, 0:1], scalar2=None,
                                                  op0=ALU.mult), reads=[("T3",), ("dmk",)], writes=[("T3",)])
            k.op("dve", lambda e: e.scalar_tensor_tensor(out=T3[:], in0=T1[:], scalar=dmk[:, 1:2], in1=T3[:],
                                                         op0=ALU.mult, op1=ALU.add),
                 reads=[("T1",), ("T3",), ("dmk",)], writes=[("T3",)])
            k.op("pool", lambda e: e.memset(negA[:, 0:1], 0.0), writes=[("negApl",)])
            k.op("pool", lambda e: e.memset(negA[:, L + 1:L + 2], 0.0), writes=[("negApr",)])
            k.op("dve", lambda e: e.tensor_scalar(out=negA[:, 1:L + 1], in0=T3[:], scalar1=-1.0, scalar2=None,
                                                  op0=ALU.mult), reads=[("T3",)], writes=[("negA",)])
            k.op("dve", lambda e: e.tensor_tensor(out=T2[:], in0=T2[:], in1=T3[:], op=ALU.subtract),
                 reads=[("T2",), ("T3",)], writes=[("T2",)])
            k.op("act", lambda e: e.activation(out=eM[:], in_=T2[:], func=AF.Exp), reads=[("T2",)],
                 writes=[("eM",)])
            b = k.psum()
            k.op("pe", [lambda e, n=n, b=b: e.transpose(out=ps[b][:, n * 8:(n + 1) * 8],
                                                       in_=T4[0:8, n * 128:(n + 1) * 128],
                                                       identity=ident_f[0:8, 0:8]) for n in range(NCH)],
                 reads=[("T4",), ("ident_f",)], writes=[("ps", b)])
            k.op("dve", lambda e, b=b: e.tensor_copy(out=acol[:].rearrange("p a b -> p (a b)"), in_=ps[b][:, 0:256]),
                 reads=[("ps", b)], writes=[("acol",)])
            k.op("dve", lambda e: e.tensor_scalar(out=acols[:].rearrange("p a b -> p (a b)"),
                                                  in0=acol[:].rearrange("p a b -> p (a b)"), scalar1=LNS,
                                                  scalar2=None, op0=ALU.add), reads=[("acol",)], writes=[("acols",)])
        k.barrier()

        raw = sb("rawc", [128, L + 2], BF)
        ctmp = [sb(f"cct{i}", [128, 512], F32) for i in range(2)]
        ctm2 = [sb(f"cc2{i}", [128, 512], F32) for i in range(2)]
        qT = [sb(f"qT{i}", [128, L], BF) for i in range(2)]
        kT = [sb(f"kT{i}", [128, L], BF) for i in range(2)]
        ktm = [sb(f"ktm{i}", [128, NCH, 128], BF) for i in range(2)]
        vext = [sb(f"vext{i}", [128, NCH, 256], BF) for i in range(2)]
        hT = [sb(f"hT{i}", [128, L], F32) for i in range(2)]
        Cst = [sb(f"Cst{i}", [128, 256], F32) for i in range(4)]
        Cbf = [sb(f"Cbf{i}", [128, 256], BF) for i in range(4)]
        scA = [sb(f"scA{i}", [128, 4, NCH], F32) for i in range(4)]
        NB = 8
        wT = [sb(f"wT{i}", [128, 128], BF) for i in range(NB)]
        wTm = [sb(f"wTm{i}", [128, 128], BF) for i in range(NB)]
        itb = [sb(f"itb{i}", [128, 128], BF) for i in range(NB)]
        qtl = [sb(f"qtl{i}", [128, 128], BF) for i in range(NB)]
        pT = [sb(f"pT{i}", [128, 128], BF) for i in range(NB)]
        vw = [sb(f"vw{i}", [128, 256], BF) for i in range(NB)]
        sc = [sb(f"sc{i}", [128, 8], F32) for i in range(NB)]
        dn = [sb(f"dn{i}", [128, 128], F32) for i in range(NB)]
        hb = [sb(f"hb{i}", [128, 128], F32) for i in range(NB)]
        ntmp = ctmp
        nsq = [sb(f"nsq{i}", [128, 512], BF) for i in range(2)]
        nrt = [ctm2[0]] * 2
        emr = [sb(f"emr{i}", [128, 128], F32) for i in range(NB)]
        nout = [sb(f"nout{i}", [128, 512], BF) for i in range(2)]
        k.op("pool", lambda e: e.memset(raw[:, 0:1], 0.0), writes=[("rawcl",)])
        k.op("pool", lambda e: e.memset(raw[:, L + 1:L + 2], 0.0), writes=[("rawcr",)])
        for i in range(2):
            k.op("pool", lambda e, i=i: e.memset(vext[i][:, :, 128:256], 1.0), writes=[("vones", i)])
        cnt = {"ev": 0, "r": 0}

        def ev():
            cnt["ev"] += 1
            return evac_engine(cnt["ev"])

        def conv_silu(row0, wcol, dst, dkey):
            k.dma("sp", raw[:, 1:L + 1], projT[row0:row0 + 128, :], reads=[("projT", row0 // 128)],
                  writes=[("rawc",)])
            rk = [("rawc",), ("rawcl",), ("rawcr",), ("ml_cw",), ("ml_cb",)]
            for pc in range(8):
                c = pc % 2
                s0 = pc * 512
                k.op("dve", lambda e, s0=s0, c=c: e.tensor_scalar(
                    out=ctmp[c][:], in0=raw[:, s0:s0 + 512], scalar1=ml_cw[:, wcol * 3:wcol * 3 + 1],
                    scalar2=ml_cb[:, wcol:wcol + 1], op0=ALU.mult, op1=ALU.add), reads=rk, writes=[("cct", c)])
                k.op("dve", lambda e, s0=s0, c=c: e.scalar_tensor_tensor(
                    out=ctmp[c][:], in0=raw[:, s0 + 1:s0 + 513], scalar=ml_cw[:, wcol * 3 + 1:wcol * 3 + 2],
                    in1=ctmp[c][:], op0=ALU.mult, op1=ALU.add), reads=rk + [("cct", c)], writes=[("cct", c)])
                k.op("dve", lambda e, s0=s0, c=c: e.scalar_tensor_tensor(
                    out=ctm2[c][:], in0=raw[:, s0 + 2:s0 + 514], scalar=ml_cw[:, wcol * 3 + 2:wcol * 3 + 3],
                    in1=ctmp[c][:], op0=ALU.mult, op1=ALU.add), reads=rk + [("cct", c)], writes=[("cc2", c)])
                k.op("act", lambda e, s0=s0, c=c: e.activation(out=dst[:, s0:s0 + 512], in_=ctm2[c][:],
                                                               func=AF.Silu),
                     reads=[("cc2", c)], writes=[dkey])

        def to_token_major(src, skey, dst, dkey, width):
            for nb in range(4):
                b = k.psum()
                pv = psbf[b].rearrange("p (n f) -> p n f", f=128)
                k.op("pe", [lambda e, nn=nn, pv=pv, nb=nb: e.transpose(
                    out=pv[:, nn, :], in_=src[:, (nb * 8 + nn) * 128:(nb * 8 + nn + 1) * 128], identity=ident_bf[:])
                    for nn in range(8)], reads=[skey, ("ident_bf",)], writes=[("ps", b)])
                copy_op(k, ev(), dst[:, nb * 8:(nb + 1) * 8, 0:128], pv, reads=[("ps", b)], writes=[dkey])

        ctxs = {}

        def chain_scalars(head, d, cidx):
            c = d * 4 + head
            off = 0 if d == 0 else 1
            b = k.psum()
            k.op("pe", lambda e: e.matmul(ps[b][:, 0:33], lhsT=sel[:, c, :], rhs=negA[:, off:off + 32 * 128 + 1:128],
                                          start=True, stop=True),
                 reads=[("sel",), ("negA",), ("negApl",), ("negApr",)], writes=[("ps", b)])
            prev0, end0 = (0, 1) if d == 0 else (1, 0)
            t = scA[cidx]
            k.op("dve", lambda e: e.tensor_scalar(out=t[:, 0, :], in0=ps[b][:, prev0:prev0 + NCH], scalar1=-1.0,
                                                  scalar2=LNS, op0=ALU.mult, op1=ALU.add),
                 reads=[("ps", b)], writes=[("scA", cidx)])
            k.op("dve", lambda e: e.tensor_copy(out=t[:, 1, :], in_=ps[b][:, end0:end0 + NCH]),
                 reads=[("ps", b)], writes=[("scA", cidx)])
            k.op("dve", lambda e: e.tensor_tensor(out=t[:, 2, :], in0=t[:, 0, :], in1=t[:, 1, :], op=ALU.add),
                 reads=[("scA", cidx)], writes=[("scA", cidx)])
            k.op("act", lambda e: e.activation(out=t[:, 2, :], in_=t[:, 2, :], func=AF.Exp, bias=-LNS),
                 reads=[("scA", cidx)], writes=[("scA", cidx)])
            k.op("dve", lambda e: e.tensor_tensor(out=t[:, 3, :], in0=acol[:, :, c], in1=t[:, 1, :], op=ALU.add),
                 reads=[("scA", cidx), ("acol",)], writes=[("scA", cidx)])
            k.op("act", lambda e: e.activation(out=t[:, 3, :], in_=t[:, 3, :], func=AF.Exp),
                 reads=[("scA", cidx)], writes=[("scA", cidx)])

        def stage_a1(hl, head, d, n, cidx):
            c = d * 4 + head
            r = cnt["r"] % NB
            cnt["r"] += 1
            t0 = n * 128
            bx = k.psum()
            ctxs[(cidx, n)] = (r, bx)
            k.op("pe", [lambda e: e.matmul(ps[bx][:, 0:128], lhsT=sel[:, c, :], rhs=negA[:, t0 + 1:t0 + 129],
                                           start=True, stop=True),
                        lambda e: e.matmul(ps[bx][:, 130:258], lhsT=sel[:, c, :], rhs=eM[:, t0:t0 + 128],
                                           start=True, stop=True),
                        lambda e: e.matmul(ps[bx][:, 258:386], lhsT=kT[hl][:, t0:t0 + 128],
                                           rhs=qT[hl][:, t0:t0 + 128], start=True, stop=True)],
                 reads=[("sel",), ("negA",), ("negApl",), ("negApr",), ("eM",), ("kT", hl), ("qT", hl)],
                 writes=[("ps", bx)])
            k.op("act", lambda e: e.activation(out=wT[r][:], in_=ps[bx][:, 0:128], func=AF.Exp,
                                               bias=acols[:, n, c:c + 1]),
                 reads=[("ps", bx), ("acols",)], writes=[("wT", r)])
            k.op("act", lambda e: e.activation(out=itb[r][:], in_=ps[bx][:, 0:128], func=AF.Exp,
                                               bias=scA[cidx][:, 0, n:n + 1]),
                 reads=[("ps", bx), ("scA", cidx)], writes=[("itb", r)])
            k.op("act", lambda e: e.copy(out=emr[r][:], in_=ps[bx][:, 130:258]), reads=[("ps", bx)],
                 writes=[("emr", r)])

        def stage_a2(hl, head, d, n, cidx):
            r, bx = ctxs[(cidx, n)]
            t0 = n * 128
            k.op("pool", lambda e: e.tensor_tensor(out=wTm[r][:], in0=wT[r][:], in1=mask[d][:], op=ALU.mult),
                 reads=[("wT", r), ("mask", d)], writes=[("wTm", r)])
            k.op("pool", lambda e: e.tensor_tensor(out=qtl[r][:], in0=qT[hl][:, t0:t0 + 128], in1=itb[r][:],
                                                   op=ALU.mult),
                 reads=[("qT", hl), ("itb", r)], writes=[("qtl", r)])
            k.op("pool", lambda e: e.tensor_scalar(out=vw[r][:], in0=vext[hl][:, n, :], scalar1=scA[cidx][:, 3, n:n + 1],
                                                   scalar2=1.0, op0=ALU.mult, op1=ALU.mult),
                 reads=[("vext", hl), ("vones", hl), ("scA", cidx)], writes=[("vw", r)])
            k.op("dve", lambda e: e.tensor_tensor(out=pT[r][:], in0=ps[bx][:, 258:386], in1=wTm[r][:],
                                                  op=ALU.mult),
                 reads=[("ps", bx), ("wTm", r)], writes=[("pT", r)])

        def stage_b1(hl, head, d, n, cidx):
            r, bx = ctxs[(cidx, n)]
            by = k.psum()
            ctxs[(cidx, n)] = (r, bx, by)
            k.op("pe", [lambda e: e.matmul(ps[by][:, 0:128], lhsT=vext[hl][:, n, 0:128], rhs=pT[r][:],
                                           start=True, stop=False),
                        lambda e: e.matmul(ps[by][:, 0:128], lhsT=Cbf[cidx][:, 0:128], rhs=qtl[r][:],
                                           start=False, stop=True),
                        lambda e: e.matmul(ps[by][:, 128:256], lhsT=ones_bf[:, :], rhs=pT[r][:],
                                           start=True, stop=False),
                        lambda e: e.matmul(ps[by][:, 128:256], lhsT=Cbf[cidx][:, 128:256],
                                           rhs=qtl[r][:], start=False, stop=True),
                        lambda e: e.matmul(ps[by][:, 256:512], lhsT=ktm[hl][:, n, :], rhs=vw[r][:],
                                           start=True, stop=True)],
                 reads=[("vext", hl), ("vones", hl), ("pT", r), ("Cbf", cidx), ("qtl", r), ("ones_bf",),
                        ("ktm", hl), ("vw", r)],
                 writes=[("ps", by)])

        def stage_b2(hl, head, d, n, cidx):
            r, bx, by = ctxs.pop((cidx, n))
            t0 = n * 128
            k.op("dve", lambda e: e.scalar_tensor_tensor(out=Cst[cidx][:], in0=Cst[cidx][:],
                                                         scalar=scA[cidx][:, 2, n:n + 1], in1=ps[by][:, 256:512],
                                                         op0=ALU.mult, op1=ALU.add),
                 reads=[("ps", by), ("scA", cidx), ("Cst", cidx)], writes=[("Cst", cidx)])
            k.op("act", lambda e: e.copy(out=Cbf[cidx][:], in_=Cst[cidx][:]), reads=[("Cst", cidx)],
                 writes=[("Cbf", cidx)])
            k.op("act", lambda e: e.activation(out=dn[r][:], in_=ps[by][:, 128:256], func=AF.Abs),
                 reads=[("ps", by)], writes=[("dn", r)])
            k.op("dve", lambda e: e.tensor_tensor(out=dn[r][:], in0=emr[r][:], in1=dn[r][:],
                                                  op=ALU.max),
                 reads=[("emr", r), ("dn", r)], writes=[("dn", r)])
            k.op("act", lambda e: e.activation(out=dn[r][:], in_=dn[r][:], func=AF.Ln), reads=[("dn", r)],
                 writes=[("dn", r)])
            k.op("act", lambda e: e.activation(out=dn[r][:], in_=dn[r][:], func=AF.Exp, scale=-1.0),
                 reads=[("dn", r)], writes=[("dn", r)])
            k.op("dve", lambda e: e.tensor_tensor(out=hb[r][:], in0=ps[by][:, 0:128], in1=dn[r][:],
                                                  op=ALU.mult),
                 reads=[("ps", by), ("dn", r)], writes=[("hb", r)])
            k.op("pool", lambda e: e.tensor_tensor(out=hT[hl][:, t0:t0 + 128], in0=hT[hl][:, t0:t0 + 128],
                                                   in1=hb[r][:], op=ALU.add),
                 reads=[("hb", r), ("hT", hl, n)], writes=[("hT", hl, n)])

        def finish_head(hl, head):
            k.dma("sp", raw[:, 1:L + 1], projT[3072 + head * 128:3072 + (head + 1) * 128, :],
                  reads=[("projT", 24 + head)], writes=[("rawc",)])
            for tg in range(8):
                q = tg % 2
                sl = slice(tg * 512, (tg + 1) * 512)
                k.op("act", lambda e, sl=sl, q=q: e.activation(out=ntmp[q][:], in_=raw[:, 1 + sl.start:1 + sl.stop],
                                                               func=AF.Sigmoid),
                     reads=[("rawc",)], writes=[("cct", q)])
                k.op("dve", lambda e, sl=sl, q=q: e.tensor_tensor(out=ntmp[q][:], in0=ntmp[q][:], in1=hT[hl][:, sl],
                                                                  op=ALU.mult),
                     reads=[("cct", q)] + [("hT", hl, n) for n in range(tg * 4, tg * 4 + 4)], writes=[("cct", q)])
                k.op("pool", lambda e, q=q: e.tensor_tensor(out=nsq[q][:], in0=ntmp[q][:], in1=ntmp[q][:],
                                                            op=ALU.mult),
                     reads=[("cct", q)], writes=[("nsq", q)])
                b = k.psum()
                k.op("pe", lambda e, b=b, q=q: e.matmul(ps[b][:, :], lhsT=ones_bf[:, :], rhs=nsq[q][:], start=True,
                                                        stop=True), reads=[("ones_bf",), ("nsq", q)],
                     writes=[("ps", b)])
                k.op("act", lambda e, b=b, q=q: e.activation(out=nrt[q][:], in_=ps[b][:, :], func=AF.Ln,
                                                             scale=1.0 / 128.0, bias=epsc[:, 0:1]),
                     reads=[("ps", b), ("epsc",)], writes=[("cc2", 0)])
                k.op("act", lambda e, q=q: e.activation(out=nrt[q][:], in_=nrt[q][:], func=AF.Exp, scale=-0.5),
                     reads=[("cc2", 0)], writes=[("cc2", 0)])
                k.op("dve", lambda e, q=q: e.scalar_tensor_tensor(out=nout[q][:], in0=ntmp[q][:],
                                                                  scalar=ml_ng[:, head:head + 1], in1=nrt[q][:],
                                                                  op0=ALU.mult, op1=ALU.mult),
                     reads=[("cct", q), ("cc2", 0), ("ml_ng",)], writes=[("nout", q)])
                t = k.dma("sp", mixT[512 + head * 128:512 + (head + 1) * 128, sl], nout[q][:],
                          reads=[("nout", q)], writes=[("mixT", 4 + head, tg)])
                finals.append(t)

        for hp in range(2):
            for hl in range(2):
                head = hp * 2 + hl
                conv_silu(1536 + head * 128, head, qT[hl], ("qT", hl))
                conv_silu(2048 + head * 128, 4 + head, kT[hl], ("kT", hl))
                to_token_major(kT[hl], ("kT", hl), ktm[hl], ("ktm", hl), 128)
                k.dma("sp", raw[:, 1:L + 1], projT[2560 + head * 128:2560 + (head + 1) * 128, :],
                      reads=[("projT", 20 + head)], writes=[("rawc",)])
                to_token_major(raw[:, 1:L + 1], ("rawc",), vext[hl], ("vext", hl), 128)
                k.op("pool", lambda e, hl=hl: e.memset(hT[hl][:], 0.0), writes=[("hT", hl, n) for n in range(NCH)])
                for d in range(2):
                    ci = hl * 2 + d
                    chain_scalars(head, d, ci)
                    k.op("pool", lambda e, ci=ci: e.memset(Cst[ci][:], 0.0), writes=[("Cst", ci)])
                    k.op("pool", lambda e, ci=ci: e.memset(Cbf[ci][:], 0.0), writes=[("Cbf", ci)])
            def chains(step):
                return [(hl, hp * 2 + hl, d, (step if d == 0 else NCH - 1 - step), hl * 2 + d)
                        for hl in range(2) for d in range(2)]
            for step in range(NCH + 1):
                if step < NCH:
                    for a in chains(step):
                        stage_a1(*a)
                    for a in chains(step):
                        stage_a2(*a)
                if step >= 1:
                    for a in chains(step - 1):
                        stage_b1(*a)
                    for a in chains(step - 1):
                        stage_b2(*a)
            for hl in range(2):
                finish_head(hl, hp * 2 + hl)
    k.barrier()


def phase_D(nc, k, dr, ps, psbf, ident_bf, mixT, out, finals):
    import contextlib
    TG = 256
    NG = L // TG
    NT = TG // 128
    with contextlib.ExitStack() as st:
        def sb(name, shape, dt):
            return st.enter_context(nc.sbuf_tensor("s_" + name, shape, dt))
        gcols = sb("gcolsD", [128, 3, 8], F32)
        gfin = sb("gfin", [128, D], F32)
        kTm = sb("kTm", [128, 8, NMEM], BF)
        Vm = sb("Vm", [128, 2, D], BF)
        w_out = sb("w_outb", [128, 8, D], BF)
        wq = sb("wqb", [128, 8, D], BF)
        wo = sb("wob", [128, 8, D], BF)
        w2 = sb("w2b", [128, 32, D], BF)
        k.dma("sp", gcols[:, 0, :], dr["gx_col"], writes=[("gcolsD",)])
        k.dma("sp", gcols[:, 1, :], dr["gmem_col"], writes=[("gcolsD",)])
        k.dma("sp", gcols[:, 2, :], dr["gff_col"], writes=[("gcolsD",)])
        k.dma("sp", gfin[:], dr["gfin_row"].partition_broadcast(128), writes=[("gfin",)])
        cnt = {"ev": 0, "st": 0}

        def ev():
            cnt["ev"] += 1
            return evac_engine(cnt["ev"])

        w1d = dr["wscr"]["ff_w1"]
        with contextlib.ExitStack() as st2:
            def sb2(name, shape, dt):
                return st2.enter_context(nc.sbuf_tensor("s_" + name, shape, dt))
            wk = sb2("wkb", [128, 8, D], BF)
            wv = sb2("wvb", [128, 8, D], BF)
            mt = sb2("memt", [128, D], F32)
            mst = sb2("memst", [128, 4], F32)
            mn = sb2("memn", [128, D], BF)
            mnT = sb2("memnT", [128, 8, NMEM], BF)

            def wl(q, dst, nme, key, c0, c1):
                k.dma(q, dst[:, c0:c1, :], dr["wscr"][nme].rearrange("(c p) n -> p c n", p=128)[:, c0:c1, :],
                      reads=[("wscr", nme)], writes=[key])
            wl("sp", wk, "xa_wk", ("wk",), 0, 8)
            wl("act", wv, "xa_wv", ("wv",), 0, 8)
            wl("sp", w_out, "w_out", ("w_out",), 0, 8)
            wl("act", wq, "xa_wq", ("wq",), 0, 8)
            wl("sp", wo, "xa_wo", ("wo",), 0, 8)
            for i in range(4):
                wl("act" if i % 2 else "sp", w2, "ff_w2", ("w2",), i * 8, (i + 1) * 8)
            for mtile in range(2):
                k.dma("sp", mt[:], dr["mem"][mtile * 128:(mtile + 1) * 128, :], writes=[("memt",)])
                k.op("act", lambda e, mtile=mtile: e.activation(out=mn[:], in_=mt[:], func=AF.Square,
                                                                accum_out=mst[:, 0:1]),
                     reads=[("memt",)], writes=[("memn",), ("memst",)])
                k.op("act", lambda e: e.activation(out=mst[:, 1:2], in_=mst[:, 0:1], func=AF.Sqrt, scale=1.0 / D,
                                                   bias=EPS), reads=[("memst",)], writes=[("memst",)])
                k.op("dve", lambda e: e.reciprocal(out=mst[:, 2:3], in_=mst[:, 1:2]), reads=[("memst",)],
                     writes=[("memst",)])
                k.op("act", lambda e: e.activation(out=mn[:], in_=mt[:], func=AF.Copy, scale=mst[:, 2:3]),
                     reads=[("memt",), ("memst",)], writes=[("memn",)])
                b = k.psum()
                pv = psbf[b].rearrange("p (c t) -> p c t", t=128)
                k.op("pe", [lambda e, c=c, pv=pv: e.transpose(out=pv[:, c, :], in_=mn[:, c * 128:(c + 1) * 128],
                                                              identity=ident_bf[:]) for c in range(8)],
                     reads=[("memn",), ("ident_bf",)], writes=[("ps", b)])
                copy_op(k, ev(), mnT[:, :, mtile * 128:(mtile + 1) * 128], pv, reads=[("ps", b)],
                        writes=[("memnT",)])
            for cc in range(8):
                b = k.psum()
                k.op("pe", [lambda e, kc=kc, cc=cc, b=b: e.matmul(ps[b][:, 0:NMEM],
                                                                  lhsT=wk[:, kc, cc * 128:(cc + 1) * 128],
                                                                  rhs=mnT[:, kc, :], start=(kc == 0), stop=(kc == 7))
                            for kc in range(8)], reads=[("wk",), ("memnT",)], writes=[("ps", b)])
                copy_op(k, ev(), kTm[:, cc, :], ps[b][:, 0:NMEM], reads=[("ps", b)], writes=[("kTm",)])
            for mc in range(2):
                for half in range(2):
                    b = k.psum()
                    k.op("pe", [lambda e, kc=kc, mc=mc, half=half, b=b: e.matmul(
                        ps[b][:, :], lhsT=mnT[:, kc, mc * 128:(mc + 1) * 128],
                        rhs=wv[:, kc, half * 512:(half + 1) * 512], start=(kc == 0), stop=(kc == 7))
                        for kc in range(8)], reads=[("wv",), ("memnT",)], writes=[("ps", b)])
                    copy_op(k, ev(), Vm[:, mc, half * 512:(half + 1) * 512], ps[b][:, :], reads=[("ps", b)],
                            writes=[("Vm",)])
        k.barrier()

        w1s = [sb(f"w1s{i}", [128, 8, 256], BF) for i in range(3)]
        mx = sb("mxD", [128, 8, TG], BF)
        xt = sb("xtD", [128, D], F32)
        h = [[sb(f"hD{p}_{i}", [128, D], F32) for i in range(NT)] for p in range(2)]
        xn = [sb(f"xnD{i}", [128, D], BF) for i in range(2)]
        stt = sb("statD", [128, 16], F32)
        xnT1 = sb("xnT1D", [128, 8, TG], BF)
        xnT2 = [sb(f"xnT2D{i}", [128, 8, TG], BF) for i in range(2)]
        qT = sb("qTD", [128, 8, TG], BF)
        smx = sb("smx", [128, 8], F32)
        Pun = [sb(f"Pun{i}", [128, 2, NMEM], BF) for i in range(2)]
        Pn = [sb(f"Pn{i}", [128, 2, NMEM], BF) for i in range(2)]
        PT = sb("PTD", [128, 4, 2, TG], BF)
        oT = qT
        hid = sb("hidD", [128, 32, TG], BF)
        rtmp = [sb(f"rtD{i}", [128, TG], BF) for i in range(2)]
        c2 = {"w1": 0, "sx": 0, "rt": 0}
        SC = 1.0 / 16.0

        def rms_to_T(hsrc, hkey, dstT, dkey, tt, scol):
            j = tt % 2
            k.op("act", lambda e: e.activation(out=xn[j][:], in_=hsrc[:], func=AF.Square,
                                               accum_out=stt[:, scol:scol + 1]),
                 reads=[hkey], writes=[("xnD", j), ("statD", scol)])
            k.op("act", lambda e: e.activation(out=stt[:, scol + 1:scol + 2], in_=stt[:, scol:scol + 1],
                                               func=AF.Sqrt, scale=1.0 / D, bias=EPS),
                 reads=[("statD", scol)], writes=[("statD", scol)])
            k.op("dve", lambda e: e.reciprocal(out=stt[:, scol + 2:scol + 3], in_=stt[:, scol + 1:scol + 2]),
                 reads=[("statD", scol)], writes=[("statD", scol)])
            k.op("act", lambda e: e.activation(out=xn[j][:], in_=hsrc[:], func=AF.Copy,
                                               scale=stt[:, scol + 2:scol + 3]),
                 reads=[hkey, ("statD", scol)], writes=[("xnD", j)])
            b = k.psum()
            pv = psbf[b].rearrange("p (c t) -> p c t", t=128)
            k.op("pe", [lambda e, c=c, pv=pv: e.transpose(out=pv[:, c, :], in_=xn[j][:, c * 128:(c + 1) * 128],
                                                          identity=ident_bf[:]) for c in range(8)],
                 reads=[("xnD", j), ("ident_bf",)], writes=[("ps", b)])
            copy_op(k, ev(), dstT[:, :, tt * 128:(tt + 1) * 128], pv, reads=[("ps", b)], writes=[dkey])

        def proj_token_major(srcT, skey, W, wkey, nk, tt, hdst, hkey, addsrc, addkey):
            for half in range(2):
                b = k.psum()
                k.op("pe", [lambda e, kc=kc, half=half, b=b: e.matmul(
                    ps[b][:, :], lhsT=srcT[:, kc, tt * 128:(tt + 1) * 128], rhs=W[:, kc, half * 512:(half + 1) * 512],
                    start=(kc == 0), stop=(kc == nk - 1)) for kc in range(nk)],
                    reads=[skey, wkey], writes=[("ps", b)])
                k.op("dve", lambda e, b=b, half=half: e.tensor_tensor(
                    out=hdst[:, half * 512:(half + 1) * 512], in0=ps[b][:, :],
                    in1=addsrc[:, half * 512:(half + 1) * 512], op=ALU.add),
                    reads=[("ps", b), addkey], writes=[hkey])

        def X_steps(g):
            tok0 = g * TG
            hp_ = g % 2
            hh_ = h[hp_]
            xo = xnT2[hp_]

            def s1():
                k.dma("sp", mx[:], mixT.rearrange("(c p) t -> p c t", p=128)[:, :, tok0:tok0 + TG],
                      reads=[("mixT", j, tg) for j in range(8) for tg in range(8)], writes=[("mxD",)])
                for tt in range(NT):
                    k.dma("sp", xt[:], dr["x"][tok0 + tt * 128:tok0 + (tt + 1) * 128, :], writes=[("xtD",)])
                    proj_token_major(mx, ("mxD",), w_out, ("w_out",), 8, tt, hh_[tt], ("hD", hp_, tt), xt,
                                     ("xtD",))
                    rms_to_T(hh_[tt], ("hD", hp_, tt), xnT1, ("xnT1D",), tt, 0)

            def s2():
                for cc in range(8):
                    b = k.psum()
                    k.op("pe", [lambda e, kc=kc, cc=cc, b=b: e.matmul(
                        ps[b][:, 0:TG], lhsT=wq[:, kc, cc * 128:(cc + 1) * 128], rhs=xnT1[:, kc, :],
                        start=(kc == 0), stop=(kc == 7)) for kc in range(8)],
                        reads=[("wq",), ("xnT1D",)], writes=[("ps", b)])
                    copy_op(k, ev(), qT[:, cc, :], ps[b][:, 0:TG], reads=[("ps", b)], writes=[("qTD",)])

            def s3(tt):
                for hp in range(2):
                    b = k.psum()
                    fns = []
                    for hh in range(2):
                        hd = hp * 2 + hh
                        for cq in range(2):
                            fns.append(lambda e, b=b, hh=hh, hd=hd, cq=cq: e.matmul(
                                ps[b][:, hh * NMEM:(hh + 1) * NMEM], lhsT=qT[:, 2 * hd + cq, tt * 128:(tt + 1) * 128],
                                rhs=kTm[:, 2 * hd + cq, :], start=(cq == 0), stop=(cq == 1)))
                    k.op("pe", fns, reads=[("qTD",), ("kTm",)], writes=[("ps", b)])
                    sx = c2["sx"] % 2
                    c2["sx"] += 1
                    pv = ps[b][:, :].rearrange("p (a m) -> p a m", m=NMEM)
                    k.op("dve", lambda e, pv=pv, sx=sx: e.tensor_reduce(out=smx[:, sx * 4:sx * 4 + 2], in_=pv,
                                                                        axis=AX.X, op=ALU.max),
                         reads=[("ps", b)], writes=[("smx", sx)])
                    k.op("dve", lambda e, sx=sx: e.tensor_scalar(out=smx[:, sx * 4:sx * 4 + 2],
                                                                 in0=smx[:, sx * 4:sx * 4 + 2], scalar1=-SC,
                                                                 scalar2=None, op0=ALU.mult),
                         reads=[("smx", sx)], writes=[("smx", sx)])
                    for hh in range(2):
                        k.op("act", lambda e, b=b, hh=hh, sx=sx: e.activation(
                            out=Pun[sx][:, hh, :], in_=ps[b][:, hh * NMEM:(hh + 1) * NMEM], func=AF.Exp, scale=SC,
                            bias=smx[:, sx * 4 + hh:sx * 4 + hh + 1],
                            accum_out=smx[:, sx * 4 + 2 + hh:sx * 4 + 3 + hh]),
                            reads=[("ps", b), ("smx", sx)], writes=[("Pun", sx), ("smx", sx)])
                    k.op("dve", lambda e, sx=sx: e.reciprocal(out=smx[:, sx * 4 + 2:sx * 4 + 4],
                                                              in_=smx[:, sx * 4 + 2:sx * 4 + 4]),
                         reads=[("smx", sx)], writes=[("smx", sx)])
                    for hh in range(2):
                        k.op("dve", lambda e, hh=hh, sx=sx: e.tensor_scalar(
                            out=Pn[sx][:, hh, :], in0=Pun[sx][:, hh, :],
                            scalar1=smx[:, sx * 4 + 2 + hh:sx * 4 + 3 + hh], scalar2=None, op0=ALU.mult),
                            reads=[("Pun", sx), ("smx", sx)], writes=[("Pn", sx)])
                    b2 = k.psum()
                    pv2 = psbf[b2].rearrange("p (a t) -> p a t", t=128)
                    k.op("pe", [lambda e, hh=hh, mc=mc, pv2=pv2, sx=sx: e.transpose(
                        out=pv2[:, hh * 2 + mc, :], in_=Pn[sx][:, hh, mc * 128:(mc + 1) * 128], identity=ident_bf[:])
                        for hh in range(2) for mc in range(2)],
                        reads=[("Pn", sx), ("ident_bf",)], writes=[("ps", b2)])
                    copy_op(k, ev(), PT[:, hp * 2:hp * 2 + 2, :, tt * 128:(tt + 1) * 128],
                            pv2[:, 0:4, :].rearrange("p (h m) t -> p h m t", m=2), reads=[("ps", b2)],
                            writes=[("PTD",)])

            def s4():
                for cc in range(8):
                    b = k.psum()
                    k.op("pe", [lambda e, mc=mc, cc=cc, b=b: e.matmul(
                        ps[b][:, 0:TG], lhsT=Vm[:, mc, cc * 128:(cc + 1) * 128], rhs=PT[:, cc // 2, mc, :],
                        start=(mc == 0), stop=(mc == 1)) for mc in range(2)],
                        reads=[("Vm",), ("PTD",)], writes=[("ps", b)])
                    copy_op(k, ev(), oT[:, cc, :], ps[b][:, 0:TG], reads=[("ps", b)], writes=[("qTD",)])

            def s5():
                for tt in range(NT):
                    proj_token_major(oT, ("qTD",), wo, ("wo",), 8, tt, hh_[tt], ("hD", hp_, tt), hh_[tt],
                                     ("hD", hp_, tt))
                    rms_to_T(hh_[tt], ("hD", hp_, tt), xo, ("xnT2D", hp_), tt, 4)
            return [s1, s2] + [lambda tt=tt: s3(tt) for tt in range(NT)] + [s4, s5]

        def Y_steps(g):
            tok0 = g * TG
            hp_ = g % 2
            hh_ = h[hp_]
            xi = xnT2[hp_]

            def slab(hs):
                w = c2["w1"] % 3
                c2["w1"] += 1
                k.dma("sp", w1s[w][:], w1d.rearrange("(c p) n -> p c n", p=128)[:, :, hs * 256:(hs + 1) * 256],
                      reads=[("wscr", "ff_w1")], writes=[("w1s", w)])
                for hl in range(2):
                    hc = hs * 2 + hl
                    b = k.psum()
                    k.op("pe", [lambda e, kc=kc, hl=hl, b=b, w=w: e.matmul(
                        ps[b][:, 0:TG], lhsT=w1s[w][:, kc, hl * 128:(hl + 1) * 128], rhs=xi[:, kc, :],
                        start=(kc == 0), stop=(kc == 7)) for kc in range(8)],
                        reads=[("w1s", w), ("xnT2D", hp_)], writes=[("ps", b)])
                    r = c2["rt"] % 2
                    c2["rt"] += 1
                    k.op("act", lambda e, b=b, r=r: e.activation(out=rtmp[r][:], in_=ps[b][:, 0:TG], func=AF.Relu),
                         reads=[("ps", b)], writes=[("rtD", r)])
                    k.op("pool", lambda e, r=r, hc=hc: e.tensor_tensor(out=hid[:, hc, :], in0=rtmp[r][:],
                                                                       in1=rtmp[r][:], op=ALU.mult),
                         reads=[("rtD", r)], writes=[("hidD", hc // 4)])

            def tail(tt):
                hidk = [("hidD", i) for i in range(8)]
                for half in range(2):
                    b = k.psum()
                    k.op("pe", [lambda e, kc=kc, half=half, b=b: e.matmul(
                        ps[b][:, :], lhsT=hid[:, kc, tt * 128:(tt + 1) * 128], rhs=w2[:, kc, half * 512:(half + 1) * 512],
                        start=(kc == 0), stop=(kc == 31)) for kc in range(32)],
                        reads=hidk + [("w2",)], writes=[("ps", b)])
                    k.op("dve", lambda e, b=b, half=half: e.tensor_tensor(
                        out=hh_[tt][:, half * 512:(half + 1) * 512], in0=ps[b][:, :],
                        in1=hh_[tt][:, half * 512:(half + 1) * 512], op=ALU.add),
                        reads=[("ps", b), ("hD", hp_, tt)], writes=[("hD", hp_, tt)])
                k.op("act", lambda e: e.activation(out=xn[0][:], in_=hh_[tt][:], func=AF.Square,
                                                   accum_out=stt[:, 8:9]),
                     reads=[("hD", hp_, tt)], writes=[("xnD", 0), ("statD", 8)])
                k.op("act", lambda e: e.activation(out=stt[:, 9:10], in_=stt[:, 8:9], func=AF.Sqrt, scale=1.0 / D,
                                                   bias=EPS), reads=[("statD", 8)], writes=[("statD", 8)])
                k.op("dve", lambda e: e.reciprocal(out=stt[:, 10:11], in_=stt[:, 9:10]), reads=[("statD", 8)],
                     writes=[("statD", 8)])
                k.op("dve", lambda e: e.scalar_tensor_tensor(out=hh_[tt][:], in0=hh_[tt][:], scalar=stt[:, 10:11],
                                                             in1=gfin[:], op0=ALU.mult, op1=ALU.mult),
                     reads=[("hD", hp_, tt), ("statD", 8), ("gfin",)], writes=[("hD", hp_, tt)])
                t = k.dma("act", out[tok0 + tt * 128:tok0 + (tt + 1) * 128, :], hh_[tt][:], reads=[("hD", hp_, tt)],
                          writes=[("out", g, tt)])
                finals.append(t)
            return [lambda hs=hs: slab(hs) for hs in range(16)] + [lambda tt=tt: tail(tt) for tt in range(NT)]

        XB, YB = (4, 5, 6, 7), (0, 1, 2, 3)

        def run_piece(f, banks):
            k.ps_pool = banks
            f()
            k.ps_pool = None
        for f in X_steps(0):
            run_piece(f, XB)
        for g in range(NG):
            ys = Y_steps(g)
            xs = X_steps(g + 1) if g + 1 < NG else []
            order = []
            xi_ = 0
            for i, yf in enumerate(ys):
                order.append((yf, YB))
                want = ((i + 1) * len(xs)) // len(ys)
                while xi_ < want:
                    order.append((xs[xi_], XB))
                    xi_ += 1
            for f, banks in order:
                run_piece(f, banks)
```

```python
import math
import numpy as np
import ml_dtypes
import concourse.bass as bass
import concourse.mybir as mybir
from concourse.bass_utils import run_bass_kernel_spmd

F32 = mybir.dt.float32
BF = mybir.dt.bfloat16
ALU = mybir.AluOpType
AF = mybir.ActivationFunctionType
AX = mybir.AxisListType

L = 4096
D = 1024
NTT = L // 128
DIN = 3600
NMEM = 256
EPS = 1e-6
DH = 512
ENGS = ("pe", "act", "dve", "pool", "sp")


class K:
    def __init__(self, nc):
        self.nc = nc
        self.streams = {e: [] for e in ENGS}
        self.count = {e: 0 for e in ENGS}
        self.waited = {e: {} for e in ENGS}
        self.last_write = {}
        self.readers = {}
        self.dma_slots = {"sp": 12, "pool": 6, "act": 4}
        self.dma_next = {q: 0 for q in self.dma_slots}
        self.dma_val = {}
        self.sems = {}
        self.ps_next = 0
        self.ps_pool = None
        self.ps_pool_next = {}

    def _deps(self, reads, writes):
        deps = {}

        def add(tok):
            if tok is None:
                return
            k, v = tok
            if deps.get(k, 0) < v:
                deps[k] = v
        for key in reads:
            add(self.last_write.get(key))
        for key in writes:
            add(self.last_write.get(key))
            for k, v in self.readers.get(key, {}).items():
                add((k, v))
        return deps

    def _emit_waits(self, eng, deps):
        w = self.waited[eng]
        for k, v in deps.items():
            if w.get(k, 0) >= v:
                continue
            w[k] = v
            self.streams[eng].append(("wait", k, v))

    def _record(self, tok, reads, writes):
        k, v = tok
        for key in reads:
            r = self.readers.setdefault(key, {})
            if r.get(k, 0) < v:
                r[k] = v
        for key in writes:
            self.last_write[key] = tok
            self.readers[key] = {}

    def op(self, eng, fns, reads=(), writes=(), after=()):
        if not isinstance(fns, (list, tuple)):
            fns = [fns]
        writes = list(writes) + [kk for kk in reads if kk[0] == "ps"]
        reads = [kk for kk in reads if kk[0] != "ps"]
        deps = self._deps(reads, writes)
        for tok in after:
            if tok is not None and deps.get(tok[0], 0) < tok[1]:
                deps[tok[0]] = tok[1]
        self._emit_waits(eng, deps)
        self.count[eng] += 1
        tok = (("c", eng), self.count[eng])
        for f in fns[:-1]:
            self.streams[eng].append(("ins", f, None))
        self.streams[eng].append(("ins", fns[-1], tok))
        self._record(tok, reads, writes)
        return tok

    def dma(self, q, out, in_, reads=(), writes=(), after=(), **kw):
        slot = self.dma_next[q]
        self.dma_next[q] = (slot + 1) % self.dma_slots[q]
        key = ("d", q, slot)
        prev = self.dma_val.get(key, 0)
        deps = self._deps(reads, writes)
        if prev:
            deps[key] = max(deps.get(key, 0), prev)
        for tok in after:
            if tok is not None and deps.get(tok[0], 0) < tok[1]:
                deps[tok[0]] = tok[1]
        self._emit_waits(q, deps)
        val = prev + 16
        self.dma_val[key] = val
        tok = (key, val)
        self.streams[q].append(("dma", (out, in_, kw), tok))
        self._record(tok, reads, writes)
        return tok

    def barrier(self):
        deps = {("c", e): self.count[e] for e in ENGS if self.count[e]}
        for key, v in self.dma_val.items():
            deps[key] = v
        for e in ENGS:
            self._emit_waits(e, dict(deps))

    def psum(self):
        if self.ps_pool is not None:
            banks = self.ps_pool
            i = self.ps_pool_next.get(banks, 0)
            self.ps_pool_next[banks] = (i + 1) % len(banks)
            return banks[i]
        b = self.ps_next
        self.ps_next = (b + 1) % 8
        return b

    def emit(self, final_tokens):
        nc = self.nc
        import contextlib
        with contextlib.ExitStack() as st:
            semkeys = [("c", e) for e in ENGS]
            for q, n in self.dma_slots.items():
                semkeys += [("d", q, i) for i in range(n)]
            for sk in semkeys:
                self.sems[sk] = st.enter_context(nc.semaphore("s_" + "_".join(str(x) for x in sk)))
            block = st.enter_context(nc.Block())
            deps = {}
            for tok in final_tokens:
                if deps.get(tok[0], 0) < tok[1]:
                    deps[tok[0]] = tok[1]
            self._emit_waits("sp", deps)

            def run(engname):
                def body(eng):
                    for item in self.streams[engname]:
                        if item[0] == "wait":
                            eng.wait_ge(self.sems[item[1]], item[2])
                        elif item[0] == "ins":
                            ins = item[1](eng)
                            if item[2] is not None:
                                ins.then_inc(self.sems[item[2][0]], 1)
                        else:
                            out, in_, kw = item[1]
                            eng.dma_start(out=out, in_=in_, **kw).then_inc(self.sems[item[2][0]], 16)
                return body
            block.tensor(run("pe"))
            block.scalar(run("act"))
            block.vector(run("dve"))
            block.gpsimd(run("pool"))
            block.sync(run("sp"))


def _bf(a):
    return np.ascontiguousarray(a.astype(np.float32)).astype(ml_dtypes.bfloat16)


_CONST_CACHE = {}


def host_consts():
    if _CONST_CACHE:
        return _CONST_CACHE
    c = {}
    N = 2 * L
    c["ident_bf"] = _bf(np.eye(128))
    c["ident_f"] = np.eye(128, dtype=np.float32)
    t1 = np.arange(32)[:, None]
    f1 = np.arange(64)[None, :]
    th = 2 * np.pi * t1 * (f1 + 0.5) / 64.0
    F1 = np.concatenate([np.cos(th), -np.sin(th)], axis=1)
    F1pad = np.zeros((32, 4, 4, 128))
    for q in range(4):
        F1pad[:, q, q, :] = F1
    c["F1pad"] = _bf(F1pad.reshape(128, 4 * 128))
    th2 = 2 * np.pi * (t1 + 32) * (f1 + 0.5) / 64.0
    F1b = np.concatenate([np.cos(th2), -np.sin(th2)], axis=1)
    F1pad2 = np.zeros((32, 4, 4, 128))
    for q in range(4):
        F1pad2[:, q, q, :] = F1b
    c["F1pad2"] = _bf(F1pad2.reshape(128, 4 * 128))
    t2 = np.arange(128)[:, None, None]
    f1 = np.arange(64)[None, :, None]
    f2 = np.arange(64)[None, None, :]
    ph = 2 * np.pi * t2 * (f1 + 64 * f2 + 0.5) / N
    cc, ss = np.cos(ph), np.sin(ph)
    W4 = np.zeros((128, 64, 2, 128))
    W4[:, :, 0, :64] = cc
    W4[:, :, 0, 64:] = -ss
    W4[:, :, 1, :64] = ss
    W4[:, :, 1, 64:] = cc
    c["W4"] = _bf(W4.reshape(128, 64 * 2 * 128))
    cT = np.transpose(cc, (2, 1, 0))
    sT = np.transpose(ss, (2, 1, 0))
    WI = np.zeros((128, 64, 2, 128))
    WI[:64, :, 0, :] = cT
    WI[64:, :, 0, :] = -sT
    WI[:64, :, 1, :] = sT
    WI[64:, :, 1, :] = cT
    c["WI"] = _bf(WI.reshape(128, 64 * 2 * 128))
    f1 = np.arange(64)[:, None]
    t1 = np.arange(32)[None, :]
    th = 2 * np.pi * t1 * (f1 + 0.5) / 64.0
    G = np.concatenate([np.cos(th), -np.sin(th)], axis=0) * (2.0 / N)
    c["G"] = _bf(G)
    k = np.arange(128)[:, None]
    m = np.arange(128)[None, :]
    c["Pswap"] = _bf((k == (m + 64) % 128) * 1.0)
    D1 = ((k == (m % 64)) * 1.0)
    sg = np.where(m < 64, -1.0, 1.0)
    D2 = ((k == 64 + (m % 64)) * sg)
    c["Dmat"] = _bf(np.concatenate([D1, D2, -D2], axis=1))
    c["blk64"] = _bf(((k // 64) == (m // 64)) * 1.0)
    c["ones_bf"] = _bf(np.ones((128, 128)))
    c["mask_f"] = (k <= m).astype(np.float32)
    c["mask_b"] = (k >= m).astype(np.float32)
    sel = np.zeros((16, 16, 128), np.float32)
    for p in range(16):
        sel[p, p, :] = 1.0
    c["sel"] = sel.reshape(16, 16 * 128)
    f32 = np.float32
    t = np.linspace(0.0, 1.0, L, dtype=f32)[:, None]
    ang = (f32(2.0 * math.pi / L) * np.arange(L, dtype=f32))[:, None]
    bands = np.linspace(1e-4, 16 - 1, 16, dtype=f32)[None, :]
    z = np.concatenate([t, np.cos(bands * ang), -np.sin(bands * ang)], axis=-1).astype(f32)
    c["zT"] = np.ascontiguousarray(z.T)
    c["tlin"] = np.ascontiguousarray(t.T)
    max_decay = math.log(1e-2) / 0.3
    min_decay = math.log(1e-2) / 1.5
    deltas = np.linspace(min_decay, max_decay, DH, dtype=f32)
    c["negabsdelta"] = np.ascontiguousarray((-np.abs(deltas)).reshape(4, 128).T.astype(f32))
    dm = np.zeros((8, 2), np.float32)
    dm[0:4, 0] = 1.0
    dm[4:8, 1] = 1.0
    c["dirmask"] = dm
    _CONST_CACHE.update(c)
    return c


CONST_DT = {"ident_bf": BF, "ident_f": F32, "F1pad": BF, "F1pad2": BF, "W4": BF, "WI": BF, "G": BF, "Pswap": BF,
            "Dmat": BF, "blk64": BF, "ones_bf": BF, "mask_f": F32, "mask_b": F32, "sel": F32,
            "zT": F32, "tlin": F32, "negabsdelta": F32, "dirmask": F32}


IN_SPECS = [
    ("x", [L, D], F32), ("mem", [NMEM, D], F32), ("w_in", [D, DIN], F32),
    ("gmix_col", [128, 8], F32),
    ("hy_cw", [128, 12 * 3], F32), ("hy_cb", [128, 12], F32),
    ("f_w1", [33, 64], F32), ("f_b1", [64, 1], F32), ("f_fr1", [64, 1], F32),
    ("f_w2", [64, 64], F32), ("f_b2", [64, 1], F32), ("f_fr2", [64, 1], F32),
    ("f_w3", [64, 2048], F32),
    ("hy_skip", [128, 8], F32), ("hy_ng", [128, 4], F32),
    ("ml_cw", [128, 8 * 3], F32), ("ml_cb", [128, 8], F32), ("ml_gb", [16, 1], F32),
    ("ml_ng", [128, 4], F32),
    ("w_out", [D, D], F32), ("gx_col", [128, 8], F32), ("gmem_col", [128, 8], F32),
    ("xa_wq", [D, D], F32), ("xa_wk", [D, D], F32), ("xa_wv", [D, D], F32), ("xa_wo", [D, D], F32),
    ("gff_col", [128, 8], F32), ("ff_w1", [D, 4 * D], F32), ("ff_w2", [4 * D, D], F32),
    ("gfin_row", [1, D], F32),
]


DEBUG = False
epsc = None


def build(debug=False, phases="ABCD"):
    global DEBUG
    DEBUG = debug
    import contextlib
    nc = bass.Bass("TRN2", target_bir_lowering=False)
    k = K(nc)
    dr = {}
    for name, shape, dt in IN_SPECS:
        dr[name] = nc.dram_tensor(name, shape, dt, kind="ExternalInput").ap()
    hc = host_consts()
    for name, arr in hc.items():
        dr[name] = nc.dram_tensor("c_" + name, list(arr.shape), CONST_DT[name], kind="ExternalInput").ap()
    out = nc.dram_tensor("out", [L, D], F32, kind="ExternalOutput").ap()
    dbgkind = "ExternalOutput" if debug else "Internal"
    projT = nc.dram_tensor("projT", [3584, L], BF, kind=dbgkind).ap()
    gatesT = nc.dram_tensor("gatesT", [16, L], F32, kind=dbgkind).ap()
    mixT = nc.dram_tensor("mixT", [D, L], BF, kind=dbgkind).ap()
    wscr = {n: nc.dram_tensor("bf_" + n, shp, BF).ap() for n, shp in
            (("w_out", [D, D]), ("xa_wq", [D, D]), ("xa_wk", [D, D]), ("xa_wv", [D, D]), ("xa_wo", [D, D]),
             ("ff_w2", [4 * D, D]), ("ff_w1", [D, 4 * D]))}
    dr["wscr"] = wscr

    finals = []
    with contextlib.ExitStack() as top:
        ps = [top.enter_context(nc.psum_tensor(f"ps{b}", [128, 512], F32)) for b in range(8)]
        psbf = [p[:].bitcast(BF) for p in ps]

        def sbuf(st, name, shape, dt):
            return st.enter_context(nc.sbuf_tensor("s_" + name, shape, dt))

        ident_bf = sbuf(top, "ident_bf", [128, 128], BF)
        ident_f = sbuf(top, "ident_f", [128, 128], F32)
        global epsc
        epsc = sbuf(top, "epsc", [128, 1], F32)
        k.op("pool", lambda e: e.memset(epsc[:], EPS), writes=[("epsc",)])
        k.dma("sp", ident_bf[:], dr["ident_bf"], writes=[("ident_bf",)])
        k.dma("sp", ident_f[:], dr["ident_f"], writes=[("ident_f",)])

        if "A" in phases:
            phase_A(nc, k, dr, ps, psbf, ident_bf, projT, gatesT, finals)
        k.barrier()
        if "B" in phases:
            phase_B(nc, k, dr, ps, psbf, ident_bf, projT, mixT, finals)
        if "C" in phases:
            phase_C(nc, k, dr, ps, psbf, ident_bf, ident_f, projT, gatesT, mixT, finals)
        if "D" in phases:
            phase_D(nc, k, dr, ps, psbf, ident_bf, mixT, out, finals)
        k.emit(finals)
    return nc


def evac_engine(i):
    return "act" if i % 2 == 0 else "dve"


def copy_op(k, eng, out, in_, reads, writes):
    if eng == "act":
        return k.op("act", lambda e: e.copy(out=out, in_=in_), reads=reads, writes=writes)
    return k.op(eng, lambda e: e.tensor_copy(out=out, in_=in_), reads=reads, writes=writes)


def phase_A(nc, k, dr, ps, psbf, ident_bf, projT, gatesT, finals):
    import contextlib
    with contextlib.ExitStack() as st:
        def sb(name, shape, dt):
            return st.enter_context(nc.sbuf_tensor("s_" + name, shape, dt))
        uT = sb("uT", [128, 8, L], BF)
        wbf = sb("wbf", [128, 8, DIN], BF)
        wst = [sb(f"wst{i}", [128, DIN], F32) for i in range(2)]
        gcol = sb("gcol", [128, 8], F32)
        xt = [sb(f"xt{i}", [128, D], F32) for i in range(2)]
        xn = [sb(f"xn{i}", [128, D], BF) for i in range(2)]
        sq = sb("sq", [128, D], BF)
        stat = sb("stat", [128, 3 * NTT], F32)
        stg = [sb(f"stg{i}", [128, L], BF) for i in range(2)]
        gstg = [sb(f"gstg{i}", [16, 512], F32) for i in range(2)]

        gD = sb("gD", [128, 3, 8], F32)
        cst = [sb(f"cst{i}", [128, D], F32) for i in range(3)]
        cbo = [sb(f"cbo{i}", [128, D], BF) for i in range(3)]
        k.dma("sp", gD[:, 0, :], dr["gx_col"], writes=[("gD",)])
        k.dma("sp", gD[:, 1, :], dr["gmem_col"], writes=[("gD",)])
        k.dma("sp", gD[:, 2, :], dr["gff_col"], writes=[("gD",)])
        jobs = []
        for c in range(8):
            rows = slice(c * 128, (c + 1) * 128)
            jobs.append(("w_out", rows, slice(0, D), None))
            jobs.append(("xa_wq", rows, slice(0, D), (0, c)))
            jobs.append(("xa_wk", rows, slice(0, D), (1, c)))
            jobs.append(("xa_wv", rows, slice(0, D), (1, c)))
            jobs.append(("xa_wo", rows, slice(0, D), None))
        for c in range(32):
            jobs.append(("ff_w2", slice(c * 128, (c + 1) * 128), slice(0, D), None))
        for c in range(8):
            for qd in range(4):
                jobs.append(("ff_w1", slice(c * 128, (c + 1) * 128), slice(qd * D, (qd + 1) * D), (2, c)))

        def wload(i):
            nme, rows, cols, g = jobs[i]
            k.dma("pool", cst[i % 3][:], dr[nme][rows, cols], writes=[("cst", i % 3)])

        def wconv(i):
            nme, rows, cols, g = jobs[i]
            s = i % 3
            if g is None:
                k.op("pool", lambda e: e.tensor_copy(out=cbo[s][:], in_=cst[s][:]), reads=[("cst", s)],
                     writes=[("cbo", s)])
            else:
                gi, c = g
                k.op("pool", lambda e: e.tensor_scalar(out=cbo[s][:], in0=cst[s][:], scalar1=gD[:, gi, c:c + 1],
                                                       scalar2=1.0, op0=ALU.mult, op1=ALU.mult),
                     reads=[("cst", s), ("gD",)], writes=[("cbo", s)])
            finals.append(k.dma("pool", dr["wscr"][nme][rows, cols], cbo[s][:], reads=[("cbo", s)],
                                writes=[("wscr", nme)]))
        wload(0)
        wload(1)
        for i in range(len(jobs)):
            if i + 2 < len(jobs):
                wload(i + 2)
            wconv(i)

        k.dma("sp", gcol[:], dr["gmix_col"], writes=[("gcol",)])
        for c in range(8):
            k.dma("sp", wst[c % 2][:], dr["w_in"][c * 128:(c + 1) * 128, :], writes=[("wst", c % 2)])
            k.op("dve", lambda e, c=c: e.tensor_scalar(out=wbf[:, c, :], in0=wst[c % 2][:],
                                                        scalar1=gcol[:, c:c + 1], scalar2=None, op0=ALU.mult),
                 reads=[("wst", c % 2), ("gcol",)], writes=[("wbf", c)])
        for i in range(NTT):
            j = i % 2
            k.dma("act", xt[j][:], dr["x"][i * 128:(i + 1) * 128, :], writes=[("xt", j)])
            k.op("act", lambda e, i=i, j=j: e.activation(out=sq[:], in_=xt[j][:], func=AF.Square,
                                                         accum_out=stat[:, i:i + 1]),
                 reads=[("xt", j)], writes=[("sq",), ("stat", i)])
            k.op("act", lambda e, i=i: e.activation(out=stat[:, NTT + i:NTT + i + 1], in_=stat[:, i:i + 1],
                                                    func=AF.Sqrt, scale=1.0 / D, bias=EPS),
                 reads=[("stat", i)], writes=[("stat", i)])
            k.op("dve", lambda e, i=i: e.reciprocal(out=stat[:, 2 * NTT + i:2 * NTT + i + 1],
                                                    in_=stat[:, NTT + i:NTT + i + 1]),
                 reads=[("stat", i)], writes=[("stat", i)])
            k.op("act", lambda e, i=i, j=j: e.activation(out=xn[j][:], in_=xt[j][:], func=AF.Copy,
                                                         scale=stat[:, 2 * NTT + i:2 * NTT + i + 1]),
                 reads=[("xt", j), ("stat", i)], writes=[("xn", j)])
            b = k.psum()
            pv = psbf[b].rearrange("p (c t) -> p c t", t=128)
            k.op("pe", [lambda e, c=c, j=j, pv=pv: e.transpose(out=pv[:, c, :], in_=xn[j][:, c * 128:(c + 1) * 128],
                                                               identity=ident_bf[:]) for c in range(8)],
                 reads=[("xn", j), ("ident_bf",)], writes=[("ps", b)])
            copy_op(k, evac_engine(i), uT[:, :, i * 128:(i + 1) * 128], pv, reads=[("ps", b)], writes=[("uT", i // 4)])
        ev = 0
        for cc in range(29):
            M = 128 if cc < 28 else 16
            s = cc % 2
            for tg in range(8):
                b = k.psum()
                k.op("pe", [lambda e, kc=kc, cc=cc, tg=tg, b=b, M=M: e.matmul(
                    ps[b][0:M, :], lhsT=wbf[:, kc, cc * 128:cc * 128 + M], rhs=uT[:, kc, tg * 512:(tg + 1) * 512],
                    start=(kc == 0), stop=(kc == 7)) for kc in range(8)],
                    reads=[("wbf", kc) for kc in range(8)] + [("uT", tg)], writes=[("ps", b)])
                if cc < 28:
                    copy_op(k, evac_engine(ev), stg[s][:, tg * 512:(tg + 1) * 512], ps[b][:], reads=[("ps", b)],
                            writes=[("stg", s)])
                else:
                    copy_op(k, evac_engine(ev), gstg[tg % 2][:], ps[b][0:16, :], reads=[("ps", b)],
                            writes=[("gstg", tg % 2)])
                    finals.append(k.dma("sp", gatesT[:, tg * 512:(tg + 1) * 512], gstg[tg % 2][:],
                                        reads=[("gstg", tg % 2)], writes=[("gatesT",)]))
                ev += 1
            if cc < 28:
                t = k.dma("sp", projT[cc * 128:(cc + 1) * 128, :], stg[s][:], reads=[("stg", s)],
                          writes=[("projT", cc)])
                finals.append(t)


def _cols(v, n):
    return np.ascontiguousarray(np.asarray(v, np.float32).reshape(n, 128).T)


def shared_inputs(inp):
    f = lambda a: np.ascontiguousarray(np.asarray(a, np.float32))
    m = {}
    m["w_in"] = f(inp["w_in"][0])
    m["gmix_col"] = _cols(inp["norm_mix_g"][0], 8)
    cw = f(inp["hy_conv_w"][0])
    m["hy_cw"] = np.ascontiguousarray(cw.reshape(3, 12, 128).transpose(2, 1, 0).reshape(128, 36))
    m["hy_cb"] = _cols(inp["hy_conv_b"][0], 12)
    m["f_w1"] = f(inp["hy_filt_w1"][0])
    m["f_b1"] = f(inp["hy_filt_b1"][0]).reshape(64, 1)
    m["f_fr1"] = f(inp["hy_filt_freq1"][0]).reshape(64, 1)
    m["f_w2"] = f(inp["hy_filt_w2"][0])
    m["f_b2"] = f(inp["hy_filt_b2"][0]).reshape(64, 1)
    m["f_fr2"] = f(inp["hy_filt_freq2"][0]).reshape(64, 1)
    m["f_w3"] = f(inp["hy_filt_w3"][0])
    m["hy_skip"] = _cols(f(inp["hy_skip"][0]).reshape(-1), 8)
    m["hy_ng"] = _cols(inp["hy_norm_g"][0], 4)
    mw = f(inp["ml_conv_w"][0])
    m["ml_cw"] = np.ascontiguousarray(mw.reshape(3, 8, 128).transpose(2, 1, 0).reshape(128, 24))
    m["ml_cb"] = _cols(inp["ml_conv_b"][0], 8)
    m["ml_gb"] = f(inp["ml_gate_b"][0]).reshape(16, 1)
    m["ml_ng"] = _cols(inp["ml_norm_g"][0], 4)
    m["w_out"] = f(inp["w_out"][0])
    m["gx_col"] = _cols(inp["norm_x_g"][0], 8)
    m["gmem_col"] = _cols(inp["norm_mem_g"][0], 8)
    for n in ("xa_wq", "xa_wk", "xa_wv", "xa_wo", "ff_w1", "ff_w2"):
        m[n] = f(inp[n][0])
    m["gff_col"] = _cols(inp["norm_ff_g"][0], 8)
    m["gfin_row"] = f(inp["final_norm_g"]).reshape(1, D)
    for name, arr in host_consts().items():
        m["c_" + name] = arr
    return m


def kernel(**inputs):
    nb = 8
    sh = shared_inputs(inputs)
    x = np.asarray(inputs["x"], np.float32)
    mem = np.asarray(inputs["mem"], np.float32)
    in_maps = []
    for b in range(nb):
        m = dict(sh)
        m["x"] = np.ascontiguousarray(x[b])
        m["mem"] = np.ascontiguousarray(mem[b])
        in_maps.append(m)
    nc = build()
    res = run_bass_kernel_spmd(nc, in_maps, core_ids=list(range(nb)))
    return np.stack([np.asarray(r["out"], np.float32) for r in res.results], axis=0)


def phase_B(nc, k, dr, ps, psbf, ident_bf, projT, mixT, finals):
    import contextlib
    PI = math.pi
    with contextlib.ExitStack() as st:
        def sb(name, shape, dt):
            return st.enter_context(nc.sbuf_tensor("s_" + name, shape, dt))
        W4 = sb("W4", [128, 64, 2, 128], BF)
        WI = sb("WI", [128, 64, 2, 128], BF)
        F1pad = sb("F1pad", [128, 4, 128], BF)
        Gm = sb("Gm", [128, 32], BF)
        Pswap = sb("Pswap", [128, 128], BF)
        Dmat = sb("Dmat", [128, 3, 128], BF)
        blk64 = sb("blk64", [128, 128], BF)
        Z1 = sb("Z1", [128, 32, 64], BF)
        Yfm = sb("Yfm", [128, 128, 64], BF)
        Yt = sb("Yt", [128, 64, 128], BF)
        Xs = [sb(f"Xs{i}", [128, 64, 64], BF) for i in range(2)]
        sgn = sb("sgn", [128, 1], F32)
        hy_cw = sb("hy_cw", [128, 36], F32)
        hy_cb = sb("hy_cb", [128, 12], F32)
        hy_skip = sb("hy_skip", [128, 8], F32)
        hy_ng = sb("hy_ng", [128, 4], F32)
        stf = contextlib.ExitStack()

        def sbf(name, shape, dt):
            return stf.enter_context(nc.sbuf_tensor("s_" + name, shape, dt))
        tl512 = sbf("tl512", [128, 512], F32)
        nad = sbf("nad", [128, 4], F32)
        wbias = sbf("wbias", [128, 32], F32)
        w3bf = sbf("w3bf", [64, 2048], BF)
        hid2 = sbf("hid2", [64, L], BF)
        F1pad2 = sbf("F1pad2", [128, 4, 128], BF)
        Z1L = sbf("Z1L", [128, 2, 32, 64], BF)
        k.dma("sp", F1pad2[:], dr["F1pad2"].rearrange("p (a b) -> p a b", a=4), writes=[("F1pad2",)])
        for name, t, src in (("W4", W4, dr["W4"].rearrange("p (a b c) -> p a b c", a=64, b=2)),
                             ("WI", WI, dr["WI"].rearrange("p (a b c) -> p a b c", a=64, b=2)),
                             ("F1pad", F1pad, dr["F1pad"].rearrange("p (a b) -> p a b", a=4)),
                             ("Dmat", Dmat, dr["Dmat"].rearrange("p (a b) -> p a b", a=3)),
                             ("Gm", Gm, dr["G"]), ("Pswap", Pswap, dr["Pswap"]), ("blk64", blk64, dr["blk64"]),
                             ("nad", nad, dr["negabsdelta"]), ("hy_cw", hy_cw, dr["hy_cw"]),
                             ("hy_cb", hy_cb, dr["hy_cb"]), ("hy_skip", hy_skip, dr["hy_skip"]),
                             ("hy_ng", hy_ng, dr["hy_ng"])):
            k.dma("sp", t[:], src, writes=[(name,)])
        k.dma("sp", tl512[:], dr["tlin"][0:1, 0:512].partition_broadcast(128), writes=[("tl512",)])
        k.op("pool", lambda e: e.memset(sgn[0:64, :], 1.0), writes=[("sgn",)])
        k.op("pool", lambda e: e.memset(sgn[64:128, :], -1.0), writes=[("sgn",)])
        for j in range(4):
            for tg in range(8):
                k.op("pool", lambda e, j=j, tg=tg: e.tensor_scalar(
                    out=wbias[:, j * 8 + tg:j * 8 + tg + 1], in0=nad[:, j:j + 1], scalar1=float(512 * tg) / (L - 1),
                    scalar2=1.0, op0=ALU.mult, op1=ALU.mult), reads=[("nad",)], writes=[("wbias",)])

        with contextlib.ExitStack() as st2:
            def sb2(name, shape, dt):
                return st2.enter_context(nc.sbuf_tensor("s_" + name, shape, dt))
            zT = sb2("zT", [33, L], F32)
            fw1 = sb2("fw1", [33, 64], F32)
            fw2 = sb2("fw2", [64, 64], F32)
            fw3 = sb2("fw3", [64, 2048], F32)
            fcol = sb2("fcol", [64, 4], F32)
            hid1 = sb2("hid1", [64, L], F32)
            arg = sb2("arg", [64, L], F32)
            cnt1 = [sb2(f"cnt1_{i}", [64, 512], F32) for i in range(2)]
            cnt2 = [sb2(f"cnt2_{i}", [64, 512], F32) for i in range(2)]
            k.dma("sp", zT[:], dr["zT"], writes=[("zT",)])
            k.dma("sp", fw1[:], dr["f_w1"], writes=[("fw1",)])
            k.dma("sp", fw2[:], dr["f_w2"], writes=[("fw2",)])
            k.dma("sp", fw3[:], dr["f_w3"], writes=[("fw3",)])
            for i, nme in enumerate(("f_b1", "f_fr1", "f_b2", "f_fr2")):
                k.dma("sp", fcol[:, i:i + 1], dr[nme], writes=[("fcol",)])
            k.op("pool", lambda e: e.tensor_copy(out=w3bf[:], in_=fw3[:]), reads=[("fw3",)], writes=[("w3bf",)])

            def ffn_layer(lhsT, lkey, kdim, rhs_t, rkey, bcol, fcolx, out_t, okey):
                for tg in range(8):
                    b = k.psum()
                    q = tg % 2
                    sl = slice(tg * 512, (tg + 1) * 512)
                    k.op("pe", lambda e, b=b, sl=sl: e.matmul(ps[b][0:64, :], lhsT=lhsT[0:kdim, :],
                                                             rhs=rhs_t[0:kdim, sl], start=True, stop=True),
                         reads=[(lkey,), (rkey,)], writes=[("ps", b)])
                    k.op("dve", lambda e, b=b, sl=sl: e.tensor_scalar(
                        out=arg[:, sl], in0=ps[b][0:64, :], scalar1=fcol[:, bcol:bcol + 1],
                        scalar2=fcol[:, fcolx:fcolx + 1], op0=ALU.add, op1=ALU.mult),
                        reads=[("ps", b), ("fcol",)], writes=[("arg", tg)])
                    k.op("dve", lambda e, sl=sl, q=q: e.tensor_single_scalar(out=cnt1[q][:], in_=arg[:, sl],
                                                                             scalar=PI, op=ALU.is_gt),
                         reads=[("arg", tg)], writes=[("cnt1", q)])
                    k.op("dve", lambda e, sl=sl, q=q: e.scalar_tensor_tensor(
                        out=cnt1[q][:], in0=arg[:, sl], scalar=3 * PI, in1=cnt1[q][:], op0=ALU.is_gt, op1=ALU.add),
                        reads=[("arg", tg), ("cnt1", q)], writes=[("cnt1", q)])
                    k.op("dve", lambda e, sl=sl, q=q: e.tensor_single_scalar(out=cnt2[q][:], in_=arg[:, sl],
                                                                             scalar=-PI, op=ALU.is_lt),
                         reads=[("arg", tg)], writes=[("cnt2", q)])
                    k.op("dve", lambda e, sl=sl, q=q: e.scalar_tensor_tensor(
                        out=cnt2[q][:], in0=arg[:, sl], scalar=-3 * PI, in1=cnt2[q][:], op0=ALU.is_lt, op1=ALU.add),
                        reads=[("arg", tg), ("cnt2", q)], writes=[("cnt2", q)])
                    k.op("dve", lambda e, sl=sl, q=q: e.scalar_tensor_tensor(
                        out=arg[:, sl], in0=cnt1[q][:], scalar=-2 * PI, in1=arg[:, sl], op0=ALU.mult, op1=ALU.add),
                        reads=[("arg", tg), ("cnt1", q)], writes=[("arg", tg)])
                    k.op("dve", lambda e, sl=sl, q=q: e.scalar_tensor_tensor(
                        out=arg[:, sl], in0=cnt2[q][:], scalar=2 * PI, in1=arg[:, sl], op0=ALU.mult, op1=ALU.add),
                        reads=[("arg", tg), ("cnt2", q)], writes=[("arg", tg)])
                k.op("act", lambda e: e.activation(out=out_t[:], in_=arg[:], func=AF.Sin),
                     reads=[("arg", tg) for tg in range(8)], writes=[(okey,)])

            ffn_layer(fw1, "fw1", 33, zT, "zT", 0, 1, hid1, "hid1")
            ffn_layer(fw2, "fw2", 64, hid1, "hid1", 2, 3, hid2, "hid2")
        k.barrier()

        cnt = {"ev": 0, "raw": 0, "pt": 0, "gt": 0, "hn": 0}

        def ev():
            cnt["ev"] += 1
            return "dve" if cnt["ev"] % 3 == 0 else "act"

        def fwd_half_gen(src, skeys, h, xs, xkey, long=False, bufs=None):
            Z1_, Z1L_, Yfm_, Yt_, tg_ = bufs if bufs is not None else (Z1, Z1L, Yfm, Yt, 0)
            sv = src[:].rearrange("p (n j) -> p j n", j=32)
            for u in range(2 if long else 1):
                for jb in range(2):
                    b = k.psum()
                    pv = psbf[b].rearrange("p (j c) -> p j c", c=64)
                    k.op("pe", [lambda e, jj=jj, pv=pv, jb=jb, u=u: e.transpose(
                        out=pv[:, jj, :], in_=sv[64 * h:64 * h + 64, jb * 16 + jj, u * 128:(u + 1) * 128],
                        identity=ident_bf[64 * h:64 * h + 64, 64 * h:64 * h + 64]) for jj in range(16)],
                        reads=list(skeys) + [("ident_bf",)], writes=[("ps", b)])
                    if long:
                        copy_op(k, ev(), Z1L_[:, u, jb * 16:(jb + 1) * 16, :], pv, reads=[("ps", b)],
                                writes=[("Z1L", tg_, u, jb)])
                    else:
                        copy_op(k, ev(), Z1_[:, jb * 16:(jb + 1) * 16, :], pv, reads=[("ps", b)], writes=[("Z1", tg_, jb)])
            yield
            for q in range(4):
                for jg in range(4):
                    b = k.psum()
                    if long:
                        k.op("pe", [lambda e, b=b, q=q, jg=jg: e.matmul(
                            ps[b][:, :], lhsT=F1pad[:, q, :], rhs=Z1L_[:, 0, jg * 8:(jg + 1) * 8, :], start=True,
                            stop=False),
                            lambda e, b=b, q=q, jg=jg: e.matmul(
                            ps[b][:, :], lhsT=F1pad2[:, q, :], rhs=Z1L_[:, 1, jg * 8:(jg + 1) * 8, :], start=False,
                            stop=True)],
                            reads=[("F1pad",), ("F1pad2",), ("Z1L", tg_, 0, jg // 2), ("Z1L", tg_, 1, jg // 2)],
                            writes=[("ps", b)])
                    else:
                        k.op("pe", lambda e, b=b, q=q, jg=jg: e.matmul(
                            ps[b][:, :], lhsT=F1pad[:, q, :], rhs=Z1_[:, jg * 8:(jg + 1) * 8, :], start=True,
                            stop=True),
                            reads=[("F1pad",), ("Z1", tg_, jg // 2)], writes=[("ps", b)])
                    t20 = 32 * q + 8 * jg
                    copy_op(k, ev(), Yfm_[:, t20:t20 + 8, :], ps[b][:, :].rearrange("p (j c) -> p j c", c=64),
                            reads=[("ps", b)], writes=[("Yfm", tg_, t20 // 8)])
            yield
            for cb in range(8):
                b = k.psum()
                pv = psbf[b].rearrange("p (c f) -> p c f", f=128)
                k.op("pe", [lambda e, cc=cc, pv=pv, cb=cb: e.transpose(
                    out=pv[:, cc, :], in_=Yfm_[:, :, cb * 8 + cc], identity=ident_bf[:]) for cc in range(8)],
                    reads=[("Yfm", tg_, i) for i in range(16)] + [("ident_bf",)], writes=[("ps", b)])
                copy_op(k, ev(), Yt_[:, cb * 8:(cb + 1) * 8, :], pv, reads=[("ps", b)], writes=[("Yt", tg_, cb)])
            yield
            for fb in range(8):
                b = k.psum()
                fns = []
                for fl in range(8):
                    f1 = fb * 8 + fl
                    fns.append(lambda e, b=b, f1=f1, fl=fl: e.matmul(
                        ps[b][:, fl * 64:(fl + 1) * 64], lhsT=W4[:, f1, 0, :], rhs=Yt_[:, :, f1], start=True,
                        stop=False))
                    fns.append(lambda e, b=b, f1=f1, fl=fl: e.matmul(
                        ps[b][:, fl * 64:(fl + 1) * 64], lhsT=W4[:, f1, 1, :], rhs=Yt_[:, :, 64 + f1], start=False,
                        stop=True))
                k.op("pe", fns, reads=[("W4",)] + [("Yt", tg_, i) for i in range(8)], writes=[("ps", b)])
                copy_op(k, ev(), xs[:, fb * 8:(fb + 1) * 8, :], ps[b][:, :].rearrange("p (f c) -> p f c", c=64),
                        reads=[("ps", b)], writes=[(xkey, fb)])


        def fwd_half(*a_, **kw_):
            for _ in fwd_half_gen(*a_, **kw_):
                pass

        hspec = nc.dram_tensor("hspec", [16, 128, 64 * 64], BF, kind=("ExternalOutput" if DEBUG else "Internal")).ap()
        dbgB = nc.dram_tensor("dbgB", [4, 128, L], BF, kind="ExternalOutput").ap() if DEBUG else None

        with contextlib.ExitStack() as st3:
            def sb3(name, shape, dt):
                return st3.enter_context(nc.sbuf_tensor("s_" + name, shape, dt))
            zsrc = [sb3("zsrc0", [128, 2 * L], BF)] * 2
            bufs1 = (None, sb3("Z1L2", [128, 2, 32, 64], BF), sb3("Yfm2", [128, 128, 64], BF),
                     sb3("Yt2", [128, 64, 128], BF), 1)
            winp = [sb3(f"winp{i}", [128, 512], F32) for i in range(2)]
            k.op("pool", lambda e: e.memset(zsrc[0][:, L:L + 1], 0.0), writes=[("zsrc", 0)])

            def make_filter(j, o, d, zs, zkey):
                col0 = o * 1024 + d * 512 + j * 128
                for tg in range(8):
                    b = k.psum()
                    k.op("pe", lambda e, b=b, tg=tg: e.matmul(ps[b][:, :], lhsT=w3bf[:, col0:col0 + 128],
                                                             rhs=hid2[:, tg * 512:(tg + 1) * 512], start=True, stop=True),
                         reads=[("w3bf",), ("hid2",)], writes=[("ps", b)])
                    w = tg % 2
                    k.op("act", lambda e, w=w, tg=tg: e.activation(out=winp[w][:], in_=tl512[:], func=AF.Exp,
                                                                   scale=nad[:, j:j + 1],
                                                                   bias=wbias[:, j * 8 + tg:j * 8 + tg + 1]),
                         reads=[("tl512",), ("nad",), ("wbias",)], writes=[("winp", w)])
                    if d == 0:
                        k.op("dve", lambda e, b=b, w=w, tg=tg: e.tensor_tensor(
                            out=zs[:, tg * 512:(tg + 1) * 512], in0=ps[b][:, :], in1=winp[w][:], op=ALU.mult),
                            reads=[("ps", b), ("winp", w)], writes=[zkey])
                    else:
                        lo = 1 if tg == 0 else 0
                        start = 2 * L - 512 * tg - lo
                        stop = 2 * L - 512 * tg - 512
                        k.op("dve", lambda e, b=b, w=w, lo=lo, start=start, stop=stop: e.scalar_tensor_tensor(
                            out=zs[:, start:stop:-1], in0=ps[b][:, lo:512], scalar=-1.0, in1=winp[w][:, lo:512],
                            op0=ALU.mult, op1=ALU.mult),
                            reads=[("ps", b), ("winp", w)], writes=[zkey])

            for j in range(4):
                for o in range(2):
                    zi = 0
                    make_filter(j, o, 0, zsrc[zi], ("zsrc", zi))
                    make_filter(j, o, 1, zsrc[zi], ("zsrc", zi))
                    import itertools
                    g0 = fwd_half_gen(zsrc[zi], [("zsrc", zi)], 0, Xs[0], "Xs0", long=True)
                    g1 = fwd_half_gen(zsrc[zi], [("zsrc", zi)], 1, Xs[1], "Xs1", long=True, bufs=bufs1)
                    for _ in itertools.zip_longest(g0, g1):
                        pass
                    for h in range(2):
                        idx = (j * 2 + o) * 2 + h
                        k.dma("sp", hspec[idx], Xs[h][:].rearrange("p a b -> p (a b)"),
                              reads=[("Xs%d" % h, i) for i in range(8)], writes=[("hspec", idx)])
        k.barrier()
        stf.close()

        raw = [sb("raw0", [128, L + 2], BF)]
        zc = sb("zc", [128, L], BF)
        xg = sb("xg", [128, L], BF)
        ctmp = [sb(f"ctmp{i}", [128, 512], F32) for i in range(2)]
        Vfm = sb("Vfm", [128, 128, 128], BF)
        Hn = [sb("Hn0", [128, 64, 64], BF)]
        hct = [sb(f"hct{i}", [128, 8, 64], BF) for i in range(4)]
        Xp = Xs[1]
        ptmp = [sb(f"ptmp{i}", [128, 512], F32) for i in range(4)]
        gtmp = [sb(f"gtmp{i}", [128, 512], F32) for i in range(2)]
        sqb = [sb(f"sqb{i}", [128, 512], BF) for i in range(2)]
        nstg = [sb(f"nstg{i}", [128, 512], BF) for i in range(2)]
        for i, r in enumerate(raw):
            k.op("pool", lambda e, r=r: e.memset(r[:, 0:1], 0.0), writes=[("rawpadl", i)])
            k.op("pool", lambda e, r=r: e.memset(r[:, L + 1:L + 2], 0.0), writes=[("rawpadr", i)])

        def short_conv(row0, wcol, dst, dkeys):
            r = 0
            k.dma("sp", raw[r][:, 1:L + 1], projT[row0:row0 + 128, :], reads=[("projT", row0 // 128)],
                  writes=[("raw", r)])
            rk = [("raw", r), ("rawpadl", r), ("rawpadr", r), ("hy_cw",), ("hy_cb",)]
            for pc in range(8):
                c = pc % 2
                s0 = pc * 512
                k.op("dve", lambda e, s0=s0, c=c: e.tensor_scalar(
                    out=ctmp[c][:], in0=raw[r][:, s0:s0 + 512], scalar1=hy_cw[:, wcol * 3:wcol * 3 + 1],
                    scalar2=hy_cb[:, wcol:wcol + 1], op0=ALU.mult, op1=ALU.add),
                    reads=rk, writes=[("ctmp", c)])
                k.op("dve", lambda e, s0=s0, c=c: e.scalar_tensor_tensor(
                    out=ctmp[c][:], in0=raw[r][:, s0 + 1:s0 + 513], scalar=hy_cw[:, wcol * 3 + 1:wcol * 3 + 2],
                    in1=ctmp[c][:], op0=ALU.mult, op1=ALU.add),
                    reads=rk + [("ctmp", c)], writes=[("ctmp", c)])
                k.op("dve", lambda e, s0=s0, c=c: e.scalar_tensor_tensor(
                    out=dst[:, s0:s0 + 512], in0=raw[r][:, s0 + 2:s0 + 514],
                    scalar=hy_cw[:, wcol * 3 + 2:wcol * 3 + 3], in1=ctmp[c][:], op0=ALU.mult, op1=ALU.add),
                    reads=rk + [("ctmp", c)], writes=dkeys)

        def inv_half(h):
            vt_v = Yt[:].rearrange("p c (r f) -> p f r c", r=2)
            for g4 in range(16):
                b = k.psum()
                fns = []
                for fl in range(4):
                    for ro in range(2):
                        f1 = g4 * 4 + fl
                        slot = fl * 2 + ro
                        fns.append(lambda e, b=b, f1=f1, ro=ro, slot=slot: e.matmul(
                            ps[b][:, slot * 64:(slot + 1) * 64], lhsT=WI[:, f1, ro, :], rhs=Xp[:, f1, :],
                            start=True, stop=True))
                k.op("pe", fns, reads=[("WI",), ("Xs1", g4 // 2)], writes=[("ps", b)])
                copy_op(k, ev(), vt_v[:, g4 * 4:(g4 + 1) * 4, :, :],
                        ps[b][:, :].rearrange("p (f r c) -> p f r c", r=2, c=64),
                        reads=[("ps", b)], writes=[("Yt", 0, i) for i in range(8)])
            for cb in range(8):
                b = k.psum()
                pv = psbf[b].rearrange("p (c f) -> p c f", f=128)
                k.op("pe", [lambda e, cc=cc, pv=pv, cb=cb: e.transpose(
                    out=pv[:, cc, :], in_=Yt[:, cb * 8 + cc, :], identity=ident_bf[:]) for cc in range(8)],
                    reads=[("Yt", 0, cb), ("ident_bf",)], writes=[("ps", b)])
                c0 = 64 * h + cb * 8
                copy_op(k, ev(), Vfm[:, c0:c0 + 8, :], pv, reads=[("ps", b)], writes=[("Vfm", c0 // 8)])

        def inv_finish(j, o):
            zv = zc[:].rearrange("p (t1 t2) -> p t2 t1", t2=128)
            xv = xg[:].rearrange("p (t1 t2) -> p t2 t1", t2=128)
            sk = hy_skip[:, o * 4 + j:o * 4 + j + 1]
            for b8 in range(8):
                b = k.psum()
                k.op("pe", [lambda e, b=b, tl=tl, b8=b8: e.matmul(
                    ps[b][:, tl * 32:(tl + 1) * 32], lhsT=Vfm[:, :, b8 * 16 + tl], rhs=Gm[:, :], start=True,
                    stop=True) for tl in range(16)],
                    reads=[("Vfm", i) for i in range(16)] + [("Gm",)], writes=[("ps", b)])
                g = cnt["gt"] % 2
                cnt["gt"] += 1
                gv = gtmp[g][:].rearrange("p (a b) -> p a b", b=32)
                k.op("dve", lambda e, b=b, b8=b8, gv=gv: e.scalar_tensor_tensor(
                    out=gv, in0=zv[:, b8 * 16:(b8 + 1) * 16, :], scalar=sk,
                    in1=ps[b][:, :].rearrange("p (a b) -> p a b", b=32), op0=ALU.mult, op1=ALU.add),
                    reads=[("ps", b), ("zc", b8), ("hy_skip",)], writes=[("gtmp", g)])
                k.op("pool", lambda e, b8=b8, gv=gv: e.tensor_tensor(
                    out=zv[:, b8 * 16:(b8 + 1) * 16, :], in0=gv, in1=xv[:, b8 * 16:(b8 + 1) * 16, :], op=ALU.mult),
                    reads=[("gtmp", g)] + XGK, writes=[("zc", b8)])

        def hy_norm(j):
            for tg in range(8):
                sl = slice(tg * 512, (tg + 1) * 512)
                q = tg % 2
                k.op("pool", lambda e, sl=sl, q=q: e.tensor_tensor(out=sqb[q][:], in0=zc[:, sl], in1=zc[:, sl],
                                                                   op=ALU.mult),
                     reads=ZCK, writes=[("sqb", q)])
                b = k.psum()
                k.op("pe", lambda e, b=b, q=q: e.matmul(ps[b][:, :], lhsT=blk64[:, :], rhs=sqb[q][:], start=True,
                                                        stop=True),
                     reads=[("blk64",), ("sqb", q)], writes=[("ps", b)])
                p = cnt["pt"] % 4
                cnt["pt"] += 1
                k.op("act", lambda e, b=b, p=p: e.activation(out=ptmp[p][:], in_=ps[b][:, :], func=AF.Ln,
                                                             scale=1.0 / 64.0, bias=epsc[:, 0:1]),
                     reads=[("ps", b), ("epsc",)], writes=[("ptmp", p)])
                k.op("act", lambda e, p=p: e.activation(out=ptmp[p][:], in_=ptmp[p][:], func=AF.Exp, scale=-0.5),
                     reads=[("ptmp", p)], writes=[("ptmp", p)])
                k.op("dve", lambda e, sl=sl, p=p, q=q: e.scalar_tensor_tensor(
                    out=nstg[q][:], in0=zc[:, sl], scalar=hy_ng[:, j:j + 1], in1=ptmp[p][:], op0=ALU.mult,
                    op1=ALU.mult), reads=ZCK + [("ptmp", p), ("hy_ng",)], writes=[("nstg", q)])
                t = k.dma("sp", mixT[j * 128:(j + 1) * 128, sl], nstg[q][:], reads=[("nstg", q)],
                          writes=[("mixT", j, tg)])
                finals.append(t)


        def spectral_mul(h, hi):
            fwd_half(zc, ZCK, h, Xs[0], "Xs0")
            for g in range(8):
                sl = slice(g * 8, (g + 1) * 8)
                hb = []
                for w in range(2):
                    b = k.psum()
                    hq = (2 * g + w) % 4
                    k.op("pe", lambda e, b=b, sl=sl, w=w: e.matmul(
                        ps[b][:, :], lhsT=Dmat[:, w, :], rhs=Hn[hi][:, sl, :], start=True, stop=True),
                        reads=[("Dmat",), ("Hn", hi)], writes=[("ps", b)])
                    copy_op(k, "act", hct[hq][:], ps[b][:, :].rearrange("p (f c) -> p f c", c=64),
                            reads=[("ps", b)], writes=[("hct", hq)])
                    hb.append(hq)
                b = k.psum()
                k.op("pe", lambda e, b=b, sl=sl: e.matmul(ps[b][:, :], lhsT=Pswap[:, :], rhs=Xs[0][:, sl, :],
                                                         start=True, stop=True),
                     reads=[("Pswap",), ("Xs0", g)], writes=[("ps", b)])
                p = cnt["pt"] % 4
                cnt["pt"] += 1
                p2 = cnt["pt"] % 4
                cnt["pt"] += 1
                pv = ptmp[p][:].rearrange("p (f c) -> p f c", c=64)
                pv2 = ptmp[p2][:].rearrange("p (f c) -> p f c", c=64)
                k.op("pool", lambda e, sl=sl, pv=pv, hq=hb[0]: e.tensor_tensor(
                    out=pv, in0=Xs[0][:, sl, :], in1=hct[hq][:], op=ALU.mult),
                    reads=[("Xs0", g), ("hct", hb[0])], writes=[("ptmp", p)])
                k.op("dve", lambda e, b=b, pv2=pv2, hq=hb[1]: e.tensor_tensor(
                    out=pv2, in0=ps[b][:, :].rearrange("p (f c) -> p f c", c=64), in1=hct[hq][:],
                    op=ALU.mult), reads=[("ps", b), ("hct", hb[1])], writes=[("ptmp", p2)])
                k.op("pool", lambda e, sl=sl, pv=pv, pv2=pv2: e.tensor_tensor(out=Xp[:, sl, :], in0=pv,
                                                                           in1=pv2, op=ALU.add),
                     reads=[("ptmp", p), ("ptmp", p2)], writes=[("Xs1", g)])

        ZCK = [("zc", i) for i in range(8)]
        XGK = [("xg",)]
        for j in range(4):
            short_conv(j * 128, j, zc, ZCK)
            if DEBUG and j == 0:
                finals.append(k.dma("sp", dbgB[0], zc[:], reads=ZCK, writes=[("dbgB", 0)]))
            for o in range(2):
                short_conv(512 * (o + 1) + j * 128, 4 * (o + 1) + j, xg, XGK)
                if DEBUG and j == 0 and o == 0:
                    finals.append(k.dma("sp", dbgB[3], xg[:], reads=XGK, writes=[("dbgB", 3)]))
                for h in range(2):
                    idx = (j * 2 + o) * 2 + h
                    hi = 0
                    k.dma("sp", Hn[hi][:].rearrange("p a b -> p (a b)"), hspec[idx], reads=[("hspec", idx)],
                          writes=[("Hn", hi)])
                    spectral_mul(h, hi)
                    inv_half(h)
                inv_finish(j, o)
                if DEBUG and j == 0:
                    finals.append(k.dma("sp", dbgB[1 + o], zc[:], reads=ZCK, writes=[("dbgB", 1 + o)]))
            hy_norm(j)
    k.barrier()


def phase_C(nc, k, dr, ps, psbf, ident_bf, ident_f, projT, gatesT, mixT, finals):
    import contextlib
    LNS = -0.5 * math.log(128.0)
    NCH = 32
    with contextlib.ExitStack() as st:
        def sb(name, shape, dt):
            return st.enter_context(nc.sbuf_tensor("s_" + name, shape, dt))
        sel = sb("sel", [8, 8, 128], F32)
        mask = [sb("mask_f", [128, 128], F32), sb("mask_b", [128, 128], F32)]
        ones_bf = sb("ones_bf", [128, 128], BF)
        ml_cw = sb("ml_cw", [128, 24], F32)
        ml_cb = sb("ml_cb", [128, 8], F32)
        ml_ng = sb("ml_ng", [128, 4], F32)
        gbi = sb("gbi", [8, 1], F32)
        gbf = sb("gbf", [8, 1], F32)
        dmk = sb("dmk", [8, 2], F32)
        negA = sb("negA", [8, L + 2], F32)
        eM = sb("eM", [8, L], F32)
        acol = sb("acol", [128, NCH, 8], F32)
        acols = sb("acols", [128, NCH, 8], F32)
        k.dma("sp", sel[:], dr["sel"][0:8, 0:1024].rearrange("p (a b) -> p a b", a=8), writes=[("sel",)])
        k.dma("sp", mask[0][:], dr["mask_f"], writes=[("mask", 0)])
        k.dma("sp", mask[1][:], dr["mask_b"], writes=[("mask", 1)])
        k.dma("sp", ones_bf[:], dr["ones_bf"], writes=[("ones_bf",)])
        k.dma("sp", ml_cw[:], dr["ml_cw"], writes=[("ml_cw",)])
        k.dma("sp", ml_cb[:], dr["ml_cb"], writes=[("ml_cb",)])
        k.dma("sp", ml_ng[:], dr["ml_ng"], writes=[("ml_ng",)])
        k.dma("sp", dmk[:], dr["dirmask"], writes=[("dmk",)])
        k.dma("sp", gbi[0:4, :], dr["ml_gb"][0:4, :], writes=[("gbi",)])
        k.dma("sp", gbi[4:8, :], dr["ml_gb"][8:12, :], writes=[("gbi",)])
        k.dma("sp", gbf[0:4, :], dr["ml_gb"][4:8, :], writes=[("gbf",)])
        k.dma("sp", gbf[4:8, :], dr["ml_gb"][12:16, :], writes=[("gbf",)])

        with contextlib.ExitStack() as st2:
            def sb2(name, shape, dt):
                return st2.enter_context(nc.sbuf_tensor("s_" + name, shape, dt))
            T1 = sb2("T1", [8, L], F32)
            T2 = sb2("T2", [8, L], F32)
            T3 = sb2("T3", [8, L], F32)
            T4 = sb2("T4", [8, L], F32)
            nfb = sb2("nfb", [8, 1], F32)
            k.dma("sp", T4[0:4, :], gatesT[0:4, :], reads=[("gatesT",)], writes=[("T4",)])
            k.dma("sp", T4[4:8, :], gatesT[8:12, :], reads=[("gatesT",)], writes=[("T4",)])
            k.dma("sp", T1[0:4, :], gatesT[4:8, :], reads=[("gatesT",)], writes=[("T1",)])
            k.dma("sp", T1[4:8, :], gatesT[12:16, :], reads=[("gatesT",)], writes=[("T1",)])
            k.op("dve", lambda e: e.tensor_scalar(out=nfb[:], in0=gbf[:], scalar1=-1.0, scalar2=None, op0=ALU.mult),
                 reads=[("gbf",)], writes=[("nfb",)])
            k.op("act", lambda e: e.activation(out=T1[:], in_=T1[:], func=AF.Exp, scale=-1.0, bias=nfb[:, 0:1]),
                 reads=[("T1",), ("nfb",)], writes=[("T1",)])
            k.op("act", lambda e: e.activation(out=T1[:], in_=T1[:], func=AF.Ln, bias=1.0),
                 reads=[("T1",)], writes=[("T1",)])
            k.op("dve", lambda e: e.tensor_tensor_scan(out=T2[:], data0=T1[:], data1=T1[:], initial=0.0,
                                                       op0=ALU.add, op1=ALU.max),
                 reads=[("T1",)], writes=[("T2",)])
            k.op("dve", lambda e: e.tensor_tensor_scan(out=T3[:, ::-1], data0=T1[:, ::-1], data1=T1[:, ::-1],
                                                       initial=0.0, op0=ALU.add, op1=ALU.max),
                 reads=[("T1",)], writes=[("T3",)])
            k.op("dve", lambda e: e.tensor_scalar(out=T2[:], in0=T2[:], scalar1=dmk[:, 0:1], scalar2=None,
                                                  op0=ALU.mult), reads=[("T2",), ("dmk",)], writes=[("T2",)])
            k.op("dve", lambda e: e.scalar_tensor_tensor(out=T2[:], in0=T3[:], scalar=dmk[:, 1:2], in1=T2[:],
                                                         op0=ALU.mult, op1=ALU.add),
                 reads=[("T2",), ("T3",), ("dmk",)], writes=[("T2",)])
            k.op("dve", lambda e: e.scalar_tensor_tensor(out=T4[:], in0=T4[:], scalar=gbi[:, 0:1], in1=T2[:],
                                                         op0=ALU.add, op1=ALU.add),
                 reads=[("T4",), ("T2",), ("gbi",)], writes=[("T4",)])
            k.op("dve", lambda e: e.tensor_tensor_scan(out=T3[:], data0=T4[:], data1=T4[:], initial=0.0,
                                                       op0=ALU.max, op1=ALU.max),
                 reads=[("T4",)], writes=[("T3",)])
            k.op("dve", lambda e: e.tensor_tensor_scan(out=T1[:, ::-1], data0=T4[:, ::-1], data1=T4[:, ::-1],
                                                       initial=0.0, op0=ALU.max, op1=ALU.max),
                 reads=[("T4",)], writes=[("T1",)])
            k.op("dve", lambda e: e.tensor_scalar(out=T3[:], in0=T3[:], scalar1=dmk[:, 0:1], scalar2=None,
                                                  op0=ALU.mult), reads=[("T3",), ("dmk",)], writes=[("T3",)])
            k.op("dve", lambda e: e.scalar_tensor_tensor(out=T3[:], in0=T1[:], scalar=dmk[:, 1:2], in1=T3[:],
                                                         op0=ALU.mult, op1=ALU.add),
                 reads=[("T1",), ("T3",), ("dmk",)], writes=[("T3",)])
            k.op("pool", lambda e: e.memset(negA[:, 0:1], 0.0), writes=[("negApl",)])
            k.op("pool", lambda e: e.memset(negA[:, L + 1:L + 2], 0.0), writes=[("negApr",)])
            k.op("dve", lambda e: e.tensor_scalar(out=negA[:, 1:L + 1], in0=T3[:], scalar1=-1.0, scalar2=None,
                                                  op0=ALU.mult), reads=[("T3",)], writes=[("negA",)])
            k.op("dve", lambda e: e.tensor_tensor(out=T2[:], in0=T2[:], in1=T3[:], op=ALU.subtract),
                 reads=[("T2",), ("T3",)], writes=[("T2",)])
            k.op("act", lambda e: e.activation(out=eM[:], in_=T2[:], func=AF.Exp), reads=[("T2",)],
                 writes=[("eM",)])
            b = k.psum()
            k.op("pe", [lambda e, n=n, b=b: e.transpose(out=ps[b][:, n * 8:(n + 1) * 8],
                                                       in_=T4[0:8, n * 128:(n + 1) * 128],
                                                       identity=ident_f[0:8, 0:8]) for n in range(NCH)],
                 reads=[("T4",), ("ident_f",)], writes=[("ps", b)])
            k.op("dve", lambda e, b=b: e.tensor_copy(out=acol[:].rearrange("p a b -> p (a b)"), in_=ps[b][:, 0:256]),
                 reads=[("ps", b)], writes=[("acol",)])
            k.op("dve", lambda e: e.tensor_scalar(out=acols[:].rearrange("p a b -> p (a b)"),
                                                  in0=acol[:].rearrange("p a b -> p (a b)"), scalar1=LNS,
                                                  scalar2=None, op0=ALU.add), reads=[("acol",)], writes=[("acols",)])
        k.barrier()

        raw = sb("rawc", [128, L + 2], BF)
        ctmp = [sb(f"cct{i}", [128, 512], F32) for i in range(2)]
        ctm2 = [sb(f"cc2{i}", [128, 512], F32) for i in range(2)]
        qT = [sb(f"qT{i}", [128, L], BF) for i in range(2)]
        kT = [sb(f"kT{i}", [128, L], BF) for i in range(2)]
        ktm = [sb(f"ktm{i}", [128, NCH, 128], BF) for i in range(2)]
        vext = [sb(f"vext{i}", [128, NCH, 256], BF) for i in range(2)]
        hT = [sb(f"hT{i}", [128, L], F32) for i in range(2)]
        Cst = [sb(f"Cst{i}", [128, 256], F32) for i in range(4)]
        Cbf = [sb(f"Cbf{i}", [128, 256], BF) for i in range(4)]
        scA = [sb(f"scA{i}", [128, 4, NCH], F32) for i in range(4)]
        NB = 8
        wT = [sb(f"wT{i}", [128, 128], BF) for i in range(NB)]
        wTm = [sb(f"wTm{i}", [128, 128], BF) for i in range(NB)]
        itb = [sb(f"itb{i}", [128, 128], BF) for i in range(NB)]
        qtl = [sb(f"qtl{i}", [128, 128], BF) for i in range(NB)]
        pT = [sb(f"pT{i}", [128, 128], BF) for i in range(NB)]
        vw = [sb(f"vw{i}", [128, 256], BF) for i in range(NB)]
        sc = [sb(f"sc{i}", [128, 8], F32) for i in range(NB)]
        dn = [sb(f"dn{i}", [128, 128], F32) for i in range(NB)]
        hb = [sb(f"hb{i}", [128, 128], F32) for i in range(NB)]
        ntmp = ctmp
        nsq = [sb(f"nsq{i}", [128, 512], BF) for i in range(2)]
        nrt = [ctm2[0]] * 2
        emr = [sb(f"emr{i}", [128, 128], F32) for i in range(NB)]
        nout = [sb(f"nout{i}", [128, 512], BF) for i in range(2)]
        k.op("pool", lambda e: e.memset(raw[:, 0:1], 0.0), writes=[("rawcl",)])
        k.op("pool", lambda e: e.memset(raw[:, L + 1:L + 2], 0.0), writes=[("rawcr",)])
        for i in range(2):
            k.op("pool", lambda e, i=i: e.memset(vext[i][:, :, 128:256], 1.0), writes=[("vones", i)])
        cnt = {"ev": 0, "r": 0}

        def ev():
            cnt["ev"] += 1
            return evac_engine(cnt["ev"])

        def conv_silu(row0, wcol, dst, dkey):
            k.dma("sp", raw[:, 1:L + 1], projT[row0:row0 + 128, :], reads=[("projT", row0 // 128)],
                  writes=[("rawc",)])
            rk = [("rawc",), ("rawcl",), ("rawcr",), ("ml_cw",), ("ml_cb",)]
            for pc in range(8):
                c = pc % 2
                s0 = pc * 512
                k.op("dve", lambda e, s0=s0, c=c: e.tensor_scalar(
                    out=ctmp[c][:], in0=raw[:, s0:s0 + 512], scalar1=ml_cw[:, wcol * 3:wcol * 3 + 1],
                    scalar2=ml_cb[:, wcol:wcol + 1], op0=ALU.mult, op1=ALU.add), reads=rk, writes=[("cct", c)])
                k.op("dve", lambda e, s0=s0, c=c: e.scalar_tensor_tensor(
                    out=ctmp[c][:], in0=raw[:, s0 + 1:s0 + 513], scalar=ml_cw[:, wcol * 3 + 1:wcol * 3 + 2],
                    in1=ctmp[c][:], op0=ALU.mult, op1=ALU.add), reads=rk + [("cct", c)], writes=[("cct", c)])
                k.op("dve", lambda e, s0=s0, c=c: e.scalar_tensor_tensor(
                    out=ctm2[c][:], in0=raw[:, s0 + 2:s0 + 514], scalar=ml_cw[:, wcol * 3 + 2:wcol * 3 + 3],
                    in1=ctmp[c][:], op0=ALU.mult, op1=ALU.add), reads=rk + [("cct", c)], writes=[("cc2", c)])
                k.op("act", lambda e, s0=s0, c=c: e.activation(out=dst[:, s0:s0 + 512], in_=ctm2[c][:],
                                                               func=AF.Silu),
                     reads=[("cc2", c)], writes=[dkey])

        def to_token_major(src, skey, dst, dkey, width):
            for nb in range(4):
                b = k.psum()
                pv = psbf[b].rearrange("p (n f) -> p n f", f=128)
                k.op("pe", [lambda e, nn=nn, pv=pv, nb=nb: e.transpose(
                    out=pv[:, nn, :], in_=src[:, (nb * 8 + nn) * 128:(nb * 8 + nn + 1) * 128], identity=ident_bf[:])
                    for nn in range(8)], reads=[skey, ("ident_bf",)], writes=[("ps", b)])
                copy_op(k, ev(), dst[:, nb * 8:(nb + 1) * 8, 0:128], pv, reads=[("ps", b)], writes=[dkey])

        ctxs = {}

        def chain_scalars(head, d, cidx):
            c = d * 4 + head
            off = 0 if d == 0 else 1
            b = k.psum()
            k.op("pe", lambda e: e.matmul(ps[b][:, 0:33], lhsT=sel[:, c, :], rhs=negA[:, off:off + 32 * 128 + 1:128],
                                          start=True, stop=True),
                 reads=[("sel",), ("negA",), ("negApl",), ("negApr",)], writes=[("ps", b)])
            prev0, end0 = (0, 1) if d == 0 else (1, 0)
            t = scA[cidx]
            k.op("dve", lambda e: e.tensor_scalar(out=t[:, 0, :], in0=ps[b][:, prev0:prev0 + NCH], scalar1=-1.0,
                                                  scalar2=LNS, op0=ALU.mult, op1=ALU.add),
                 reads=[("ps", b)], writes=[("scA", cidx)])
            k.op("dve", lambda e: e.tensor_copy(out=t[:, 1, :], in_=ps[b][:, end0:end0 + NCH]),
                 reads=[("ps", b)], writes=[("scA", cidx)])
            k.op("dve", lambda e: e.tensor_tensor(out=t[:, 2, :], in0=t[:, 0, :], in1=t[:, 1, :], op=ALU.add),
                 reads=[("scA", cidx)], writes=[("scA", cidx)])
            k.op("act", lambda e: e.activation(out=t[:, 2, :], in_=t[:, 2, :], func=AF.Exp, bias=-LNS),
                 reads=[("scA", cidx)], writes=[("scA", cidx)])
            k.op("dve", lambda e: e.tensor_tensor(out=t[:, 3, :], in0=acol[:, :, c], in1=t[:, 1, :], op=ALU.add),
                 reads=[("scA", cidx), ("acol",)], writes=[("scA", cidx)])
            k.op("act", lambda e: e.activation(out=t[:, 3, :], in_=t[:, 3, :], func=AF.Exp),
                 reads=[("scA", cidx)], writes=[("scA", cidx)])

        def stage_a1(hl, head, d, n, cidx):
            c = d * 4 + head
            r = cnt["r"] % NB
            cnt["r"] += 1
            t0 = n * 128
            bx = k.psum()
            ctxs[(cidx, n)] = (r, bx)
            k.op("pe", [lambda e: e.matmul(ps[bx][:, 0:128], lhsT=sel[:, c, :], rhs=negA[:, t0 + 1:t0 + 129],
                                           start=True, stop=True),
                        lambda e: e.matmul(ps[bx][:, 130:258], lhsT=sel[:, c, :], rhs=eM[:, t0:t0 + 128],
                                           start=True, stop=True),
                        lambda e: e.matmul(ps[bx][:, 258:386], lhsT=kT[hl][:, t0:t0 + 128],
                                           rhs=qT[hl][:, t0:t0 + 128], start=True, stop=True)],
                 reads=[("sel",), ("negA",), ("negApl",), ("negApr",), ("eM",), ("kT", hl), ("qT", hl)],
                 writes=[("ps", bx)])
            k.op("act", lambda e: e.activation(out=wT[r][:], in_=ps[bx][:, 0:128], func=AF.Exp,
                                               bias=acols[:, n, c:c + 1]),
                 reads=[("ps", bx), ("acols",)], writes=[("wT", r)])
            k.op("act", lambda e: e.activation(out=itb[r][:], in_=ps[bx][:, 0:128], func=AF.Exp,
                                               bias=scA[cidx][:, 0, n:n + 1]),
                 reads=[("ps", bx), ("scA", cidx)], writes=[("itb", r)])
            k.op("act", lambda e: e.copy(out=emr[r][:], in_=ps[bx][:, 130:258]), reads=[("ps", bx)],
                 writes=[("emr", r)])

        def stage_a2(hl, head, d, n, cidx):
            r, bx = ctxs[(cidx, n)]
            t0 = n * 128
            k.op("pool", lambda e: e.tensor_tensor(out=wTm[r][:], in0=wT[r][:], in1=mask[d][:], op=ALU.mult),
                 reads=[("wT", r), ("mask", d)], writes=[("wTm", r)])
            k.op("pool", lambda e: e.tensor_tensor(out=qtl[r][:], in0=qT[hl][:, t0:t0 + 128], in1=itb[r][:],
                                                   op=ALU.mult),
                 reads=[("qT", hl), ("itb", r)], writes=[("qtl", r)])
            k.op("pool", lambda e: e.tensor_scalar(out=vw[r][:], in0=vext[hl][:, n, :], scalar1=scA[cidx][:, 3, n:n + 1],
                                                   scalar2=1.0, op0=ALU.mult, op1=ALU.mult),
                 reads=[("vext", hl), ("vones", hl), ("scA", cidx)], writes=[("vw", r)])
            k.op("dve", lambda e: e.tensor_tensor(out=pT[r][:], in0=ps[bx][:, 258:386], in1=wTm[r][:],
                                                  op=ALU.mult),
                 reads=[("ps", bx), ("wTm", r)], writes=[("pT", r)])

        def stage_b1(hl, head, d, n, cidx):
            r, bx = ctxs[(cidx, n)]
            by = k.psum()
            ctxs[(cidx, n)] = (r, bx, by)
            k.op("pe", [lambda e: e.matmul(ps[by][:, 0:128], lhsT=vext[hl][:, n, 0:128], rhs=pT[r][:],
                                           start=True, stop=False),
                        lambda e: e.matmul(ps[by][:, 0:128], lhsT=Cbf[cidx][:, 0:128], rhs=qtl[r][:],
                                           start=False, stop=True),
                        lambda e: e.matmul(ps[by][:, 128:256], lhsT=ones_bf[:, :], rhs=pT[r][:],
                                           start=True, stop=False),
                        lambda e: e.matmul(ps[by][:, 128:256], lhsT=Cbf[cidx][:, 128:256],
                                           rhs=qtl[r][:], start=False, stop=True),
                        lambda e: e.matmul(ps[by][:, 256:512], lhsT=ktm[hl][:, n, :], rhs=vw[r][:],
                                           start=True, stop=True)],
                 reads=[("vext", hl), ("vones", hl), ("pT", r), ("Cbf", cidx), ("qtl", r), ("ones_bf",),
                        ("ktm", hl), ("vw", r)],
                 writes=[("ps", by)])

        def stage_b2(hl, head, d, n, cidx):
            r, bx, by = ctxs.pop((cidx, n))
            t0 = n * 128
            k.op("dve", lambda e: e.scalar_tensor_tensor(out=Cst[cidx][:], in0=Cst[cidx][:],
                                                         scalar=scA[cidx][:, 2, n:n + 1], in1=ps[by][:, 256:512],
                                                         op0=ALU.mult, op1=ALU.add),
                 reads=[("ps", by), ("scA", cidx), ("Cst", cidx)], writes=[("Cst", cidx)])
            k.op("act", lambda e: e.copy(out=Cbf[cidx][:], in_=Cst[cidx][:]), reads=[("Cst", cidx)],
                 writes=[("Cbf", cidx)])
            k.op("act", lambda e: e.activation(out=dn[r][:], in_=ps[by][:, 128:256], func=AF.Abs),
                 reads=[("ps", by)], writes=[("dn", r)])
            k.op("dve", lambda e: e.tensor_tensor(out=dn[r][:], in0=emr[r][:], in1=dn[r][:],
                                                  op=ALU.max),
                 reads=[("emr", r), ("dn", r)], writes=[("dn", r)])
            k.op("act", lambda e: e.activation(out=dn[r][:], in_=dn[r][:], func=AF.Ln), reads=[("dn", r)],
                 writes=[("dn", r)])
            k.op("act", lambda e: e.activation(out=dn[r][:], in_=dn[r][:], func=AF.Exp, scale=-1.0),
                 reads=[("dn", r)], writes=[("dn", r)])
            k.op("dve", lambda e: e.tensor_tensor(out=hb[r][:], in0=ps[by][:, 0:128], in1=dn[r][:],
                                                  op=ALU.mult),
                 reads=[("ps", by), ("dn", r)], writes=[("hb", r)])
            k.op("pool", lambda e: e.tensor_tensor(out=hT[hl][:, t0:t0 + 128], in0=hT[hl][:, t0:t0 + 128],
                                                   in1=hb[r][:], op=ALU.add),
                 reads=[("hb", r), ("hT", hl, n)], writes=[("hT", hl, n)])

        def finish_head(hl, head):
            k.dma("sp", raw[:, 1:L + 1], projT[3072 + head * 128:3072 + (head + 1) * 128, :],
                  reads=[("projT", 24 + head)], writes=[("rawc",)])
            for tg in range(8):
                q = tg % 2
                sl = slice(tg * 512, (tg + 1) * 512)
                k.op("act", lambda e, sl=sl, q=q: e.activation(out=ntmp[q][:], in_=raw[:, 1 + sl.start:1 + sl.stop],
                                                               func=AF.Sigmoid),
                     reads=[("rawc",)], writes=[("cct", q)])
                k.op("dve", lambda e, sl=sl, q=q: e.tensor_tensor(out=ntmp[q][:], in0=ntmp[q][:], in1=hT[hl][:, sl],
                                                                  op=ALU.mult),
                     reads=[("cct", q)] + [("hT", hl, n) for n in range(tg * 4, tg * 4 + 4)], writes=[("cct", q)])
                k.op("pool", lambda e, q=q: e.tensor_tensor(out=nsq[q][:], in0=ntmp[q][:], in1=ntmp[q][:],
                                                            op=ALU.mult),
                     reads=[("cct", q)], writes=[("nsq", q)])
                b = k.psum()
                k.op("pe", lambda e, b=b, q=q: e.matmul(ps[b][:, :], lhsT=ones_bf[:, :], rhs=nsq[q][:], start=True,
                                                        stop=True), reads=[("ones_bf",), ("nsq", q)],
                     writes=[("ps", b)])
                k.op("act", lambda e, b=b, q=q: e.activation(out=nrt[q][:], in_=ps[b][:, :], func=AF.Ln,
                                                             scale=1.0 / 128.0, bias=epsc[:, 0:1]),
                     reads=[("ps", b), ("epsc",)], writes=[("cc2", 0)])
                k.op("act", lambda e, q=q: e.activation(out=nrt[q][:], in_=nrt[q][:], func=AF.Exp, scale=-0.5),
                     reads=[("cc2", 0)], writes=[("cc2", 0)])
                k.op("dve", lambda e, q=q: e.scalar_tensor_tensor(out=nout[q][:], in0=ntmp[q][:],
                                                                  scalar=ml_ng[:, head:head + 1], in1=nrt[q][:],
                                                                  op0=ALU.mult, op1=ALU.mult),
                     reads=[("cct", q), ("cc2", 0), ("ml_ng",)], writes=[("nout", q)])
                t = k.dma("sp", mixT[512 + head * 128:512 + (head + 1) * 128, sl], nout[q][:],
                          reads=[("nout", q)], writes=[("mixT", 4 + head, tg)])
                finals.append(t)

        for hp in range(2):
            for hl in range(2):
                head = hp * 2 + hl
                conv_silu(1536 + head * 128, head, qT[hl], ("qT", hl))
                conv_silu(2048 + head * 128, 4 + head, kT[hl], ("kT", hl))
                to_token_major(kT[hl], ("kT", hl), ktm[hl], ("ktm", hl), 128)
                k.dma("sp", raw[:, 1:L + 1], projT[2560 + head * 128:2560 + (head + 1) * 128, :],
                      reads=[("projT", 20 + head)], writes=[("rawc",)])
                to_token_major(raw[:, 1:L + 1], ("rawc",), vext[hl], ("vext", hl), 128)
                k.op("pool", lambda e, hl=hl: e.memset(hT[hl][:], 0.0), writes=[("hT", hl, n) for n in range(NCH)])
                for d in range(2):
                    ci = hl * 2 + d
                    chain_scalars(head, d, ci)
                    k.op("pool", lambda e, ci=ci: e.memset(Cst[ci][:], 0.0), writes=[("Cst", ci)])
                    k.op("pool", lambda e, ci=ci: e.memset(Cbf[ci][:], 0.0), writes=[("Cbf", ci)])
            def chains(step):
                return [(hl, hp * 2 + hl, d, (step if d == 0 else NCH - 1 - step), hl * 2 + d)
                        for hl in range(2) for d in range(2)]
            for step in range(NCH + 1):
                if step < NCH:
                    for a in chains(step):
                        stage_a1(*a)
                    for a in chains(step):
                        stage_a2(*a)
                if step >= 1:
                    for a in chains(step - 1):
                        stage_b1(*a)
                    for a in chains(step - 1):
                        stage_b2(*a)
            for hl in range(2):
                finish_head(hl, hp * 2 + hl)
    k.barrier()


def phase_D(nc, k, dr, ps, psbf, ident_bf, mixT, out, finals):
    import contextlib
    TG = 256
    NG = L // TG
    NT = TG // 128
    with contextlib.ExitStack() as st:
        def sb(name, shape, dt):
            return st.enter_context(nc.sbuf_tensor("s_" + name, shape, dt))
        gcols = sb("gcolsD", [128, 3, 8], F32)
        gfin = sb("gfin", [128, D], F32)
        kTm = sb("kTm", [128, 8, NMEM], BF)
        Vm = sb("Vm", [128, 2, D], BF)
        w_out = sb("w_outb", [128, 8, D], BF)
        wq = sb("wqb", [128, 8, D], BF)
        wo = sb("wob", [128, 8, D], BF)
        w2 = sb("w2b", [128, 32, D], BF)
        k.dma("sp", gcols[:, 0, :], dr["gx_col"], writes=[("gcolsD",)])
        k.dma("sp", gcols[:, 1, :], dr["gmem_col"], writes=[("gcolsD",)])
        k.dma("sp", gcols[:, 2, :], dr["gff_col"], writes=[("gcolsD",)])
        k.dma("sp", gfin[:], dr["gfin_row"].partition_broadcast(128), writes=[("gfin",)])
        cnt = {"ev": 0, "st": 0}

        def ev():
            cnt["ev"] += 1
            return "act" if cnt["ev"] % 3 == 0 else "dve"

        w1d = dr["wscr"]["ff_w1"]
        with contextlib.ExitStack() as st2:
            def sb2(name, shape, dt):
                return st2.enter_context(nc.sbuf_tensor("s_" + name, shape, dt))
            wk = sb2("wkb", [128, 8, D], BF)
            wv = sb2("wvb", [128, 8, D], BF)
            mt = sb2("memt", [128, D], F32)
            mst = sb2("memst", [128, 4], F32)
            mn = sb2("memn", [128, D], BF)
            mnT = sb2("memnT", [128, 8, NMEM], BF)

            def wl(q, dst, nme, key, c0, c1):
                k.dma(q, dst[:, c0:c1, :], dr["wscr"][nme].rearrange("(c p) n -> p c n", p=128)[:, c0:c1, :],
                      reads=[("wscr", nme)], writes=[key])
            wl("sp", wk, "xa_wk", ("wk",), 0, 8)
            wl("act", wv, "xa_wv", ("wv",), 0, 8)
            wl("sp", w_out, "w_out", ("w_out",), 0, 8)
            wl("act", wq, "xa_wq", ("wq",), 0, 8)
            wl("sp", wo, "xa_wo", ("wo",), 0, 8)
            for i in range(4):
                wl("act" if i % 2 else "sp", w2, "ff_w2", ("w2",), i * 8, (i + 1) * 8)
            for mtile in range(2):
                k.dma("sp", mt[:], dr["mem"][mtile * 128:(mtile + 1) * 128, :], writes=[("memt",)])
                k.op("act", lambda e, mtile=mtile: e.activation(out=mn[:], in_=mt[:], func=AF.Square,
                                                                accum_out=mst[:, 0:1]),
                     reads=[("memt",)], writes=[("memn",), ("memst",)])
                k.op("act", lambda e: e.activation(out=mst[:, 1:2], in_=mst[:, 0:1], func=AF.Sqrt, scale=1.0 / D,
                                                   bias=EPS), reads=[("memst",)], writes=[("memst",)])
                k.op("dve", lambda e: e.reciprocal(out=mst[:, 2:3], in_=mst[:, 1:2]), reads=[("memst",)],
                     writes=[("memst",)])
                k.op("act", lambda e: e.activation(out=mn[:], in_=mt[:], func=AF.Copy, scale=mst[:, 2:3]),
                     reads=[("memt",), ("memst",)], writes=[("memn",)])
                b = k.psum()
                pv = psbf[b].rearrange("p (c t) -> p c t", t=128)
                k.op("pe", [lambda e, c=c, pv=pv: e.transpose(out=pv[:, c, :], in_=mn[:, c * 128:(c + 1) * 128],
                                                              identity=ident_bf[:]) for c in range(8)],
                     reads=[("memn",), ("ident_bf",)], writes=[("ps", b)])
                copy_op(k, ev(), mnT[:, :, mtile * 128:(mtile + 1) * 128], pv, reads=[("ps", b)],
                        writes=[("memnT",)])
            for cc in range(8):
                b = k.psum()
                k.op("pe", [lambda e, kc=kc, cc=cc, b=b: e.matmul(ps[b][:, 0:NMEM],
                                                                  lhsT=wk[:, kc, cc * 128:(cc + 1) * 128],
                                                                  rhs=mnT[:, kc, :], start=(kc == 0), stop=(kc == 7))
                            for kc in range(8)], reads=[("wk",), ("memnT",)], writes=[("ps", b)])
                copy_op(k, ev(), kTm[:, cc, :], ps[b][:, 0:NMEM], reads=[("ps", b)], writes=[("kTm",)])
            for mc in range(2):
                for half in range(2):
                    b = k.psum()
                    k.op("pe", [lambda e, kc=kc, mc=mc, half=half, b=b: e.matmul(
                        ps[b][:, :], lhsT=mnT[:, kc, mc * 128:(mc + 1) * 128],
                        rhs=wv[:, kc, half * 512:(half + 1) * 512], start=(kc == 0), stop=(kc == 7))
                        for kc in range(8)], reads=[("wv",), ("memnT",)], writes=[("ps", b)])
                    copy_op(k, ev(), Vm[:, mc, half * 512:(half + 1) * 512], ps[b][:, :], reads=[("ps", b)],
                            writes=[("Vm",)])
        k.barrier()

        w1s = [sb(f"w1s{i}", [128, 8, 256], BF) for i in range(3)]
        mx = sb("mxD", [128, 8, TG], BF)
        xt = sb("xtD", [128, D], F32)
        h = [[sb(f"hD{p}_{i}", [128, D], F32) for i in range(NT)] for p in range(2)]
        xn = [sb(f"xnD{i}", [128, D], BF) for i in range(2)]
        stt = sb("statD", [128, 16], F32)
        xnT1 = sb("xnT1D", [128, 8, TG], BF)
        xnT2 = [sb(f"xnT2D{i}", [128, 8, TG], BF) for i in range(2)]
        qT = sb("qTD", [128, 8, TG], BF)
        smx = sb("smx", [128, 8], F32)
        Pun = [sb(f"Pun{i}", [128, 2, NMEM], BF) for i in range(2)]
        Pn = [sb(f"Pn{i}", [128, 2, NMEM], BF) for i in range(2)]
        PT = sb("PTD", [128, 4, 2, TG], BF)
        oT = qT
        hid = sb("hidD", [128, 32, TG], BF)
        rtmp = [sb(f"rtD{i}", [128, TG], BF) for i in range(2)]
        c2 = {"w1": 0, "sx": 0, "rt": 0}
        SC = 1.0 / 16.0

        def rms_to_T(hsrc, hkey, dstT, dkey, tt, scol):
            j = tt % 2
            k.op("act", lambda e: e.activation(out=xn[j][:], in_=hsrc[:], func=AF.Square,
                                               accum_out=stt[:, scol:scol + 1]),
                 reads=[hkey], writes=[("xnD", j), ("statD", scol)])
            k.op("act", lambda e: e.activation(out=stt[:, scol + 1:scol + 2], in_=stt[:, scol:scol + 1],
                                               func=AF.Sqrt, scale=1.0 / D, bias=EPS),
                 reads=[("statD", scol)], writes=[("statD", scol)])
            k.op("dve", lambda e: e.reciprocal(out=stt[:, scol + 2:scol + 3], in_=stt[:, scol + 1:scol + 2]),
                 reads=[("statD", scol)], writes=[("statD", scol)])
            k.op("act", lambda e: e.activation(out=xn[j][:], in_=hsrc[:], func=AF.Copy,
                                               scale=stt[:, scol + 2:scol + 3]),
                 reads=[hkey, ("statD", scol)], writes=[("xnD", j)])
            b = k.psum()
            pv = psbf[b].rearrange("p (c t) -> p c t", t=128)
            k.op("pe", [lambda e, c=c, pv=pv: e.transpose(out=pv[:, c, :], in_=xn[j][:, c * 128:(c + 1) * 128],
                                                          identity=ident_bf[:]) for c in range(8)],
                 reads=[("xnD", j), ("ident_bf",)], writes=[("ps", b)])
            copy_op(k, ev(), dstT[:, :, tt * 128:(tt + 1) * 128], pv, reads=[("ps", b)], writes=[dkey])

        def proj_token_major(srcT, skey, W, wkey, nk, tt, hdst, hkey, addsrc, addkey):
            for half in range(2):
                b = k.psum()
                k.op("pe", [lambda e, kc=kc, half=half, b=b: e.matmul(
                    ps[b][:, :], lhsT=srcT[:, kc, tt * 128:(tt + 1) * 128], rhs=W[:, kc, half * 512:(half + 1) * 512],
                    start=(kc == 0), stop=(kc == nk - 1)) for kc in range(nk)],
                    reads=[skey, wkey], writes=[("ps", b)])
                k.op("dve", lambda e, b=b, half=half: e.tensor_tensor(
                    out=hdst[:, half * 512:(half + 1) * 512], in0=ps[b][:, :],
                    in1=addsrc[:, half * 512:(half + 1) * 512], op=ALU.add),
                    reads=[("ps", b), addkey], writes=[hkey])

        def X_steps(g):
            tok0 = g * TG
            hp_ = g % 2
            hh_ = h[hp_]
            xo = xnT2[hp_]

            def s1():
                k.dma("sp", mx[:], mixT.rearrange("(c p) t -> p c t", p=128)[:, :, tok0:tok0 + TG],
                      reads=[("mixT", j, tg) for j in range(8) for tg in range(8)], writes=[("mxD",)])
                for tt in range(NT):
                    k.dma("sp", xt[:], dr["x"][tok0 + tt * 128:tok0 + (tt + 1) * 128, :], writes=[("xtD",)])
                    proj_token_major(mx, ("mxD",), w_out, ("w_out",), 8, tt, hh_[tt], ("hD", hp_, tt), xt,
                                     ("xtD",))
                    rms_to_T(hh_[tt], ("hD", hp_, tt), xnT1, ("xnT1D",), tt, 0)

            def s2():
                for cc in range(8):
                    b = k.psum()
                    k.op("pe", [lambda e, kc=kc, cc=cc, b=b: e.matmul(
                        ps[b][:, 0:TG], lhsT=wq[:, kc, cc * 128:(cc + 1) * 128], rhs=xnT1[:, kc, :],
                        start=(kc == 0), stop=(kc == 7)) for kc in range(8)],
                        reads=[("wq",), ("xnT1D",)], writes=[("ps", b)])
                    copy_op(k, ev(), qT[:, cc, :], ps[b][:, 0:TG], reads=[("ps", b)], writes=[("qTD",)])

            def s3(tt):
                for hp in range(2):
                    b = k.psum()
                    fns = []
                    for hh in range(2):
                        hd = hp * 2 + hh
                        for cq in range(2):
                            fns.append(lambda e, b=b, hh=hh, hd=hd, cq=cq: e.matmul(
                                ps[b][:, hh * NMEM:(hh + 1) * NMEM], lhsT=qT[:, 2 * hd + cq, tt * 128:(tt + 1) * 128],
                                rhs=kTm[:, 2 * hd + cq, :], start=(cq == 0), stop=(cq == 1)))
                    k.op("pe", fns, reads=[("qTD",), ("kTm",)], writes=[("ps", b)])
                    sx = c2["sx"] % 2
                    c2["sx"] += 1
                    pv = ps[b][:, :].rearrange("p (a m) -> p a m", m=NMEM)
                    k.op("dve", lambda e, pv=pv, sx=sx: e.tensor_reduce(out=smx[:, sx * 4:sx * 4 + 2], in_=pv,
                                                                        axis=AX.X, op=ALU.max),
                         reads=[("ps", b)], writes=[("smx", sx)])
                    k.op("dve", lambda e, sx=sx: e.tensor_scalar(out=smx[:, sx * 4:sx * 4 + 2],
                                                                 in0=smx[:, sx * 4:sx * 4 + 2], scalar1=-SC,
                                                                 scalar2=None, op0=ALU.mult),
                         reads=[("smx", sx)], writes=[("smx", sx)])
                    for hh in range(2):
                        k.op("act", lambda e, b=b, hh=hh, sx=sx: e.activation(
                            out=Pun[sx][:, hh, :], in_=ps[b][:, hh * NMEM:(hh + 1) * NMEM], func=AF.Exp, scale=SC,
                            bias=smx[:, sx * 4 + hh:sx * 4 + hh + 1],
                            accum_out=smx[:, sx * 4 + 2 + hh:sx * 4 + 3 + hh]),
                            reads=[("ps", b), ("smx", sx)], writes=[("Pun", sx), ("smx", sx)])
                    k.op("dve", lambda e, sx=sx: e.reciprocal(out=smx[:, sx * 4 + 2:sx * 4 + 4],
                                                              in_=smx[:, sx * 4 + 2:sx * 4 + 4]),
                         reads=[("smx", sx)], writes=[("smx", sx)])
                    for hh in range(2):
                        k.op("dve", lambda e, hh=hh, sx=sx: e.tensor_scalar(
                            out=Pn[sx][:, hh, :], in0=Pun[sx][:, hh, :],
                            scalar1=smx[:, sx * 4 + 2 + hh:sx * 4 + 3 + hh], scalar2=None, op0=ALU.mult),
                            reads=[("Pun", sx), ("smx", sx)], writes=[("Pn", sx)])
                    b2 = k.psum()
                    pv2 = psbf[b2].rearrange("p (a t) -> p a t", t=128)
                    k.op("pe", [lambda e, hh=hh, mc=mc, pv2=pv2, sx=sx: e.transpose(
                        out=pv2[:, hh * 2 + mc, :], in_=Pn[sx][:, hh, mc * 128:(mc + 1) * 128], identity=ident_bf[:])
                        for hh in range(2) for mc in range(2)],
                        reads=[("Pn", sx), ("ident_bf",)], writes=[("ps", b2)])
                    copy_op(k, ev(), PT[:, hp * 2:hp * 2 + 2, :, tt * 128:(tt + 1) * 128],
                            pv2[:, 0:4, :].rearrange("p (h m) t -> p h m t", m=2), reads=[("ps", b2)],
                            writes=[("PTD",)])

            def s4():
                for cc in range(8):
                    b = k.psum()
                    k.op("pe", [lambda e, mc=mc, cc=cc, b=b: e.matmul(
                        ps[b][:, 0:TG], lhsT=Vm[:, mc, cc * 128:(cc + 1) * 128], rhs=PT[:, cc // 2, mc, :],
                        start=(mc == 0), stop=(mc == 1)) for mc in range(2)],
                        reads=[("Vm",), ("PTD",)], writes=[("ps", b)])
                    copy_op(k, ev(), oT[:, cc, :], ps[b][:, 0:TG], reads=[("ps", b)], writes=[("qTD",)])

            def s5():
                for tt in range(NT):
                    proj_token_major(oT, ("qTD",), wo, ("wo",), 8, tt, hh_[tt], ("hD", hp_, tt), hh_[tt],
                                     ("hD", hp_, tt))
                    rms_to_T(hh_[tt], ("hD", hp_, tt), xo, ("xnT2D", hp_), tt, 4)
            return [s1, s2] + [lambda tt=tt: s3(tt) for tt in range(NT)] + [s4, s5]

        def Y_steps(g):
            tok0 = g * TG
            hp_ = g % 2
            hh_ = h[hp_]
            xi = xnT2[hp_]

            def slab(hs):
                w = c2["w1"] % 3
                c2["w1"] += 1
                k.dma("sp", w1s[w][:], w1d.rearrange("(c p) n -> p c n", p=128)[:, :, hs * 256:(hs + 1) * 256],
                      reads=[("wscr", "ff_w1")], writes=[("w1s", w)])
                for hl in range(2):
                    hc = hs * 2 + hl
                    b = k.psum()
                    k.op("pe", [lambda e, kc=kc, hl=hl, b=b, w=w: e.matmul(
                        ps[b][:, 0:TG], lhsT=w1s[w][:, kc, hl * 128:(hl + 1) * 128], rhs=xi[:, kc, :],
                        start=(kc == 0), stop=(kc == 7)) for kc in range(8)],
                        reads=[("w1s", w), ("xnT2D", hp_)], writes=[("ps", b)])
                    r = c2["rt"] % 2
                    c2["rt"] += 1
                    k.op("act", lambda e, b=b, r=r: e.activation(out=rtmp[r][:], in_=ps[b][:, 0:TG], func=AF.Relu),
                         reads=[("ps", b)], writes=[("rtD", r)])
                    k.op("pool", lambda e, r=r, hc=hc: e.tensor_tensor(out=hid[:, hc, :], in0=rtmp[r][:],
                                                                       in1=rtmp[r][:], op=ALU.mult),
                         reads=[("rtD", r)], writes=[("hidD", hc // 4)])

            def tail(tt):
                hidk = [("hidD", i) for i in range(8)]
                for half in range(2):
                    b = k.psum()
                    k.op("pe", [lambda e, kc=kc, half=half, b=b: e.matmul(
                        ps[b][:, :], lhsT=hid[:, kc, tt * 128:(tt + 1) * 128], rhs=w2[:, kc, half * 512:(half + 1) * 512],
                        start=(kc == 0), stop=(kc == 31)) for kc in range(32)],
                        reads=hidk + [("w2",)], writes=[("ps", b)])
                    k.op("dve", lambda e, b=b, half=half: e.tensor_tensor(
                        out=hh_[tt][:, half * 512:(half + 1) * 512], in0=ps[b][:, :],
                        in1=hh_[tt][:, half * 512:(half + 1) * 512], op=ALU.add),
                        reads=[("ps", b), ("hD", hp_, tt)], writes=[("hD", hp_, tt)])
                k.op("act", lambda e: e.activation(out=xn[0][:], in_=hh_[tt][:], func=AF.Square,
                                                   accum_out=stt[:, 8:9]),
                     reads=[("hD", hp_, tt)], writes=[("xnD", 0), ("statD", 8)])
                k.op("act", lambda e: e.activation(out=stt[:, 9:10], in_=stt[:, 8:9], func=AF.Sqrt, scale=1.0 / D,
                                                   bias=EPS), reads=[("statD", 8)], writes=[("statD", 8)])
                k.op("dve", lambda e: e.reciprocal(out=stt[:, 10:11], in_=stt[:, 9:10]), reads=[("statD", 8)],
                     writes=[("statD", 8)])
                k.op("dve", lambda e: e.scalar_tensor_tensor(out=hh_[tt][:], in0=hh_[tt][:], scalar=stt[:, 10:11],
                                                             in1=gfin[:], op0=ALU.mult, op1=ALU.mult),
                     reads=[("hD", hp_, tt), ("statD", 8), ("gfin",)], writes=[("hD", hp_, tt)])
                t = k.dma("act", out[tok0 + tt * 128:tok0 + (tt + 1) * 128, :], hh_[tt][:], reads=[("hD", hp_, tt)],
                          writes=[("out", g, tt)])
                finals.append(t)
            return [lambda hs=hs: slab(hs) for hs in range(16)] + [lambda tt=tt: tail(tt) for tt in range(NT)]

        XB, YB = (4, 5, 6, 7), (0, 1, 2, 3)

        def run_piece(f, banks):
            k.ps_pool = banks
            f()
            k.ps_pool = None
        for f in X_steps(0):
            run_piece(f, XB)
        for g in range(NG):
            ys = Y_steps(g)
            xs = X_steps(g + 1) if g + 1 < NG else []
            order = []
            xi_ = 0
            for i, yf in enumerate(ys):
                order.append((yf, YB))
                want = ((i + 1) * len(xs)) // len(ys)
                while xi_ < want:
                    order.append((xs[xi_], XB))
                    xi_ += 1
            for f, banks in order:
                run_piece(f, banks)
```
